# Optimizing a Trainium2 kernel written in Bass

```python
import math
import jax, jax.numpy as jnp
from jax import lax
import numpy as np

D_MODEL = 1024
BATCH = 8
SEQ = 4096
DEPTH = 2
DEC_BATCH = 8
DEC_SEQ = 64
PAST_LEN = 1024

CHUNK = 64
N_META = 16
Q_BLOCK = 128
EPS = 1e-6
D_FF = 2816
GDN_HEADS = 4
GDN_DK = 128
GDN_DV = 128
GDN_CONV = 4
MLA_HEADS = 8
MLA_NOPE = 64
MLA_ROPE = 32
MLA_V = 64
MLA_Q_RANK = 384
MLA_KV_RANK = 256
ROPE_THETA = 10000.0
SC_WIDTH = 3

N_A_LAYERS = (DEPTH + 1) // 2
N_C_LAYERS = DEPTH // 2
GDN_QKV = GDN_HEADS * (2 * GDN_DK + GDN_DV)
GDN_COLS = GDN_QKV + GDN_HEADS * GDN_DV + 2 * GDN_HEADS
MLA_COLS = MLA_Q_RANK + MLA_KV_RANK + MLA_ROPE
IN_COLS = GDN_COLS + MLA_COLS
MIX_WIDTH = GDN_HEADS * GDN_DV + MLA_HEADS * MLA_V
BIG_CHUNK_ID = 1 << 30

kernel_name = 'hybrid_streaming_gdn_mla_shortconv_step'


def rms_norm(x, g):
    xf = x.astype(jnp.float32)
    y = xf * lax.rsqrt(jnp.mean(xf * xf, -1, keepdims=True) + EPS)
    return (y * g.astype(jnp.float32)).astype(x.dtype)


def l2norm(x):
    xf = x.astype(jnp.float32)
    return xf * lax.rsqrt(jnp.sum(xf * xf, -1, keepdims=True) + EPS)


def half_ffn(x, g, w_gate, w_up, w_down):
    h = rms_norm(x, g)
    return x + 0.5 * ((jax.nn.silu(h @ w_gate) * (h @ w_up)) @ w_down)


def causal_dwconv(x, w, prev):
    width = w.shape[0]
    t = x.shape[1]
    xp = jnp.concatenate([prev, x], 1)
    y = sum(xp[:, i:i + t] * w[i] for i in range(width))
    return y, xp[:, xp.shape[1] - (width - 1):]


def rope(x, pos):
    half = x.shape[-1] // 2
    inv = ROPE_THETA ** (-jnp.arange(half, dtype=jnp.float32) / half)
    ang = pos.astype(jnp.float32)[:, None] * inv
    shape = (pos.shape[0],) + (1,) * (x.ndim - 3) + (half,)
    cos, sin = jnp.cos(ang).reshape(shape), jnp.sin(ang).reshape(shape)
    xf = x.astype(jnp.float32)
    x1, x2 = xf[..., :half], xf[..., half:]
    return jnp.concatenate([x1 * cos - x2 * sin, x2 * cos + x1 * sin], -1).astype(x.dtype)


def gated_delta_chunked(q, k, v, g, beta, S0):
    b, T, H, dk = q.shape
    dv = v.shape[-1]
    n = T // CHUNK
    blk = lambda a: a.reshape((b, n, CHUNK) + a.shape[2:])
    to_h = lambda a: jnp.moveaxis(a, 3, 2)
    qh, kh, vh = to_h(blk(q)), to_h(blk(k)), to_h(blk(v))
    bh = to_h(blk(beta))
    Gh = jnp.cumsum(to_h(blk(g)), axis=-1)
    dG = Gh[..., :, None] - Gh[..., None, :]
    idx = jnp.arange(CHUNK)
    dec_strict = jnp.exp(jnp.where(idx[:, None] > idx[None, :], dG, -jnp.inf))
    dec_causal = jnp.exp(jnp.where(idx[:, None] >= idx[None, :], dG, -jnp.inf))
    kk = jnp.einsum('bnhid,bnhjd->bnhij', kh, kh)
    A = jnp.eye(CHUNK, dtype=jnp.float32) + bh[..., :, None] * kk * dec_strict
    rhs = jnp.concatenate([bh[..., None] * vh, (bh * jnp.exp(Gh))[..., None] * kh], -1)
    sol = lax.linalg.triangular_solve(A, rhs, left_side=True, lower=True)
    U, Wk = sol[..., :dv], sol[..., dv:]
    qk = jnp.einsum('bnhid,bnhjd->bnhij', qh, kh) * dec_causal
    qd = qh * jnp.exp(Gh)[..., None]
    g_last = Gh[..., -1]
    kt = kh * jnp.exp(g_last[..., None] - Gh)[..., None]

    def step(S, xs):
        qk_c, qd_c, U_c, Wk_c, kt_c, gl_c = xs
        W = U_c - jnp.einsum('bhcd,bhde->bhce', Wk_c, S)
        o = jnp.einsum('bhcd,bhde->bhce', qd_c, S) + jnp.einsum('bhij,bhje->bhie', qk_c, W)
        S = S * jnp.exp(gl_c)[..., None, None] + jnp.einsum('bhcd,bhce->bhde', kt_c, W)
        return S, o

    front = lambda a: jnp.moveaxis(a, 1, 0)
    S, o = lax.scan(step, S0, (front(qk), front(qd), front(U), front(Wk), front(kt), front(g_last)))
    o = jnp.transpose(o, (1, 0, 3, 2, 4)).reshape(b, T, H, dv)
    return o, S


def gdn_mixer(h, S0, conv_prev, conv_w, A_log, dt_bias, o_norm):
    b, t, _ = h.shape
    s1 = GDN_QKV + GDN_HEADS * GDN_DV
    qkv, z, a, bt = jnp.split(h, [GDN_QKV, s1, s1 + GDN_HEADS], axis=-1)
    qkv, conv_new = causal_dwconv(qkv, conv_w, conv_prev)
    qkv = jax.nn.silu(qkv)
    q, k, v = jnp.split(qkv, [GDN_HEADS * GDN_DK, 2 * GDN_HEADS * GDN_DK], axis=-1)
    q = l2norm(q.reshape(b, t, GDN_HEADS, GDN_DK)) * (GDN_DK ** -0.5)
    k = l2norm(k.reshape(b, t, GDN_HEADS, GDN_DK))
    v = v.reshape(b, t, GDN_HEADS, GDN_DV).astype(jnp.float32)
    g = -jnp.exp(A_log.astype(jnp.float32)) * jax.nn.softplus(a.astype(jnp.float32) + dt_bias.astype(jnp.float32))
    beta = jax.nn.sigmoid(bt.astype(jnp.float32))
    pad = (-t) % CHUNK
    padt = lambda u: jnp.pad(u, [(0, 0), (0, pad)] + [(0, 0)] * (u.ndim - 2))
    o, S = gated_delta_chunked(padt(q), padt(k), padt(v), padt(g), padt(beta), S0.astype(jnp.float32))
    o = rms_norm(o[:, :t], o_norm) * jax.nn.silu(z.reshape(b, t, GDN_HEADS, GDN_DV).astype(jnp.float32))
    return o.reshape(b, t, GDN_HEADS * GDN_DV).astype(h.dtype), S.astype(S0.dtype), conv_new


def attn_probs_v(qn, qr, kn, kr, v, mask):
    s = jnp.einsum('bqhd,bkhd->bhqk', qn, kn) + jnp.einsum('bqhd,bkd->bhqk', qr, kr)
    s = s.astype(jnp.float32) * ((MLA_NOPE + MLA_ROPE) ** -0.5)
    if mask is not None:
        s = jnp.where(mask, s, -jnp.inf)
    p = jax.nn.softmax(s, axis=-1).astype(v.dtype)
    return jnp.einsum('bhqk,bkhd->bqhd', p, v)


def mla_mixer(h, pos, cache_ckv, cache_kr, q_norm, w_uq, kv_norm, w_ukv, qn_norm, qr_norm, kn_norm, kr_norm):
    b, t, _ = h.shape
    cq, ckv, kr = jnp.split(h, [MLA_Q_RANK, MLA_Q_RANK + MLA_KV_RANK], axis=-1)
    q = (rms_norm(cq, q_norm) @ w_uq).reshape(b, t, MLA_HEADS, MLA_NOPE + MLA_ROPE)
    q_nope = rms_norm(q[..., :MLA_NOPE], qn_norm)
    q_rope = rope(rms_norm(q[..., MLA_NOPE:], qr_norm), pos)
    ckv_new = rms_norm(ckv, kv_norm)
    kr_new = rope(rms_norm(kr, kr_norm), pos)
    if cache_ckv is None:
        ckv_all, kr_all = ckv_new, kr_new
    else:
        ckv_all = jnp.concatenate([cache_ckv, ckv_new], 1)
        kr_all = jnp.concatenate([cache_kr, kr_new], 1)
    kv = (ckv_all @ w_ukv).reshape(b, ckv_all.shape[1], MLA_HEADS, MLA_NOPE + MLA_V)
    k_nope = rms_norm(kv[..., :MLA_NOPE], kn_norm)
    v = kv[..., MLA_NOPE:]
    if cache_ckv is None:
        L = t
        cid = (jnp.arange(L) - N_META) // CHUNK
        nb = -(-L // Q_BLOCK)
        pad = nb * Q_BLOCK - L
        qblk = lambda a: jnp.moveaxis(jnp.pad(a, [(0, 0), (0, pad), (0, 0), (0, 0)]).reshape((b, nb, Q_BLOCK) + a.shape[2:]), 1, 0)
        qcid = jnp.pad(cid, (0, pad), constant_values=BIG_CHUNK_ID).reshape(nb, Q_BLOCK)

        def block(args):
            qn_b, qr_b, qc_b = args
            return attn_probs_v(qn_b, qr_b, k_nope, kr_all, v, cid[None, :] <= qc_b[:, None])

        o = lax.map(block, (qblk(q_nope), qblk(q_rope), qcid))
        o = jnp.moveaxis(o, 0, 1).reshape(b, nb * Q_BLOCK, MLA_HEADS, MLA_V)[:, :L]
    else:
        o = attn_probs_v(q_nope, q_rope, k_nope, kr_all, v, None)
    return o.reshape(b, t, MLA_HEADS * MLA_V), ckv_new, kr_new


def mixer_ab(hn, pos, S0, conv_prev, cache_ckv, cache_kr, w_in, w_out, conv_w, A_log, dt_bias, o_norm,
             q_norm, w_uq, kv_norm, w_ukv, qn_norm, qr_norm, kn_norm, kr_norm):
    h = hn @ w_in
    o_g, S_new, conv_new = gdn_mixer(h[..., :GDN_COLS], S0, conv_prev, conv_w, A_log, dt_bias, o_norm)
    o_m, ckv_new, kr_new = mla_mixer(h[..., GDN_COLS:], pos, cache_ckv, cache_kr, q_norm, w_uq, kv_norm,
                                     w_ukv, qn_norm, qr_norm, kn_norm, kr_norm)
    y = jnp.concatenate([o_g, o_m], -1) @ w_out
    return y, ckv_new, kr_new, S_new, conv_new


def short_conv_mixer(hn, prev, w_in, conv_w, w_out):
    bg, cg, xin = jnp.split(hn @ w_in, 3, axis=-1)
    y, new_prev = causal_dwconv(cg * xin, conv_w, prev)
    return (bg * y) @ w_out, new_prev


def setup_inputs(seed: int = 0) -> dict:
    key = jax.random.key(seed)
    ks = iter(jax.random.split(key, 64))
    f32 = jnp.float32
    nrm = lambda shape, scale: jax.random.normal(next(ks), shape, f32) * scale
    gain = lambda shape: 1.0 + 0.02 * jax.random.normal(next(ks), shape, f32)
    NA, NC = N_A_LAYERS, N_C_LAYERS
    dt = jnp.exp(jax.random.uniform(next(ks), (NA, GDN_HEADS), f32, math.log(1e-3), math.log(1e-1)))
    return {
        'x_prompt': nrm((BATCH, SEQ, D_MODEL), 1.0),
        'x_sample': nrm((DEC_BATCH, DEC_SEQ, D_MODEL), 1.0),
        'cache_mla_ckv': nrm((NA, DEC_BATCH, PAST_LEN, MLA_KV_RANK), 1.0),
        'cache_mla_krope': nrm((NA, DEC_BATCH, PAST_LEN, MLA_ROPE), 1.0),
        'state_gdn_S': nrm((NA, DEC_BATCH, GDN_HEADS, GDN_DK, GDN_DV), 0.1),
        'state_gdn_conv': nrm((NA, DEC_BATCH, GDN_CONV - 1, GDN_QKV), 1.0),
        'state_sconv': nrm((NC, DEC_BATCH, SC_WIDTH - 1, D_MODEL), 1.0),
        'meta_tokens': nrm((N_META, D_MODEL), 1.0),
        'ffn1_norm': gain((DEPTH, D_MODEL)),
        'ffn1_w_gate': nrm((DEPTH, D_MODEL, D_FF), D_MODEL ** -0.5),
        'ffn1_w_up': nrm((DEPTH, D_MODEL, D_FF), D_MODEL ** -0.5),
        'ffn1_w_down': nrm((DEPTH, D_FF, D_MODEL), D_FF ** -0.5),
        'ffn2_norm': gain((DEPTH, D_MODEL)),
        'ffn2_w_gate': nrm((DEPTH, D_MODEL, D_FF), D_MODEL ** -0.5),
        'ffn2_w_up': nrm((DEPTH, D_MODEL, D_FF), D_MODEL ** -0.5),
        'ffn2_w_down': nrm((DEPTH, D_FF, D_MODEL), D_FF ** -0.5),
        'mix_norm': gain((DEPTH, D_MODEL)),
        'ab_w_in': nrm((NA, D_MODEL, IN_COLS), D_MODEL ** -0.5),
        'ab_w_out': nrm((NA, MIX_WIDTH, D_MODEL), MIX_WIDTH ** -0.5),
        'gdn_conv_w': nrm((NA, GDN_CONV, GDN_QKV), GDN_CONV ** -0.5),
        'gdn_A_log': jnp.log(jax.random.uniform(next(ks), (NA, GDN_HEADS), f32, 1.0, 16.0)),
        'gdn_dt_bias': dt + jnp.log(-jnp.expm1(-dt)),
        'gdn_o_norm': gain((NA, GDN_DV)),
        'mla_q_norm': gain((NA, MLA_Q_RANK)),
        'mla_w_uq': nrm((NA, MLA_Q_RANK, MLA_HEADS * (MLA_NOPE + MLA_ROPE)), MLA_Q_RANK ** -0.5),
        'mla_kv_norm': gain((NA, MLA_KV_RANK)),
        'mla_w_ukv': nrm((NA, MLA_KV_RANK, MLA_HEADS * (MLA_NOPE + MLA_V)), MLA_KV_RANK ** -0.5),
        'mla_qn_norm': gain((NA, MLA_NOPE)),
        'mla_qr_norm': gain((NA, MLA_ROPE)),
        'mla_kn_norm': gain((NA, MLA_NOPE)),
        'mla_kr_norm': gain((NA, MLA_ROPE)),
        'sc_w_in': nrm((NC, D_MODEL, 3 * D_MODEL), D_MODEL ** -0.5),
        'sc_conv_w': nrm((NC, SC_WIDTH, D_MODEL), SC_WIDTH ** -0.5),
        'sc_w_out': nrm((NC, D_MODEL, D_MODEL), D_MODEL ** -0.5),
    }


def reference(x_prompt, x_sample, cache_mla_ckv, cache_mla_krope, state_gdn_S, state_gdn_conv, state_sconv,
              meta_tokens, ffn1_norm, ffn1_w_gate, ffn1_w_up, ffn1_w_down, ffn2_norm, ffn2_w_gate, ffn2_w_up,
              ffn2_w_down, mix_norm, ab_w_in, ab_w_out, gdn_conv_w, gdn_A_log, gdn_dt_bias, gdn_o_norm,
              mla_q_norm, mla_w_uq, mla_kv_norm, mla_w_ukv, mla_qn_norm, mla_qr_norm, mla_kn_norm, mla_kr_norm,
              sc_w_in, sc_conv_w, sc_w_out):
    bp, tp = x_prompt.shape[0], x_prompt.shape[1]
    bs, ts = x_sample.shape[0], x_sample.shape[1]
    dtype = x_prompt.dtype
    xp = jnp.concatenate([jnp.broadcast_to(meta_tokens.astype(dtype)[None], (bp, N_META, D_MODEL)), x_prompt], 1)
    xs = x_sample
    pos_p = jnp.arange(N_META + tp)
    pos_s = PAST_LEN + jnp.arange(ts)
    p_ckv, p_kr, p_S, p_conv, p_sc = [], [], [], [], []
    s_ckv, s_kr, s_S, s_conv, s_sc = [], [], [], [], []
    for l in range(DEPTH):
        xp = half_ffn(xp, ffn1_norm[l], ffn1_w_gate[l], ffn1_w_up[l], ffn1_w_down[l])
        xs = half_ffn(xs, ffn1_norm[l], ffn1_w_gate[l], ffn1_w_up[l], ffn1_w_down[l])
        hp, hs = rms_norm(xp, mix_norm[l]), rms_norm(xs, mix_norm[l])
        if l % 2 == 0:
            i = l // 2
            wa = (ab_w_in[i], ab_w_out[i], gdn_conv_w[i], gdn_A_log[i], gdn_dt_bias[i], gdn_o_norm[i],
                  mla_q_norm[i], mla_w_uq[i], mla_kv_norm[i], mla_w_ukv[i], mla_qn_norm[i], mla_qr_norm[i],
                  mla_kn_norm[i], mla_kr_norm[i])
            S0p = jnp.zeros((bp, GDN_HEADS, GDN_DK, GDN_DV), dtype)
            c0p = jnp.zeros((bp, GDN_CONV - 1, GDN_QKV), dtype)
            yp, ckv, kr, S, cv = mixer_ab(hp, pos_p, S0p, c0p, None, None, *wa)
            p_ckv.append(ckv); p_kr.append(kr); p_S.append(S); p_conv.append(cv)
            ys, ckv, kr, S, cv = mixer_ab(hs, pos_s, state_gdn_S[i], state_gdn_conv[i], cache_mla_ckv[i],
                                          cache_mla_krope[i], *wa)
            s_ckv.append(ckv); s_kr.append(kr); s_S.append(S); s_conv.append(cv)
        else:
            i = l // 2
            yp, cp = short_conv_mixer(hp, jnp.zeros((bp, SC_WIDTH - 1, D_MODEL), dtype), sc_w_in[i], sc_conv_w[i], sc_w_out[i])
            ys, cs = short_conv_mixer(hs, state_sconv[i], sc_w_in[i], sc_conv_w[i], sc_w_out[i])
            p_sc.append(cp); s_sc.append(cs)
        xp, xs = xp + yp, xs + ys
        xp = half_ffn(xp, ffn2_norm[l], ffn2_w_gate[l], ffn2_w_up[l], ffn2_w_down[l])
        xs = half_ffn(xs, ffn2_norm[l], ffn2_w_gate[l], ffn2_w_up[l], ffn2_w_down[l])
    y_prompt = xp[:, N_META:]
    y_sample = xs
    return (y_prompt, y_sample,
            jnp.stack(p_ckv), jnp.stack(p_kr), jnp.stack(p_S), jnp.stack(p_conv), jnp.stack(p_sc),
            jnp.stack(s_ckv), jnp.stack(s_kr), jnp.stack(s_S), jnp.stack(s_conv), jnp.stack(s_sc))
```

```python
import numpy as np
from contextlib import ExitStack
import concourse.bass as bass
import concourse.mybir as mybir
from concourse.bass_utils import run_bass_kernel_spmd

F32 = mybir.dt.float32
BF16 = mybir.dt.bfloat16
U8 = mybir.dt.uint8
AF = mybir.ActivationFunctionType
ALU = mybir.AluOpType
EPS = 1e-6
DEBUG = False
DFF = 2816
HFF = 1408
NCH = 11
SAME_ENG_SYNC = True


class Tile:
    def __init__(self, ap, res):
        self.ap = ap
        self.res = tuple(res)

    def __getitem__(self, k):
        return self.ap[k]


def _flat(xs):
    out = []
    for x in xs:
        if isinstance(x, Tile):
            out.extend(x.res)
        elif isinstance(x, (list, tuple)) and x and isinstance(x[0], (Tile, list, tuple)):
            out.extend(_flat(x))
        else:
            out.append(x)
    return out


class Op:
    __slots__ = ("eng", "fn", "R", "W", "dma", "deps", "sig", "val", "sem", "tag")


class Prog:
    ENG = ("pe", "act", "dve", "pool", "sp")

    def __init__(self, nc):
        self.nc = nc
        self.ops = []
        self.dcount = {}
        self._tag = None
        self._g0 = None

    def add(self, eng, fn, R=(), W=(), dma=None):
        o = Op()
        o.eng, o.fn, o.R, o.W, o.dma = eng, fn, _flat(R), _flat(W), dma
        for r in o.R:
            if isinstance(r, tuple) and r and r[0] == "PS" and r not in o.W:
                o.W.append(r)
        o.deps = ()
        o.sig = dma is not None
        o.tag = self._tag
        self.ops.append(o)
        return o

    def begin_group(self):
        self._g0 = len(self.ops)

    def stage(self, item, k):
        self._tag = (item, k)

    def end_group(self):
        seg = self.ops[self._g0:]
        assert all(o.tag is not None for o in seg)
        order = sorted(range(len(seg)), key=lambda i: (seg[i].tag[0] + seg[i].tag[1], -seg[i].tag[1], i))
        self.ops[self._g0:] = [seg[i] for i in order]
        self._tag = None
        self._g0 = None

    def pe(self, fn, R=(), W=()):
        return self.add("pe", fn, R, W)

    def act(self, fn, R=(), W=()):
        return self.add("act", fn, R, W)

    def dve(self, fn, R=(), W=()):
        return self.add("dve", fn, R, W)

    DPOOL = {"sp": 32, "pool": 12}

    def dma(self, fn, R, W, sem, q="sp"):
        c = self.dcount.get(q, 0)
        self.dcount[q] = c + 1
        return self.add(q, fn, R, W, dma=(q, c % self.DPOOL[q]))

    def finish(self, stack):
        nc = self.nc
        ops = self.ops
        lastw = {}
        readers = {}
        lastdma = {}
        for i, o in enumerate(ops):
            d = set()
            if o.dma is not None:
                if o.dma in lastdma:
                    d.add(lastdma[o.dma])
                lastdma[o.dma] = i
            for r in o.R:
                if r in lastw:
                    d.add(lastw[r])
            for r in o.W:
                if r in lastw:
                    d.add(lastw[r])
                rd = readers.get(r)
                if rd:
                    d.update(rd.values())
            d.discard(i)
            o.deps = d
            for r in o.R:
                readers.setdefault(r, {})[o.eng if o.dma is None else ("dma", i)] = i
            for r in o.W:
                lastw[r] = i
                readers[r] = {}
        fin = Op()
        fin.eng, fin.fn, fin.R, fin.W, fin.dma, fin.sig = "sp", None, [], [], None, False
        fin.deps = set(i for i, o in enumerate(ops) if o.dma is not None)
        ops.append(fin)
        for o in ops:
            for j in o.deps:
                y = ops[j]
                if y.dma is None and not (y.eng == o.eng and o.dma is None and (o.eng == "pe" or not SAME_ENG_SYNC)):
                    y.sig = True
        esem = {e: stack.enter_context(nc.semaphore("s_" + e)) for e in self.ENG}
        dsem = {}
        ecnt = {e: 0 for e in self.ENG}
        dcnt = {}
        for o in ops:
            if o.dma is not None:
                if o.dma not in dsem:
                    dsem[o.dma] = stack.enter_context(nc.semaphore("d_%d" % len(dsem)))
                    dcnt[o.dma] = 0
                dcnt[o.dma] += 16
                o.sem, o.val = dsem[o.dma], dcnt[o.dma]
            elif o.sig:
                ecnt[o.eng] += 1
                o.sem, o.val = esem[o.eng], ecnt[o.eng]
        per = {e: [] for e in self.ENG}
        for o in ops:
            per[o.eng].append(o)

        def run(eng_name, e):
            waited = {}
            for o in per[eng_name]:
                need = {}
                for j in o.deps:
                    y = ops[j]
                    if y.dma is None and y.eng == eng_name and o.dma is None and (eng_name == "pe" or not SAME_ENG_SYNC):
                        continue
                    k = id(y.sem)
                    if waited.get(k, 0) >= y.val:
                        continue
                    if k not in need or need[k][1] < y.val:
                        need[k] = (y.sem, y.val)
                for k, (s, v) in need.items():
                    e.wait_ge(s, v)
                    waited[k] = v
                if o.fn is None:
                    continue
                ins = o.fn(e)
                if o.dma is not None:
                    ins.then_inc(o.sem, 16)
                elif o.sig:
                    ins.then_inc(o.sem, 1)

        with nc.Block() as block:
            @block.sync
            def _(e):
                run("sp", e)

            @block.tensor
            def _(e):
                run("pe", e)

            @block.scalar
            def _(e):
                run("act", e)

            @block.vector
            def _(e):
                run("dve", e)

            @block.gpsimd
            def _(e):
                run("pool", e)


class Arena:
    GRAN = 256

    def __init__(self, t):
        self.t = t
        self.off = 0

    def reset(self):
        self.off = 0

    def alloc(self, shape, dtype):
        esz = 4 if dtype == F32 else 2
        n = int(np.prod(shape)) * esz
        off = (self.off + self.GRAN - 1) // self.GRAN * self.GRAN
        self.off = off + n
        assert self.off <= self.t.shape[1], ("arena overflow", self.off)
        ap = self.t[:, off:off + n].bitcast(dtype)
        if len(shape) == 2:
            ap = ap.rearrange("p (a b) -> p a b", b=shape[1])
        elif len(shape) == 3:
            ap = ap.rearrange("p (a b c) -> p a b c", b=shape[1], c=shape[2])
        return Tile(ap, [("AR", g) for g in range(off // self.GRAN, (off + n - 1) // self.GRAN + 1)])


class Ring:
    def __init__(self, tiles):
        self.tiles = tiles
        self.i = 0
        for j, t in enumerate(tiles):
            if isinstance(t, Tile):
                t.ri = j

    def next(self):
        t = self.tiles[self.i % len(self.tiles)]
        self.i += 1
        return t


def build(NT, sched='full'):
    NF = NT * 512
    NTOK = NF + 80
    NK = NTOK + 1024
    nc = bass.Bass("TRN2", target_bir_lowering=False)
    P = Prog(nc)
    D = {}

    def din(name, shape, dt=F32):
        D[name] = nc.dram_tensor(name, list(shape), dt, kind="ExternalInput").ap()
        return D[name]

    def dout(name, shape):
        D[name] = nc.dram_tensor(name, list(shape), F32, kind="ExternalOutput").ap()
        return D[name]

    def dint(name, shape, dt):
        D[name] = nc.dram_tensor(name, list(shape), dt, kind=("ExternalOutput" if DEBUG else "Internal")).ap()
        return D[name]

    xp = din("xp", [NF, 1024]); xs = din("xs", [64, 1024]); meta = din("meta", [16, 1024])
    cckv = din("cckv", [1024, 256]); ckr = din("ckr", [1024, 32]); gS = din("gS", [4, 128, 128])
    gconv = din("gconv", [3, 1536]); sconv = din("sconv", [2, 1024])
    f1n = din("f1n", [2, 1024]); f2n = din("f2n", [2, 1024]); mixn = din("mixn", [2, 1024])
    fw = {}
    for k in ("f1g", "f1u", "f2g", "f2u"):
        fw[k] = din(k, [2, 1024, DFF])
    for k in ("f1d", "f2d"):
        fw[k] = din(k, [2, DFF, 1024])
    w_in = din("w_in", [1024, 2728]); w_out = din("w_out", [1024, 1024]); convw = din("convw", [4, 1536])
    alog = din("alog", [1, 4]); dtb = din("dtb", [1, 4]); onorm = din("onorm", [1, 128])
    qnorm = din("qnorm", [1, 384]); wuq = din("wuq", [384, 768]); kvnorm = din("kvnorm", [1, 256])
    wukv = din("wukv", [256, 1024]); qnn = din("qnn", [1, 64]); qrn = din("qrn", [1, 32])
    knn = din("knn", [1, 64]); krn = din("krn", [1, 32])
    scin = din("scin", [1024, 3072]); sccw = din("sccw", [3, 1024]); scout = din("scout", [1024, 1024])
    c_ident = din("c_ident", [128, 128]); c_tri = din("c_tri", [128, 128]); c_mgt = din("c_mgt", [128, 128])
    c_mlt = din("c_mlt", [128, 128]); c_amask = din("c_amask", [128, 4, 512]); c_b96 = din("c_b96", [96, 96])
    c_pT = din("c_pT", [96, 96]); c_cs = din("c_cs", [NTOK, 32]); c_C96 = din("c_C96", [96, NTOK])
    c_S96 = din("c_S96", [96, NTOK])
    y_p = dout("y_p", [NF, 1024]); y_s = dout("y_s", [64, 1024])
    o_pckv = dout("o_pckv", [NF + 16, 256]); o_pkr = dout("o_pkr", [NF + 16, 32])
    o_pS = dout("o_pS", [4, 128, 128]); o_pconv = dout("o_pconv", [3, 1536]); o_psc = dout("o_psc", [2, 1024])
    o_sckv = dout("o_sckv", [64, 256]); o_skr = dout("o_skr", [64, 32]); o_sS = dout("o_sS", [4, 128, 128])
    o_sconv = dout("o_sconv", [3, 1536]); o_ssc = dout("o_ssc", [2, 1024])
    XA = dint("XA", [NTOK, 1024], F32); XB = dint("XB", [NTOK, 1024], F32)
    HTD = dint("HTD", [8, 128, NTOK], BF16)
    QKVD = dint("QKVD", [12, 128, NTOK], F32); ZTD = dint("ZTD", [4, 128, NTOK], BF16)
    GBD = dint("GBD", [NTOK, 8], F32)
    KTD = dint("KTD", [8, 96, NK], BF16); QTD = dint("QTD", [8, 96, NTOK], BF16)
    VSD = dint("VSD", [NK, 8, 65], BF16); OTD = dint("OTD", [8, 128, NTOK], BF16)

    tiles = [(512 * t, 512) for t in range(NT)] + [(NF, 80)]
    NTL = len(tiles)

    def subs(n):
        return [(s, min(128, n - 128 * s)) for s in range((n + 127) // 128)]

    with ExitStack() as st:
        SLOT_EL = 34368
        slots = [st.enter_context(nc.sbuf_tensor("wslot%d" % i, [128, SLOT_EL], BF16)) for i in range(2)]
        ARB = 69 * 1024
        art = st.enter_context(nc.sbuf_tensor("arena", [128, ARB], U8))
        AR = Arena(art)
        cst = st.enter_context(nc.sbuf_tensor("consts", [128, 6 * 128 + 96 * 2], F32))
        cstb = st.enter_context(nc.sbuf_tensor("constsb", [128, 128 + 96 * 2], BF16))
        psb = [st.enter_context(nc.psum_tensor("psb%d" % i, [128, 512], F32)) for i in range(8)]
        PS = [Tile(psb[i][:], [("PS", i)]) for i in range(8)]
        psr = Ring(PS)
        identf = Tile(cst[:, 0:128], ["c_identf"])
        onesf = Tile(cst[:, 128:256], ["c_onesf"])
        tri = Tile(cst[:, 256:384], ["c_tri"])
        mgt = Tile(cst[:, 384:512], ["c_mgt"])
        mlt = Tile(cst[:, 512:640], ["c_mlt"])
        identb = Tile(cstb[:, 0:128], ["c_identb"])
        b96 = Tile(cstb[:, 128:224], ["c_b96"])
        pT96 = Tile(cstb[:, 224:320], ["c_pT"])
        onesb = Tile(cst[:, 640:768].bitcast(BF16)[:, 0:128], ["c_onesb"])

        P.dma(lambda e: e.dma_start(out=identf[:], in_=c_ident[:, :]), [], [identf], "c0")
        P.dma(lambda e: e.dma_start(out=tri[:], in_=c_tri[:, :]), [], [tri], "c0")
        P.dma(lambda e: e.dma_start(out=mgt[:], in_=c_mgt[:, :]), [], [mgt], "c0")
        P.dma(lambda e: e.dma_start(out=mlt[:], in_=c_mlt[:, :]), [], [mlt], "c0")
        P.dma(lambda e: e.dma_start(out=identb[:], in_=c_ident[:, :]), [], [identb], "c1", q="pool")
        P.dma(lambda e: e.dma_start(out=b96[0:96, :], in_=c_b96[:, :]), [], [b96], "c1", q="pool")
        P.dma(lambda e: e.dma_start(out=pT96[0:96, :], in_=c_pT[:, :]), [], [pT96], "c1", q="pool")
        P.dve(lambda e: e.memset(onesf[:], 1.0), [], [onesf])
        P.dve(lambda e: e.memset(onesb[:], 1.0), [], [onesb])

        def wview(slot, off, kc, ncol):
            return slots[slot][:, off:off + kc * ncol].rearrange("p (k n) -> p k n", n=ncol)

        def wload(slot, part, off, src, kc, ncol):
            dst = wview(slot, off, kc, ncol)
            res = ("WS", slot)
            for k in range(kc):
                P.dma(lambda e, k=k: e.dma_start(out=dst[:, k, :], in_=src[k * 128:(k + 1) * 128, :]),
                      [], [res], ("w", slot, part, k % 3), q="pool")
            return Tile(dst, [res])

        def load_ffn(slot, which, l, half):
            g = fw["f%dg" % which][l]; u = fw["f%du" % which][l]; d = fw["f%dd" % which][l]
            c0 = half * HFF
            wg = wload(slot, 0, 0, g[:, c0:c0 + HFF], 8, HFF)
            wu = wload(slot, 1, 8 * HFF, u[:, c0:c0 + HFF], 8, HFF)
            wd = wload(slot, 2, 16 * HFF, d[c0:c0 + HFF, :], NCH, 1024)
            return wg, wu, wd

        def load_mix0(slot):
            a = wload(slot, 0, 0, w_in, 8, 2728)
            b = wload(slot, 1, 21824, wuq, 3, 768)
            c = wload(slot, 2, 24128, wukv, 2, 1024)
            d = wload(slot, 3, 26176, w_out, 8, 1024)
            return a, b, c, d

        def load_sc(slot):
            a = wload(slot, 0, 0, scin, 8, 3072)
            b = wload(slot, 1, 24576, scout, 8, 1024)
            return a, b

        def xrows(X, kind, r0, r):
            if kind == "in":
                if r0 < NF:
                    return [(0, r, xp[r0:r0 + r, :])]
                return [(0, 16, meta[:, :]), (16, 64, xs[:, :])]
            if kind == "out":
                if r0 < NF:
                    return [(0, r, y_p[r0:r0 + r, :])]
                return [(16, 64, y_s[:, :])]
            return [(0, r, X[r0:r0 + r, :])]

        def norm_T(h, gt, xt, hn, stat, src, key_src, ti, r0, n):
            for s, r in subs(n):
                x = xt.next()
                for (p0, cnt, sap) in xrows(src[1], src[0], r0 + 128 * s, r):
                    P.dma(lambda e, x=x, p0=p0, cnt=cnt, sap=sap: e.dma_start(out=x[p0:p0 + cnt, :], in_=sap),
                          [(key_src, ti)], [x], ("xt", x.ri))
                hb = hn.next(); s1 = stat.next(); s2 = stat.next()
                P.act(lambda e, x=x, hb=hb, s1=s1, r=r: e.activation(out=hb[0:r, :], in_=x[0:r, :], func=AF.Square, accum_out=s1[0:r, 0:1]), [x], [hb, s1])
                P.act(lambda e, s1=s1, s2=s2, r=r: e.activation(out=s2[0:r, 0:1], in_=s1[0:r, 0:1], func=AF.Ln, bias=EPS, scale=1.0 / 1024), [s1], [s2])
                P.act(lambda e, s2=s2, r=r: e.activation(out=s2[0:r, 1:2], in_=s2[0:r, 0:1], func=AF.Exp, scale=-0.5), [s2], [s2])
                P.dve(lambda e, x=x, hb=hb, s2=s2, r=r: e.scalar_tensor_tensor(out=hb[0:r, :], in0=x[0:r, :], scalar=s2[0:r, 1:2], in1=gt[0:r, :], op0=ALU.mult, op1=ALU.mult), [x, s2, gt], [hb])
                pt = psr.next()
                ptb = pt.ap.bitcast(BF16).rearrange("p (k t) -> p k t", t=128)
                for k in range(8):
                    P.pe(lambda e, k=k, hb=hb, ptb=ptb, r=r: e.transpose(out=ptb[:, k, 0:r], in_=hb[0:r, k * 128:(k + 1) * 128], identity=identb[0:r, 0:r]), [hb, identb], [pt])
                if s % 2 == 0:
                    P.act(lambda e, h=h, ptb=ptb, s=s, r=r: e.copy(out=h[:, :, s * 128:s * 128 + r], in_=ptb[:, :, 0:r]), [pt], [h])
                else:
                    P.dve(lambda e, h=h, ptb=ptb, s=s, r=r: e.tensor_copy(out=h[:, :, s * 128:s * 128 + r], in_=ptb[:, :, 0:r]), [pt], [h])

        def load_gt(gt, norm_ap):
            P.dma(lambda e: e.dma_start(out=gt[:], in_=norm_ap.partition_broadcast(128)), [], [gt], "gt")

        def ffn_pass(wts, half, norm_ap, src, dst, ti_key_src, ti_key_dst):
            wg, wu, wd = wts
            AR.reset()
            xt = Ring([AR.alloc([1024], F32) for _ in range(6)])
            hT = Ring([AR.alloc([8, 512], BF16) for _ in range(2)])
            actb = [AR.alloc([512], BF16) for _ in range(NCH)]
            hn = Ring([AR.alloc([1024], BF16) for _ in range(2)])
            sg = Ring([AR.alloc([512], BF16) for _ in range(2)])
            gt = AR.alloc([1024], F32)
            stat = Ring([AR.alloc([4], F32) for _ in range(4)])
            if half == 0:
                load_gt(gt, norm_ap)
            def hload(ti):
                r0, n = tiles[ti]
                h = hT.next()
                P.dma(lambda e, h=h, r0=r0, n=n: e.dma_start(out=h[:, :, 0:n], in_=HTD[:, :, r0:r0 + n].rearrange("k p t -> p k t")),
                      [("HTD", ti)], [h], ("hld", h.ri))
                return h

            hnext = hload(0) if half == 1 else None
            for ti, (r0, n) in enumerate(tiles):
                SB = subs(n)
                if half == 0:
                    h = hT.next()
                    norm_T(h, gt, xt, hn, stat, src, ti_key_src, ti, r0, n)
                    P.dma(lambda e, h=h, r0=r0, n=n: e.dma_start(out=HTD[:, :, r0:r0 + n].rearrange("k p t -> p k t"), in_=h[:, :, 0:n]),
                          [h], [("HTD", ti)], ("hst", h.ri))
                else:
                    h = hnext
                ys = []
                for s, r in SB:
                    y = xt.next()
                    ys.append(y)
                    for (p0, cnt, sap) in xrows(src[1], src[0], r0 + 128 * s, r):
                        P.dma(lambda e, y=y, p0=p0, cnt=cnt, sap=sap: e.dma_start(out=y[p0:p0 + cnt, :], in_=sap),
                              [(ti_key_src, ti)], [y], ("xt", y.ri))
                for c in range(NCH):
                    pg = psr.next(); pu = psr.next()
                    for k in range(8):
                        P.pe(lambda e, c=c, k=k, pg=pg, h=h, n=n: e.matmul(pg[:, 0:n], wg[:, k, c * 128:(c + 1) * 128], h[:, k, 0:n], start=(k == 0), stop=(k == 7)), [wg, h], [pg])
                    for k in range(8):
                        P.pe(lambda e, c=c, k=k, pu=pu, h=h, n=n: e.matmul(pu[:, 0:n], wu[:, k, c * 128:(c + 1) * 128], h[:, k, 0:n], start=(k == 0), stop=(k == 7)), [wu, h], [pu])
                    sgt = sg.next()
                    P.act(lambda e, sgt=sgt, pg=pg, n=n: e.activation(out=sgt[:, 0:n], in_=pg[:, 0:n], func=AF.Silu), [pg], [sgt])
                    P.dve(lambda e, c=c, sgt=sgt, pu=pu, n=n: e.tensor_tensor(out=actb[c][:, 0:n], in0=sgt[:, 0:n], in1=pu[:, 0:n], op=ALU.mult), [sgt, pu], [actb[c]])
                if half == 1 and ti + 1 < NTL:
                    hnext = hload(ti + 1)
                for s, r in SB:
                    y = ys[s]
                    for dh in range(2):
                        pd = psr.next()
                        for c in range(NCH):
                            P.pe(lambda e, c=c, pd=pd, s=s, r=r, dh=dh: e.matmul(pd[0:r, :], actb[c][:, s * 128:s * 128 + r], wd[:, c, dh * 512:(dh + 1) * 512], start=(c == 0), stop=(c == NCH - 1)), [wd, actb[c]], [pd])
                        P.dve(lambda e, pd=pd, y=y, r=r, dh=dh: e.scalar_tensor_tensor(out=y[0:r, dh * 512:(dh + 1) * 512], in0=pd[0:r, :], scalar=0.5, in1=y[0:r, dh * 512:(dh + 1) * 512], op0=ALU.mult, op1=ALU.add), [pd, y], [y])
                    for (p0, cnt, dap) in xrows(dst[1], dst[0], r0 + 128 * s, r):
                        P.dma(lambda e, y=y, p0=p0, cnt=cnt, dap=dap: e.dma_start(out=dap, in_=y[p0:p0 + cnt, :]),
                              [y], [(ti_key_dst, ti)], ("xo", y.ri))

        def sconv_pass(wts, src, dst, key_src, key_dst):
            wsi, wso = wts
            AR.reset()
            xt = Ring([AR.alloc([1024], F32) for _ in range(3)])
            hT = Ring([AR.alloc([8, 512], BF16) for _ in range(2)])
            hn = Ring([AR.alloc([1024], BF16) for _ in range(2)])
            gt = AR.alloc([1024], F32)
            stat = Ring([AR.alloc([4], F32) for _ in range(4)])
            gT = Ring([[AR.alloc([512], BF16) for _ in range(8)] for _ in range(2)])
            pcr = Ring([AR.alloc([516], F32) for _ in range(2)])
            tmp = Ring([AR.alloc([512], F32) for _ in range(2)])
            yb = Ring([AR.alloc([512], F32) for _ in range(2)])
            hist = {"p": AR.alloc([8, 2], F32), "s": AR.alloc([8, 2], F32)}
            cw = AR.alloc([3, 8], F32)
            load_gt(gt, mixn[1])
            for i in range(3):
                P.dma(lambda e, i=i: e.dma_start(out=cw[:, i, :], in_=sccw[i].rearrange("(c p) -> p c", p=128), allow_slow_non_contiguous=True), [], [cw], "cw")
            for t_ in range(2):
                P.dma(lambda e, t_=t_: e.dma_start(out=hist["s"][:, :, t_], in_=sconv[t_].rearrange("(c p) -> p c", p=128), allow_slow_non_contiguous=True), [], [hist["s"]], "hs")
            P.dve(lambda e: e.memset(hist["p"][:], 0.0), [], [hist["p"]])
            order = [NTL - 1] + list(range(NTL - 1))
            for ti in order:
                r0, n = tiles[ti]
                segs = [(0, 16, "p"), (16, 64, "s")] if ti == NTL - 1 else [(0, n, "p")]
                h = hT.next()
                norm_T(h, gt, xt, hn, stat, src, key_src, ti, r0, n)
                g = gT.next()
                for c in range(8):
                    pcg = psr.next(); pxi = psr.next(); pbg = psr.next()
                    for (pp, base) in ((pcg, 1024), (pxi, 2048), (pbg, 0)):
                        for k in range(8):
                            P.pe(lambda e, pp=pp, base=base, c=c, k=k, h=h, n=n: e.matmul(pp[:, 0:n], wsi[:, k, base + c * 128:base + (c + 1) * 128], h[:, k, 0:n], start=(k == 0), stop=(k == 7)), [wsi, h], [pp])
                    tm = tmp.next()
                    P.act(lambda e, tm=tm, pcg=pcg, n=n: e.copy(out=tm[:, 0:n], in_=pcg[:, 0:n]), [pcg], [tm])
                    for (c0, ns, sq) in segs:
                        pc = pcr.next(); y2 = yb.next(); hs = hist[sq]
                        P.act(lambda e, pc=pc, hs=hs, c=c: e.copy(out=pc[:, 0:2], in_=hs[:, c, :]), [hs], [pc])
                        P.dve(lambda e, pc=pc, tm=tm, pxi=pxi, c0=c0, ns=ns: e.tensor_tensor(out=pc[:, 2:2 + ns], in0=tm[:, c0:c0 + ns], in1=pxi[:, c0:c0 + ns], op=ALU.mult), [tm, pxi], [pc])
                        P.act(lambda e, pc=pc, hs=hs, c=c, ns=ns: e.copy(out=hs[:, c, :], in_=pc[:, ns:ns + 2]), [pc], [hs])
                        P.dve(lambda e, pc=pc, y2=y2, c=c, ns=ns: e.tensor_scalar(out=y2[:, 0:ns], in0=pc[:, 0:ns], scalar1=cw[:, 0, c:c + 1], scalar2=None, op0=ALU.mult), [pc, cw], [y2])
                        for i in (1, 2):
                            P.dve(lambda e, pc=pc, y2=y2, c=c, ns=ns, i=i: e.scalar_tensor_tensor(out=y2[:, 0:ns], in0=pc[:, i:i + ns], scalar=cw[:, i, c:c + 1], in1=y2[:, 0:ns], op0=ALU.mult, op1=ALU.add), [pc, cw, y2], [y2])
                        P.dve(lambda e, g=g, c=c, pbg=pbg, y2=y2, c0=c0, ns=ns: e.tensor_tensor(out=g[c][:, c0:c0 + ns], in0=pbg[:, c0:c0 + ns], in1=y2[:, 0:ns], op=ALU.mult), [pbg, y2], [g[c]])
                for s, r in subs(n):
                    y = xt.next()
                    for (p0, cnt, sap) in xrows(src[1], src[0], r0 + 128 * s, r):
                        P.dma(lambda e, y=y, p0=p0, cnt=cnt, sap=sap: e.dma_start(out=y[p0:p0 + cnt, :], in_=sap),
                              [(key_src, ti)], [y], ("xt", y.ri))
                    for dh in range(2):
                        pd = psr.next()
                        for c in range(8):
                            P.pe(lambda e, c=c, pd=pd, s=s, r=r, dh=dh, g=g: e.matmul(pd[0:r, :], g[c][:, s * 128:s * 128 + r], wso[:, c, dh * 512:(dh + 1) * 512], start=(c == 0), stop=(c == 7)), [wso, g[c]], [pd])
                        P.dve(lambda e, pd=pd, y=y, r=r, dh=dh: e.tensor_tensor(out=y[0:r, dh * 512:(dh + 1) * 512], in0=pd[0:r, :], in1=y[0:r, dh * 512:(dh + 1) * 512], op=ALU.add), [pd, y], [y])
                    for (p0, cnt, dap) in xrows(dst[1], dst[0], r0 + 128 * s, r):
                        P.dma(lambda e, y=y, p0=p0, cnt=cnt, dap=dap: e.dma_start(out=dap, in_=y[p0:p0 + cnt, :]),
                              [y], [(key_dst, ti)], ("xo", y.ri))
            for t_ in range(2):
                P.dma(lambda e, t_=t_: e.dma_start(out=o_psc[t_].rearrange("(c p) -> p c", p=128), in_=hist["p"][:, :, t_], allow_slow_non_contiguous=True), [hist["p"]], ["o_psc"], "ho")
                P.dma(lambda e, t_=t_: e.dma_start(out=o_ssc[t_].rearrange("(c p) -> p c", p=128), in_=hist["s"][:, :, t_], allow_slow_non_contiguous=True), [hist["s"]], ["o_ssc"], "ho")

        def p1_pass(wts, src, key_src):
            w_in_t = wts[0]
            AR.reset()
            xt = Ring([AR.alloc([1024], F32) for _ in range(2)])
            hT = Ring([AR.alloc([8, 512], BF16) for _ in range(2)])
            hn = Ring([AR.alloc([1024], BF16) for _ in range(2)])
            gt = AR.alloc([1024], F32)
            stat = Ring([AR.alloc([4], F32) for _ in range(4)])
            f5 = Ring([AR.alloc([512], F32) for _ in range(6)])
            b5 = Ring([AR.alloc([512], BF16) for _ in range(4)])
            xcr = Ring([AR.alloc([516], BF16) for _ in range(3)])
            dgr = Ring([[AR.alloc([128], BF16) for _ in range(4)] for _ in range(2)])
            histb = {"p": AR.alloc([12, 3], BF16), "s": AR.alloc([12, 3], BF16)}
            hc32 = {"p": AR.alloc([12, 3], F32), "s": AR.alloc([12, 3], F32)}
            cwg = AR.alloc([4, 12], F32)
            dtb_t = AR.alloc([4], F32); negA = AR.alloc([4], F32)
            sm = Ring([AR.alloc([8], F32) for _ in range(6)])
            load_gt(gt, mixn[0])
            for i in range(4):
                P.dma(lambda e, i=i: e.dma_start(out=cwg[:, i, :], in_=convw[i].rearrange("(c p) -> p c", p=128), allow_slow_non_contiguous=True), [], [cwg], "cw")
            for t_ in range(3):
                P.dma(lambda e, t_=t_: e.dma_start(out=hc32["s"][:, :, t_], in_=gconv[t_].rearrange("(c p) -> p c", p=128), allow_slow_non_contiguous=True), [], [hc32["s"]], "hs")
            P.act(lambda e: e.copy(out=histb["s"][:], in_=hc32["s"][:]), [hc32["s"]], [histb["s"]])
            P.dve(lambda e: e.memset(histb["p"][:], 0.0), [], [histb["p"]])
            P.dma(lambda e: e.dma_start(out=dtb_t[:], in_=dtb[0].partition_broadcast(128)), [], [dtb_t], "pv")
            P.dma(lambda e: e.dma_start(out=negA[:], in_=alog[0].partition_broadcast(128)), [], [negA], "pv2")
            P.act(lambda e: e.activation(out=negA[:], in_=negA[:], func=AF.Exp), [negA], [negA])
            P.dve(lambda e: e.tensor_scalar(out=negA[:], in0=negA[:], scalar1=-1.0, scalar2=None, op0=ALU.mult), [negA], [negA])
            pp_r = Ring(PS[0:3]); pc_r = Ring(PS[3:6]); pm_r = Ring(PS[6:8])
            order = [NTL - 1] + list(range(NTL - 1))
            for ti in order:
                r0, n = tiles[ti]
                segs = [(0, 16, "p"), (16, 64, "s")] if ti == NTL - 1 else [(0, n, "p")]
                h = hT.next()
                norm_T(h, gt, xt, hn, stat, src, key_src, ti, r0, n)
                P.dma(lambda e, h=h, r0=r0, n=n: e.dma_start(out=HTD[:, :, r0:r0 + n].rearrange("k p t -> p k t"), in_=h[:, :, 0:n]),
                      [h], [("HTD", ti)], ("hst", h.ri))
                SK = (n == 512)
                if SK:
                    P.begin_group()
                for c in range(12):
                    if SK:
                        P.stage(c, 0)
                    pp = pp_r.next()
                    for k in range(8):
                        P.pe(lambda e, pp=pp, c=c, k=k, h=h, n=n: e.matmul(pp[:, 0:n], w_in_t[:, k, c * 128:(c + 1) * 128], h[:, k, 0:n], start=(k == 0), stop=(k == 7)), [w_in_t, h], [pp])
                    if SK:
                        P.stage(c, 1)
                    dg = dgr.next()
                    for i in range(4):
                        P.dve(lambda e, dg=dg, i=i, c=c: e.tensor_scalar(out=dg[i][:], in0=identb[:], scalar1=cwg[:, i, c:c + 1], scalar2=None, op0=ALU.mult), [identb, cwg], [dg[i]])
                    for (c0, ns, sq_) in segs:
                        xc = xcr.next(); hb = histb[sq_]
                        P.act(lambda e, xc=xc, hb=hb, c=c: e.copy(out=xc[:, 0:3], in_=hb[:, c, :]), [hb], [xc])
                        P.act(lambda e, xc=xc, pp=pp, c0=c0, ns=ns: e.copy(out=xc[:, 3:3 + ns], in_=pp[:, c0:c0 + ns]), [pp], [xc])
                        P.dve(lambda e, xc=xc, hb=hb, c=c, ns=ns: e.tensor_copy(out=hb[:, c, :], in_=xc[:, ns:ns + 3]), [xc], [hb])
                        if sq_ == "s" or ti == NT - 1:
                            P.dve(lambda e, pp=pp, c=c, c0=c0, ns=ns, sq_=sq_: e.tensor_copy(out=hc32[sq_][:, c, :], in_=pp[:, c0 + ns - 3:c0 + ns]), [pp], [hc32[sq_]])
                        if SK:
                            P.stage(c, 2)
                        pc2 = pc_r.next()
                        for i in range(4):
                            P.pe(lambda e, pc2=pc2, dg=dg, i=i, xc=xc, ns=ns: e.matmul(pc2[:, 0:ns], dg[i][:], xc[:, i:i + ns], start=(i == 0), stop=(i == 3)), [dg[i], xc], [pc2])
                        if SK:
                            P.stage(c, 3)
                        so = f5.next()
                        P.act(lambda e, so=so, pc2=pc2, ns=ns: e.activation(out=so[:, 0:ns], in_=pc2[:, 0:ns], func=AF.Exp, scale=-1.0), [pc2], [so])
                        P.act(lambda e, so=so, ns=ns: e.activation(out=so[:, 0:ns], in_=so[:, 0:ns], func=AF.Ln, bias=1.0), [so], [so])
                        P.act(lambda e, so=so, ns=ns: e.activation(out=so[:, 0:ns], in_=so[:, 0:ns], func=AF.Exp, scale=-1.0), [so], [so])
                        P.dve(lambda e, so=so, pc2=pc2, ns=ns: e.tensor_tensor(out=so[:, 0:ns], in0=so[:, 0:ns], in1=pc2[:, 0:ns], op=ALU.mult), [so, pc2], [so])
                        if c < 8:
                            sq = b5.next()
                            P.act(lambda e, sq=sq, so=so, ns=ns: e.activation(out=sq[:, 0:ns], in_=so[:, 0:ns], func=AF.Square), [so], [sq])
                            if SK:
                                P.stage(c, 4)
                            pm = pm_r.next()
                            P.pe(lambda e, pm=pm, sq=sq, ns=ns: e.matmul(pm[:, 0:ns], onesb[:], sq[:, 0:ns], start=True, stop=True), [onesb, sq], [pm])
                            if SK:
                                P.stage(c, 5)
                            t1 = f5.next()
                            P.act(lambda e, t1=t1, pm=pm, ns=ns: e.activation(out=t1[:, 0:ns], in_=pm[:, 0:ns], func=AF.Ln, bias=EPS), [pm], [t1])
                            P.act(lambda e, t1=t1, ns=ns: e.activation(out=t1[:, 0:ns], in_=t1[:, 0:ns], func=AF.Exp, scale=-0.5), [t1], [t1])
                            oq = f5.next()
                            P.dve(lambda e, oq=oq, so=so, t1=t1, ns=ns, c=c: e.scalar_tensor_tensor(out=oq[:, 0:ns], in0=so[:, 0:ns], scalar=(128.0 ** -0.5 if c < 4 else 1.0), in1=t1[:, 0:ns], op0=ALU.mult, op1=ALU.mult), [so, t1], [oq])
                        else:
                            oq = so
                        if SK:
                            P.stage(c, 5)
                        P.dma(lambda e, oq=oq, c=c, r0=r0, c0=c0, ns=ns: e.dma_start(out=QKVD[c, :, r0 + c0:r0 + c0 + ns], in_=oq[:, 0:ns]), [oq], [("QKVD", ti)], ("f5", oq.ri))
                for c in range(4):
                    if SK:
                        P.stage(12 + c, 0)
                    pp = pp_r.next()
                    for k in range(8):
                        P.pe(lambda e, pp=pp, c=c, k=k, h=h, n=n: e.matmul(pp[:, 0:n], w_in_t[:, k, 1536 + c * 128:1536 + (c + 1) * 128], h[:, k, 0:n], start=(k == 0), stop=(k == 7)), [w_in_t, h], [pp])
                    if SK:
                        P.stage(12 + c, 1)
                    zb = b5.next(); zt = f5.next()
                    P.act(lambda e, zt=zt, pp=pp, n=n: e.activation(out=zt[:, 0:n], in_=pp[:, 0:n], func=AF.Exp, scale=-1.0), [pp], [zt])
                    P.act(lambda e, zt=zt, n=n: e.activation(out=zt[:, 0:n], in_=zt[:, 0:n], func=AF.Ln, bias=1.0), [zt], [zt])
                    P.act(lambda e, zt=zt, n=n: e.activation(out=zt[:, 0:n], in_=zt[:, 0:n], func=AF.Exp, scale=-1.0), [zt], [zt])
                    P.dve(lambda e, zb=zb, zt=zt, pp=pp, n=n: e.tensor_tensor(out=zb[:, 0:n], in0=zt[:, 0:n], in1=pp[:, 0:n], op=ALU.mult), [zt, pp], [zb])
                    P.dma(lambda e, zb=zb, c=c, r0=r0, n=n: e.dma_start(out=ZTD[c, :, r0:r0 + n], in_=zb[:, 0:n]), [zb], [("ZTD", ti)], ("b5", zb.ri))
                for s, r in subs(n):
                    if SK:
                        P.stage(16 + s, 0)
                    pp = pp_r.next()
                    for k in range(8):
                        P.pe(lambda e, pp=pp, k=k, h=h, s=s, r=r: e.matmul(pp[0:r, 0:8], h[:, k, s * 128:s * 128 + r], w_in_t[:, k, 2048:2056], start=(k == 0), stop=(k == 7)), [w_in_t, h], [pp])
                    if SK:
                        P.stage(16 + s, 1)
                    ta = sm.next(); tb = sm.next(); gb = sm.next()
                    P.dve(lambda e, ta=ta, pp=pp, r=r: e.tensor_tensor(out=ta[0:r, 0:4], in0=pp[0:r, 0:4], in1=dtb_t[0:r, :], op=ALU.add), [pp, dtb_t], [ta])
                    P.dve(lambda e, ta=ta, tb=tb, r=r: e.tensor_scalar(out=tb[0:r, 0:4], in0=ta[0:r, 0:4], scalar1=-1.0, scalar2=None, op0=ALU.mult), [ta], [tb])
                    P.dve(lambda e, ta=ta, tb=tb, r=r: e.tensor_tensor(out=tb[0:r, 0:4], in0=tb[0:r, 0:4], in1=ta[0:r, 0:4], op=ALU.min), [ta, tb], [tb])
                    P.act(lambda e, tb=tb, r=r: e.activation(out=tb[0:r, 0:4], in_=tb[0:r, 0:4], func=AF.Exp), [tb], [tb])
                    P.act(lambda e, tb=tb, r=r: e.activation(out=tb[0:r, 0:4], in_=tb[0:r, 0:4], func=AF.Ln, bias=1.0), [tb], [tb])
                    P.dve(lambda e, ta=ta, tb=tb, r=r: e.scalar_tensor_tensor(out=ta[0:r, 0:4], in0=ta[0:r, 0:4], scalar=0.0, in1=tb[0:r, 0:4], op0=ALU.max, op1=ALU.add), [ta, tb], [ta])
                    P.dve(lambda e, ta=ta, gb=gb, r=r: e.tensor_tensor(out=gb[0:r, 0:4], in0=ta[0:r, 0:4], in1=negA[0:r, :], op=ALU.mult), [ta, negA], [gb])
                    P.act(lambda e, gb=gb, pp=pp, r=r: e.activation(out=gb[0:r, 4:8], in_=pp[0:r, 4:8], func=AF.Exp, scale=-1.0), [pp], [gb])
                    P.act(lambda e, gb=gb, r=r: e.activation(out=gb[0:r, 4:8], in_=gb[0:r, 4:8], func=AF.Ln, bias=1.0), [gb], [gb])
                    P.act(lambda e, gb=gb, r=r: e.activation(out=gb[0:r, 4:8], in_=gb[0:r, 4:8], func=AF.Exp, scale=-1.0), [gb], [gb])
                    P.dma(lambda e, gb=gb, r0=r0, s=s, r=r: e.dma_start(out=GBD[r0 + s * 128:r0 + s * 128 + r, :], in_=gb[0:r, :]), [gb], [("GBD", ti)], ("sm", gb.ri))
                if SK:
                    P.end_group()
            for t_ in range(3):
                P.dma(lambda e, t_=t_: e.dma_start(out=o_pconv[t_].rearrange("(c p) -> p c", p=128), in_=hc32["p"][:, :, t_], allow_slow_non_contiguous=True), [hc32["p"]], ["o_pconv"], "ho")
                P.dma(lambda e, t_=t_: e.dma_start(out=o_sconv[t_].rearrange("(c p) -> p c", p=128), in_=hc32["s"][:, :, t_], allow_slow_non_contiguous=True), [hc32["s"]], ["o_sconv"], "ho")

        def p2_pass(wts):
            w_in_t, wuq_t, wukv_t = wts[0], wts[1], wts[2]
            wukv_v = wukv_t.ap.rearrange("p k (h t d) -> p k h t d", h=8, t=2)
            AR.reset()
            hT = Ring([AR.alloc([8, 512], BF16) for _ in range(1)])
            f5 = Ring([AR.alloc([512], F32) for _ in range(5)])
            b5 = Ring([AR.alloc([512], BF16) for _ in range(5)])
            c96 = Ring([AR.alloc([512], F32) for _ in range(2)]); s96 = Ring([AR.alloc([512], F32) for _ in range(2)])
            cst_r = Ring([AR.alloc([4, 32], F32) for _ in range(2)])
            cqs = [AR.alloc([512], F32) for _ in range(3)]
            cqn = [AR.alloc([512], BF16) for _ in range(3)]
            sqc = [AR.alloc([512], BF16) for _ in range(3)]
            ckvnT = Ring([[AR.alloc([512], BF16) for _ in range(2)] for _ in range(2)])
            krT = Ring([AR.alloc([512], BF16) for _ in range(2)])
            kvn = Ring([AR.alloc([288], F32) for _ in range(3)])
            gain_t = AR.alloc([288], F32)
            vt = Ring([AR.alloc([8, 65], BF16) for _ in range(2)])
            st = Ring([AR.alloc([4], F32) for _ in range(3)])
            r16 = Ring([AR.alloc([16], F32) for _ in range(4)])
            inv_t = AR.alloc([2], F32); gain96 = AR.alloc([1], F32); knn_t = AR.alloc([1], F32); qnorm_t = AR.alloc([3], F32)
            xck = Ring([AR.alloc([288], F32) for _ in range(4)])
            P.dma(lambda e: e.dma_start(out=gain_t[:, 0:256], in_=kvnorm[0].partition_broadcast(128)), [], [gain_t], "pv")
            P.dma(lambda e: e.dma_start(out=gain_t[:, 256:288], in_=krn[0].partition_broadcast(128)), [], [gain_t], "pv2")
            P.dma(lambda e: e.dma_start(out=gain96[0:64, :], in_=qnn[0].rearrange("(p o) -> p o", o=1)), [], [gain96], "pv3")
            P.dma(lambda e: e.dma_start(out=gain96[64:96, :], in_=qrn[0].rearrange("(p o) -> p o", o=1)), [], [gain96], "pv4")
            P.dma(lambda e: e.dma_start(out=knn_t[0:64, :], in_=knn[0].rearrange("(p o) -> p o", o=1)), [], [knn_t], "pv5")
            P.dma(lambda e: e.dma_start(out=qnorm_t[:], in_=qnorm[0].rearrange("(c p) -> p c", p=128), allow_slow_non_contiguous=True), [], [qnorm_t], "pv6")
            P.dve(lambda e: e.memset(inv_t[:, 0:1], 1.0 / 256), [], [inv_t])
            P.dve(lambda e: e.memset(inv_t[:, 1:2], 1.0 / 32), [], [inv_t])
            wv = AR.alloc([2, 512], BF16)
            for k in range(2):
                P.dve(lambda e, k=k: e.tensor_copy(out=wv[:, k, :].rearrange("p (h d) -> p h d", d=64), in_=wukv_v[:, k, :, 1, :]), [wukv_t], [wv])
            kvb = Ring([AR.alloc([384], BF16) for _ in range(2)])
            for v_ in vt.tiles:
                P.dve(lambda e, v_=v_: e.memset(v_[:], 1.0), [], [v_])
            for kv_ in kvb.tiles:
                P.dve(lambda e, kv_=kv_: e.memset(kv_[:], 0.0), [], [kv_])
            for kv_ in kvn.tiles:
                P.dve(lambda e, kv_=kv_: e.memset(kv_[:], 0.0), [], [kv_])

            def kv_expand(ck, kr_, n, kcol, vrow, key):
                for hh in range(8):
                    pk = psr.next()
                    for k in range(2):
                        P.pe(lambda e, pk=pk, k=k, hh=hh, ck=ck, n=n: e.matmul(pk[0:64, 0:n], wukv_t[:, k, hh * 128:hh * 128 + 64], ck[k][:, 0:n], start=(k == 0), stop=(k == 1)), [wukv_t, ck[k]], [pk])
                    sq = b5.next()
                    P.act(lambda e, sq=sq, pk=pk, n=n: e.activation(out=sq[0:64, 0:n], in_=pk[0:64, 0:n], func=AF.Square), [pk], [sq])
                    pm = psr.next()
                    P.pe(lambda e, pm=pm, sq=sq, n=n: e.matmul(pm[0:64, 0:n], b96[0:64, 0:64], sq[0:64, 0:n], start=True, stop=True), [b96, sq], [pm])
                    t1 = f5.next()
                    P.act(lambda e, t1=t1, pm=pm, n=n: e.activation(out=t1[0:64, 0:n], in_=pm[0:64, 0:n], func=AF.Ln, bias=EPS), [pm], [t1])
                    P.act(lambda e, t1=t1, n=n: e.activation(out=t1[0:64, 0:n], in_=t1[0:64, 0:n], func=AF.Exp, scale=-0.5), [t1], [t1])
                    kn = b5.next()
                    P.dve(lambda e, kn=kn, pk=pk, t1=t1, n=n: e.scalar_tensor_tensor(out=kn[0:64, 0:n], in0=pk[0:64, 0:n], scalar=knn_t[0:64, 0:1], in1=t1[0:64, 0:n], op0=ALU.mult, op1=ALU.mult), [pk, knn_t, t1], [kn])
                    P.dma(lambda e, kn=kn, hh=hh, n=n: e.dma_start(out=KTD[hh, 0:64, kcol:kcol + n], in_=kn[0:64, 0:n]), [kn], [key], ("b5", kn.ri))
                    P.dma(lambda e, kr_=kr_, hh=hh, n=n: e.dma_start(out=KTD[hh, 64:96, kcol:kcol + n], in_=kr_[0:32, 0:n]), [kr_], [key], ("krd", hh % 4))
                for s, r in subs(n):
                    pv = psr.next()
                    pvv = pv.ap.rearrange("p (h d) -> p h d", d=64)
                    for k in range(2):
                        P.pe(lambda e, pv=pv, k=k, s=s, r=r, ck=ck: e.matmul(pv[0:r, :], ck[k][:, s * 128:s * 128 + r], wv[:, k, :], start=(k == 0), stop=(k == 1)), [wv, ck[k]], [pv])
                    v_ = vt.next()
                    P.act(lambda e, v_=v_, pvv=pvv, r=r: e.copy(out=v_[0:r, :, 0:64], in_=pvv[0:r, :, :]), [pv], [v_])
                    P.dma(lambda e, v_=v_, s=s, r=r: e.dma_start(out=VSD[vrow + s * 128:vrow + s * 128 + r, :, :], in_=v_[0:r, :, :]), [v_], [key], ("vt", v_.ri))

            for ti, (r0, n) in enumerate(tiles):
                h = hT.next()
                P.dma(lambda e, h=h, r0=r0, n=n: e.dma_start(out=h[:, :, 0:n], in_=HTD[:, :, r0:r0 + n].rearrange("k p t -> p k t")),
                      [("HTD", ti)], [h], ("hld", h.ri))
                cs_t = cst_r.next(); Ct = c96.next(); St = s96.next()
                if n == 512:
                    P.dma(lambda e, cs_t=cs_t, r0=r0: e.dma_start(out=cs_t[:, :, :], in_=c_cs[r0:r0 + 512, :].rearrange("(s p) c -> p s c", p=128)), [], [cs_t], ("cst", cs_t.ri))
                else:
                    P.dma(lambda e, cs_t=cs_t, r0=r0, n=n: e.dma_start(out=cs_t[0:n, 0, :], in_=c_cs[r0:r0 + n, :]), [], [cs_t], ("cst", cs_t.ri))
                P.dma(lambda e, Ct=Ct, r0=r0, n=n: e.dma_start(out=Ct[0:96, 0:n], in_=c_C96[:, r0:r0 + n]), [], [Ct], ("c96", Ct.ri))
                P.dma(lambda e, St=St, r0=r0, n=n: e.dma_start(out=St[0:96, 0:n], in_=c_S96[:, r0:r0 + n]), [], [St], ("s96", St.ri))
                ck = ckvnT.next(); kr_ = krT.next()
                for s, r in (subs(n) if 'a' in P2PARTS else []):
                    pp = psr.next()
                    for k in range(8):
                        P.pe(lambda e, pp=pp, k=k, h=h, s=s, r=r: e.matmul(pp[0:r, 0:288], h[:, k, s * 128:s * 128 + r], w_in_t[:, k, 2440:2728], start=(k == 0), stop=(k == 7)), [w_in_t, h], [pp])
                    s1 = st.next(); s1b = st.next(); s2 = st.next(); jk = xck.next(); kv = kvn.next(); kraw = xck.next()
                    P.act(lambda e, kraw=kraw, pp=pp, r=r: e.copy(out=kraw[0:r, 0:288], in_=pp[0:r, 0:288]), [pp], [kraw])
                    P.act(lambda e, jk=jk, kraw=kraw, s1=s1, r=r: e.activation(out=jk[0:r, 0:256], in_=kraw[0:r, 0:256], func=AF.Square, accum_out=s1[0:r, 0:1]), [kraw], [jk, s1])
                    P.act(lambda e, jk=jk, kraw=kraw, s1b=s1b, r=r: e.activation(out=jk[0:r, 256:288], in_=kraw[0:r, 256:288], func=AF.Square, accum_out=s1b[0:r, 0:1]), [kraw], [jk, s1b])
                    P.act(lambda e, s1=s1, s2=s2, r=r: e.activation(out=s2[0:r, 0:1], in_=s1[0:r, 0:1], func=AF.Ln, bias=EPS, scale=1.0 / 256), [s1], [s2])
                    P.act(lambda e, s1b=s1b, s2=s2, r=r: e.activation(out=s2[0:r, 1:2], in_=s1b[0:r, 0:1], func=AF.Ln, bias=EPS, scale=1.0 / 32), [s1b], [s2])
                    P.act(lambda e, s2=s2, r=r: e.activation(out=s2[0:r, 0:2], in_=s2[0:r, 0:2], func=AF.Exp, scale=-0.5), [s2], [s2])
                    P.dve(lambda e, kv=kv, kraw=kraw, s2=s2, r=r: e.scalar_tensor_tensor(out=kv[0:r, 0:256], in0=kraw[0:r, 0:256], scalar=s2[0:r, 0:1], in1=gain_t[0:r, 0:256], op0=ALU.mult, op1=ALU.mult), [kraw, s2, gain_t], [kv])
                    P.dve(lambda e, jk=jk, kraw=kraw, s2=s2, r=r: e.scalar_tensor_tensor(out=jk[0:r, 256:288], in0=kraw[0:r, 256:288], scalar=s2[0:r, 1:2], in1=gain_t[0:r, 256:288], op0=ALU.mult, op1=ALU.mult), [kraw, s2, gain_t], [jk])
                    if A_LVL < 2:
                        continue
                    cosv = cs_t[0:r, s, 0:16]; sinv = cs_t[0:r, s, 16:32]
                    a1 = r16.next(); a2 = r16.next(); a3 = r16.next(); a4 = r16.next()
                    P.dve(lambda e, a1=a1, jk=jk, cosv=cosv, r=r: e.tensor_tensor(out=a1[0:r, :], in0=jk[0:r, 256:272], in1=cosv, op=ALU.mult), [jk, cs_t], [a1])
                    P.dve(lambda e, a2=a2, jk=jk, sinv=sinv, r=r: e.tensor_tensor(out=a2[0:r, :], in0=jk[0:r, 272:288], in1=sinv, op=ALU.mult), [jk, cs_t], [a2])
                    P.dve(lambda e, a3=a3, jk=jk, cosv=cosv, r=r: e.tensor_tensor(out=a3[0:r, :], in0=jk[0:r, 272:288], in1=cosv, op=ALU.mult), [jk, cs_t], [a3])
                    P.dve(lambda e, a4=a4, jk=jk, sinv=sinv, r=r: e.tensor_tensor(out=a4[0:r, :], in0=jk[0:r, 256:272], in1=sinv, op=ALU.mult), [jk, cs_t], [a4])
                    P.dve(lambda e, kv=kv, a1=a1, a2=a2, r=r: e.tensor_tensor(out=kv[0:r, 256:272], in0=a1[0:r, :], in1=a2[0:r, :], op=ALU.subtract), [a1, a2], [kv])
                    P.dve(lambda e, kv=kv, a3=a3, a4=a4, r=r: e.tensor_tensor(out=kv[0:r, 272:288], in0=a3[0:r, :], in1=a4[0:r, :], op=ALU.add), [a3, a4], [kv])
                    if A_LVL < 3:
                        continue
                    if r0 < NF:
                        rr = r0 + s * 128
                        P.dma(lambda e, kv=kv, rr=rr, r=r: e.dma_start(out=o_pckv[rr:rr + r, :], in_=kv[0:r, 0:256]), [kv], ["o_pckv"], ("kvo", kv.ri))
                        P.dma(lambda e, kv=kv, rr=rr, r=r: e.dma_start(out=o_pkr[rr:rr + r, :], in_=kv[0:r, 256:288]), [kv], ["o_pkr"], ("kvo2", kv.ri))
                    else:
                        P.dma(lambda e, kv=kv: e.dma_start(out=o_pckv[NF:NF + 16, :], in_=kv[0:16, 0:256]), [kv], ["o_pckv"], ("kvo", kv.ri))
                        P.dma(lambda e, kv=kv: e.dma_start(out=o_pkr[NF:NF + 16, :], in_=kv[0:16, 256:288]), [kv], ["o_pkr"], ("kvo2", kv.ri))
                        P.dma(lambda e, kv=kv: e.dma_start(out=o_sckv[:, :], in_=kv[16:80, 0:256]), [kv], ["o_sckv"], ("kvo", kv.ri))
                        P.dma(lambda e, kv=kv: e.dma_start(out=o_skr[:, :], in_=kv[16:80, 256:288]), [kv], ["o_skr"], ("kvo2", kv.ri))
                    if A_LVL < 4:
                        continue
                    pt = psr.next()
                    ptv = pt.ap.bitcast(BF16).rearrange("p (j t) -> p j t", t=128)
                    kb = kvb.next()
                    P.act(lambda e, kb=kb, kv=kv, r=r: e.copy(out=kb[0:r, 0:288], in_=kv[0:r, 0:288]), [kv], [kb])
                    for j in range(3):
                        P.pe(lambda e, ptv=ptv, j=j, kb=kb, r=r: e.transpose(out=ptv[:, j, 0:r], in_=kb[0:r, j * 128:(j + 1) * 128], identity=identb[0:r, 0:r]), [kb, identb], [pt])
                    if A_LVL < 5:
                        continue
                    for j in range(2):
                        P.act(lambda e, ck=ck, ptv=ptv, j=j, s=s, r=r: e.copy(out=ck[j][:, s * 128:s * 128 + r], in_=ptv[:, j, 0:r]), [pt], [ck[j]])
                    if A_LVL < 6:
                        continue
                    P.act(lambda e, kr_=kr_, ptv=ptv, s=s, r=r: e.copy(out=kr_[0:32, s * 128:s * 128 + r], in_=ptv[0:32, 2, 0:r]), [pt], [kr_])
                if 'b' in P2PARTS:
                    kv_expand(ck, kr_, n, r0, r0, ("KV", ti))
                if 'q' not in P2PARTS:
                    continue
                pm = psr.next()
                for c in range(3):
                    pc_ = psr.next()
                    for k in range(8):
                        P.pe(lambda e, pc_=pc_, c=c, k=k, h=h, n=n: e.matmul(pc_[:, 0:n], w_in_t[:, k, 2056 + c * 128:2056 + (c + 1) * 128], h[:, k, 0:n], start=(k == 0), stop=(k == 7)), [w_in_t, h], [pc_])
                    P.act(lambda e, c=c, pc_=pc_, n=n: e.copy(out=cqs[c][:, 0:n], in_=pc_[:, 0:n]), [pc_], [cqs[c]])
                    P.act(lambda e, c=c, pc_=pc_, n=n: e.activation(out=sqc[c][:, 0:n], in_=pc_[:, 0:n], func=AF.Square), [pc_], [sqc[c]])
                for c in range(3):
                    P.pe(lambda e, pm=pm, c=c, n=n: e.matmul(pm[:, 0:n], onesb[:], sqc[c][:, 0:n], start=(c == 0), stop=(c == 2)), [onesb, sqc[c]], [pm])
                t0_ = f5.next()
                P.act(lambda e, t0_=t0_, pm=pm, n=n: e.activation(out=t0_[:, 0:n], in_=pm[:, 0:n], func=AF.Ln, bias=EPS, scale=1.0 / 384), [pm], [t0_])
                P.act(lambda e, t0_=t0_, n=n: e.activation(out=t0_[:, 0:n], in_=t0_[:, 0:n], func=AF.Exp, scale=-0.5), [t0_], [t0_])
                for c in range(3):
                    P.dve(lambda e, c=c, t0_=t0_, n=n: e.scalar_tensor_tensor(out=cqn[c][:, 0:n], in0=cqs[c][:, 0:n], scalar=qnorm_t[:, c:c + 1], in1=t0_[:, 0:n], op0=ALU.mult, op1=ALU.mult), [cqs[c], qnorm_t, t0_], [cqn[c]])
                for hh in range(8):
                    pq = psr.next()
                    for c in range(3):
                        P.pe(lambda e, pq=pq, c=c, hh=hh, n=n: e.matmul(pq[0:96, 0:n], wuq_t[:, c, hh * 96:(hh + 1) * 96], cqn[c][:, 0:n], start=(c == 0), stop=(c == 2)), [wuq_t, cqn[c]], [pq])
                    sq = b5.next()
                    P.act(lambda e, sq=sq, pq=pq, n=n: e.activation(out=sq[0:96, 0:n], in_=pq[0:96, 0:n], func=AF.Square), [pq], [sq])
                    pm2 = psr.next()
                    P.pe(lambda e, pm2=pm2, sq=sq, n=n: e.matmul(pm2[0:96, 0:n], b96[0:96, 0:96], sq[0:96, 0:n], start=True, stop=True), [b96, sq], [pm2])
                    t1 = f5.next()
                    P.act(lambda e, t1=t1, pm2=pm2, n=n: e.activation(out=t1[0:96, 0:n], in_=pm2[0:96, 0:n], func=AF.Ln, bias=EPS), [pm2], [t1])
                    P.act(lambda e, t1=t1, n=n: e.activation(out=t1[0:96, 0:n], in_=t1[0:96, 0:n], func=AF.Exp, scale=-0.5), [t1], [t1])
                    qn = b5.next()
                    P.dve(lambda e, qn=qn, pq=pq, t1=t1, n=n: e.scalar_tensor_tensor(out=qn[0:96, 0:n], in0=pq[0:96, 0:n], scalar=gain96[0:96, 0:1], in1=t1[0:96, 0:n], op0=ALU.mult, op1=ALU.mult), [pq, gain96, t1], [qn])
                    pr = psr.next()
                    P.pe(lambda e, pr=pr, qn=qn, n=n: e.matmul(pr[0:96, 0:n], pT96[0:96, 0:96], qn[0:96, 0:n], start=True, stop=True), [pT96, qn], [pr])
                    u1 = f5.next(); u2 = f5.next(); qf = b5.next()
                    P.dve(lambda e, u1=u1, qn=qn, Ct=Ct, n=n: e.tensor_tensor(out=u1[0:96, 0:n], in0=qn[0:96, 0:n], in1=Ct[0:96, 0:n], op=ALU.mult), [qn, Ct], [u1])
                    P.dve(lambda e, u2=u2, pr=pr, St=St, n=n: e.tensor_tensor(out=u2[0:96, 0:n], in0=pr[0:96, 0:n], in1=St[0:96, 0:n], op=ALU.mult), [pr, St], [u2])
                    P.dve(lambda e, qf=qf, u1=u1, u2=u2, n=n: e.tensor_tensor(out=qf[0:96, 0:n], in0=u1[0:96, 0:n], in1=u2[0:96, 0:n], op=ALU.add), [u1, u2], [qf])
                    P.dma(lambda e, qf=qf, hh=hh, r0=r0, n=n: e.dma_start(out=QTD[hh, :, r0:r0 + n], in_=qf[0:96, 0:n]), [qf], [("QTD", ti)], ("b5", qf.ri))
            for ct in (range(2) if 'c' in P2PARTS else []):
                ck = ckvnT.next(); kr_ = krT.next()
                for s in range(4):
                    rr = ct * 512 + s * 128
                    kv = kvn.next()
                    P.dma(lambda e, kv=kv, rr=rr: e.dma_start(out=kv[:, 0:256], in_=cckv[rr:rr + 128, :]), [], [kv], ("kvo", kv.ri))
                    P.dma(lambda e, kv=kv, rr=rr: e.dma_start(out=kv[:, 256:288], in_=ckr[rr:rr + 128, :]), [], [kv], ("kvo2", kv.ri))
                    pt = psr.next()
                    ptv = pt.ap.bitcast(BF16).rearrange("p (j t) -> p j t", t=128)
                    kb = kvb.next()
                    P.act(lambda e, kb=kb, kv=kv: e.copy(out=kb[:, 0:288], in_=kv[:, 0:288]), [kv], [kb])
                    for j in range(3):
                        P.pe(lambda e, ptv=ptv, j=j, kb=kb: e.transpose(out=ptv[:, j, :], in_=kb[:, j * 128:(j + 1) * 128], identity=identb[:]), [kb, identb], [pt])
                    for j in range(2):
                        P.act(lambda e, ck=ck, ptv=ptv, j=j, s=s: e.copy(out=ck[j][:, s * 128:(s + 1) * 128], in_=ptv[:, j, :]), [pt], [ck[j]])
                    P.act(lambda e, kr_=kr_, ptv=ptv, s=s: e.copy(out=kr_[0:32, s * 128:(s + 1) * 128], in_=ptv[0:32, 2, :]), [pt], [kr_])
                kv_expand(ck, kr_, 512, NTOK + ct * 512, NTOK + ct * 512, ("KV", "c%d" % ct))

        def g_pass():
            AR.reset()
            def T4():
                return AR.alloc([4, 128], F32)
            ld = [dict(q=T4(), k=T4(), v=T4(), z=AR.alloc([4, 128], BF16), gb=AR.alloc([8], F32)) for _ in range(2)]
            Rt = T4(); D1 = T4(); D2 = T4(); Ege = T4(); Egt = T4(); Elt = T4(); QKT = T4(); tmpM = T4()
            def T2():
                return AR.alloc([2, 128], F32)
            Pk = [[T2(), T2()], [T2(), T2()]]; Qk = [[T2(), T2()], [T2(), T2()]]; Rk = [[T2(), T2()], [T2(), T2()]]
            Vb = T4(); Kb = T4(); kt = T4(); nWkT = T4(); Wsb = T4(); qdT = T4(); og = T4()
            sqo = AR.alloc([4, 128], BF16); ogz = Ring([AR.alloc([4, 128], BF16) for _ in range(2)])
            Ss = {"p": T4(), "s": T4()}
            smr = Ring([AR.alloc([4], F32) for _ in range(12)])
            onorm_t = AR.alloc([1], F32)
            P.dma(lambda e: e.dma_start(out=onorm_t[:, :], in_=onorm[0].rearrange("(p o) -> p o", o=1)), [], [onorm_t], "pv")
            for d in ld:
                for nm in ("q", "k", "v", "gb", "z"):
                    P.dve(lambda e, t=d[nm]: e.memset(t[:], 0.0), [], [d[nm]])
            P.dve(lambda e: e.memset(Ss["p"][:], 0.0), [], [Ss["p"]])
            P.dma(lambda e: e.dma_start(out=Ss["s"][:, :, :], in_=gS.rearrange("h k v -> k h v")), [], [Ss["s"]], "pv2")
            chunks = [(NF + 16, 64, "s", NTL - 1), (NF, 16, "p", NTL - 1)] + [(128 * ci, 128, "p", ci // 4) for ci in range(4 * NT)]
            tri_b = tri.ap.unsqueeze(1).broadcast_to([128, 4, 128])
            mgt_b = mgt.ap.unsqueeze(1).broadcast_to([128, 4, 128])
            mlt_b = mlt.ap.unsqueeze(1).broadcast_to([128, 4, 128])
            id_b = identf.ap.unsqueeze(1).broadcast_to([128, 4, 128])

            def loads(i):
                row0, C, sq_, ti = chunks[i]
                d = ld[i % 2]
                for a, nm in enumerate(("q", "k", "v")):
                    P.dma(lambda e, a=a, t=d[nm], row0=row0, C=C: e.dma_start(out=t[:, :, 0:C], in_=QKVD[4 * a:4 * a + 4, :, row0:row0 + C].rearrange("h p t -> p h t")),
                          [("QKVD", ti)], [d[nm]], ("gl", nm, i % 2))
                P.dma(lambda e, t=d["z"], row0=row0, C=C: e.dma_start(out=t[:, :, 0:C], in_=ZTD[:, :, row0:row0 + C].rearrange("h p t -> p h t")),
                      [("ZTD", ti)], [d["z"]], ("gl", "z", i % 2))
                P.dma(lambda e, t=d["gb"], row0=row0, C=C: e.dma_start(out=t[0:C, :], in_=GBD[row0:row0 + C, :]),
                      [("GBD", ti)], [d["gb"]], ("gl", "gb", i % 2))

            def mm4(ps, lh, rh, start=True, stop=True, R=(), lh2=None):
                for hh in range(4):
                    P.pe(lambda e, hh=hh: e.matmul(ps[:, hh * 128:(hh + 1) * 128], lh[:, hh, :], rh[:, hh, :], start=start, stop=stop), list(R), [ps])

            def compute(i):
                row0, C, sq_, ti = chunks[i]
                d = ld[i % 2]
                qT, kT, vT, zT, gb = d["q"], d["k"], d["v"], d["z"], d["gb"]
                S = Ss[sq_]
                pG = psr.next(); pGL = psr.next()
                P.pe(lambda e: e.matmul(pG[:, 0:4], tri[:], gb[:, 0:4], start=True, stop=True), [tri, gb], [pG])
                P.pe(lambda e: e.matmul(pGL[:, 0:4], onesf[:], gb[:, 0:4], start=True, stop=True), [onesf, gb], [pGL])
                G = smr.next(); eGL = smr.next(); eG = smr.next(); bEG = smr.next(); eGm = smr.next()
                P.act(lambda e: e.copy(out=G[:, :], in_=pG[:, 0:4]), [pG], [G])
                P.act(lambda e: e.activation(out=eGL[:, :], in_=pGL[:, 0:4], func=AF.Exp), [pGL], [eGL])
                P.act(lambda e: e.activation(out=eG[:, :], in_=pG[:, 0:4], func=AF.Exp), [pG], [eG])
                P.dve(lambda e: e.tensor_tensor(out=bEG[:, :], in0=eG[:, :], in1=gb[:, 4:8], op=ALU.mult), [eG, gb], [bEG])
                P.dve(lambda e: e.tensor_tensor(out=eGm[:, :], in0=pGL[:, 0:4], in1=G[:, :], op=ALU.subtract), [pGL, G], [eGm])
                P.act(lambda e: e.activation(out=eGm[:, :], in_=eGm[:, :], func=AF.Exp), [eGm], [eGm])
                for hh in range(4):
                    P.dve(lambda e, hh=hh: e.tensor_scalar(out=Rt[:, hh, :], in0=tri[:], scalar1=gb[:, hh:hh + 1], scalar2=None, op0=ALU.mult), [tri, gb], [Rt])
                pGrow = psr.next()
                P.pe(lambda e: e.matmul(pGrow[:, :], onesf[:], Rt[:, :, :], start=True, stop=True), [onesf, Rt], [pGrow])
                pGr = pGrow.ap.rearrange("p (h t) -> p h t", t=128)
                for hh in range(4):
                    P.dve(lambda e, hh=hh: e.tensor_scalar(out=D1[:, hh, :], in0=pGr[:, hh, :], scalar1=G[:, hh:hh + 1], scalar2=0.0, op0=ALU.subtract, op1=ALU.min), [pGrow, G], [D1])
                    P.dve(lambda e, hh=hh: e.tensor_scalar(out=D2[:, hh, :], in0=pGr[:, hh, :], scalar1=G[:, hh:hh + 1], scalar2=0.0, op0=ALU.subtract, op1=ALU.max), [pGrow, G], [D2])
                P.act(lambda e: e.activation(out=D1[:, :, :], in_=D1[:, :, :], func=AF.Exp), [D1], [D1])
                P.act(lambda e: e.activation(out=D2[:, :, :], in_=D2[:, :, :], func=AF.Exp, scale=-1.0), [D2], [D2])
                P.act(lambda e: e.activation(out=qdT[:, :, :], in_=pGr[:, :, :], func=AF.Exp), [pGrow], [qdT])
                P.dve(lambda e: e.tensor_tensor(out=qdT[:, :, :], in0=qdT[:, :, :], in1=qT[:, :, :], op=ALU.mult), [qdT, qT], [qdT])
                P.dve(lambda e: e.tensor_tensor(out=Ege[:, :, :], in0=D1[:, :, :], in1=tri_b, op=ALU.mult), [D1, tri], [Ege])
                P.dve(lambda e: e.tensor_tensor(out=Egt[:, :, :], in0=D1[:, :, :], in1=mgt_b, op=ALU.mult), [D1, mgt], [Egt])
                P.dve(lambda e: e.tensor_tensor(out=Elt[:, :, :], in0=D2[:, :, :], in1=mlt_b, op=ALU.mult), [D2, mlt], [Elt])
                for hh in range(4):
                    P.dve(lambda e, hh=hh: e.tensor_scalar(out=Rt[:, hh, :], in0=identf[:], scalar1=gb[:, 4 + hh:5 + hh], scalar2=None, op0=ALU.mult), [identf, gb], [Rt])
                pBrow = psr.next()
                P.pe(lambda e: e.matmul(pBrow[:, :], onesf[:], Rt[:, :, :], start=True, stop=True), [onesf, Rt], [pBrow])
                pBr = pBrow.ap.rearrange("p (h t) -> p h t", t=128)
                pKK = psr.next(); pKQ = psr.next()
                mm4(pKK, kT, kT, R=[kT]); mm4(pKQ, kT, qT, R=[kT, qT])
                pKKv = pKK.ap.rearrange("p (h t) -> p h t", t=128); pKQv = pKQ.ap.rearrange("p (h t) -> p h t", t=128)
                P.dve(lambda e: e.tensor_tensor(out=QKT[:, :, :], in0=pKQv, in1=Ege[:, :, :], op=ALU.mult), [pKQ, Ege], [QKT])
                P.dve(lambda e: e.scalar_tensor_tensor(out=tmpM[:, :, :], in0=pKKv, scalar=-1.0, in1=Egt[:, :, :], op0=ALU.mult, op1=ALU.mult), [pKK, Egt], [tmpM])
                P0, Q0, R0 = Pk[0], Qk[0], Rk[0]
                id_b2 = identf.ap.unsqueeze(1).broadcast_to([128, 2, 128])
                for g in range(2):
                    P.dve(lambda e, g=g: e.tensor_tensor(out=P0[g][:, :, :], in0=tmpM[:, 2 * g:2 * g + 2, :], in1=pBr[:, 2 * g:2 * g + 2, :], op=ALU.mult), [tmpM, pBrow], [P0[g]])
                tmpQ = D1
                P.dve(lambda e: e.tensor_tensor(out=tmpQ[:, :, :], in0=pKKv, in1=Elt[:, :, :], op=ALU.mult), [pKK, Elt], [tmpQ])
                for hh in range(4):
                    P.dve(lambda e, hh=hh: e.tensor_scalar(out=Q0[hh // 2][:, hh % 2, :], in0=tmpQ[:, hh, :], scalar1=gb[:, 4 + hh:5 + hh], scalar2=-1.0, op0=ALU.mult, op1=ALU.mult), [tmpQ, gb], [Q0[hh // 2]])
                for g in range(2):
                    P.dve(lambda e, g=g: e.tensor_tensor(out=R0[g][:, :, :], in0=P0[g][:, :, :], in1=id_b2, op=ALU.add), [P0[g], identf], [R0[g]])
                cur = 0

                def mmg(ps, lh, rh, R=()):
                    for j in range(2):
                        P.pe(lambda e, j=j: e.matmul(ps[:, j * 128:(j + 1) * 128], lh[:, j, :], rh[:, j, :], start=True, stop=True), list(R), [ps])

                def pv2(ps):
                    return ps.ap[:, 0:256].rearrange("p (h t) -> p h t", t=128)

                for lvl in range(1, 7):
                    Pc, Qc, Rc = Pk[cur], Qk[cur], Rk[cur]
                    Pn, Qn, Rn = Pk[1 - cur], Qk[1 - cur], Rk[1 - cur]
                    pQ = [psr.next(), psr.next()]
                    pP = [psr.next(), psr.next()] if lvl < 6 else None
                    for g in range(2):
                        mmg(pQ[g], Pc[g], Qc[g], R=[Pc[g], Qc[g]])
                        if lvl < 6:
                            mmg(pP[g], Qc[g], Pc[g], R=[Pc[g], Qc[g]])
                    for g in range(2):
                        P.act(lambda e, g=g, Qn=Qn, pq=pQ[g]: e.copy(out=Qn[g][:, :, :], in_=pv2(pq)), [pQ[g]], [Qn[g]])
                        if lvl < 6:
                            P.dve(lambda e, g=g, Pn=Pn, pp_=pP[g]: e.tensor_copy(out=Pn[g][:, :, :], in_=pv2(pp_)), [pP[g]], [Pn[g]])
                    pR = [psr.next(), psr.next()]
                    for g in range(2):
                        mmg(pR[g], Qn[g], Rc[g], R=[Qn[g], Rc[g]])
                    for g in range(2):
                        P.dve(lambda e, g=g, Rn=Rn, Rc=Rc, pr=pR[g]: e.tensor_tensor(out=Rn[g][:, :, :], in0=Rc[g][:, :, :], in1=pv2(pr), op=ALU.add), [Rc[g], pR[g]], [Rn[g]])
                    cur = 1 - cur
                TT = Rk[cur]
                pK = psr.next(); pV = psr.next()
                for hh in range(4):
                    P.pe(lambda e, hh=hh: e.matmul(pK[:, hh * 128:(hh + 1) * 128], kT[:, hh, :], identf[:], start=True, stop=True), [kT, identf], [pK])
                for hh in range(4):
                    P.pe(lambda e, hh=hh: e.matmul(pV[:, hh * 128:(hh + 1) * 128], vT[:, hh, :], identf[:], start=True, stop=True), [vT, identf], [pV])
                for hh in range(4):
                    P.act(lambda e, hh=hh: e.activation(out=Vb[:, hh, :], in_=pV[:, hh * 128:(hh + 1) * 128], func=AF.Copy, scale=gb[:, 4 + hh:5 + hh]), [pV, gb], [Vb])
                    P.dve(lambda e, hh=hh: e.tensor_scalar(out=Kb[:, hh, :], in0=pK[:, hh * 128:(hh + 1) * 128], scalar1=bEG[:, hh:hh + 1], scalar2=None, op0=ALU.mult), [pK, bEG], [Kb])
                    P.act(lambda e, hh=hh: e.activation(out=kt[:, hh, :], in_=pK[:, hh * 128:(hh + 1) * 128], func=AF.Copy, scale=eGm[:, hh:hh + 1]), [pK, eGm], [kt])
                pWk = psr.next()
                for hh in range(4):
                    P.pe(lambda e, hh=hh: e.matmul(pWk[:, hh * 128:(hh + 1) * 128], Kb[:, hh, :], TT[hh // 2][:, hh % 2, :], start=True, stop=True), [Kb, TT[hh // 2]], [pWk])
                P.act(lambda e: e.mul(out=nWkT[:, :, :], in_=pWk.ap.rearrange("p (h t) -> p h t", t=128), mul=-1.0), [pWk], [nWkT])
                pW = psr.next()
                for hh in range(4):
                    P.pe(lambda e, hh=hh: e.matmul(pW[:, hh * 128:(hh + 1) * 128], TT[hh // 2][:, hh % 2, :], Vb[:, hh, :], start=True, stop=False), [TT[hh // 2], Vb], [pW])
                    P.pe(lambda e, hh=hh: e.matmul(pW[:, hh * 128:(hh + 1) * 128], nWkT[:, hh, :], S[:, hh, :], start=False, stop=True), [nWkT, S], [pW])
                P.act(lambda e: e.copy(out=Wsb[:, :, :], in_=pW.ap.rearrange("p (h t) -> p h t", t=128)), [pW], [Wsb])
                pO = psr.next()
                for hh in range(4):
                    P.pe(lambda e, hh=hh: e.matmul(pO[:, hh * 128:(hh + 1) * 128], S[:, hh, :], qdT[:, hh, :], start=True, stop=False), [S, qdT], [pO])
                    P.pe(lambda e, hh=hh: e.matmul(pO[:, hh * 128:(hh + 1) * 128], Wsb[:, hh, :], QKT[:, hh, :], start=False, stop=True), [Wsb, QKT], [pO])
                pS_ = psr.next()
                mm4(pS_, kt, Wsb, R=[kt, Wsb])
                for hh in range(4):
                    P.dve(lambda e, hh=hh: e.scalar_tensor_tensor(out=S[:, hh, :], in0=S[:, hh, :], scalar=eGL[:, hh:hh + 1], in1=pS_[:, hh * 128:(hh + 1) * 128], op0=ALU.mult, op1=ALU.add), [S, eGL, pS_], [S])
                P.act(lambda e: e.activation(out=sqo[:, :, :], in_=pO.ap.rearrange("p (h t) -> p h t", t=128), func=AF.Square), [pO], [sqo])
                pMS = psr.next()
                P.pe(lambda e: e.matmul(pMS[:, :], onesb[:], sqo[:, :, :], start=True, stop=True), [onesb, sqo], [pMS])
                P.act(lambda e: e.activation(out=og[:, :, :], in_=pMS.ap.rearrange("p (h t) -> p h t", t=128), func=AF.Ln, bias=EPS, scale=1.0 / 128), [pMS], [og])
                P.act(lambda e: e.activation(out=og[:, :, :], in_=og[:, :, :], func=AF.Exp, scale=-0.5), [og], [og])
                P.dve(lambda e: e.scalar_tensor_tensor(out=og[:, :, :], in0=pO.ap.rearrange("p (h t) -> p h t", t=128), scalar=onorm_t[:, 0:1], in1=og[:, :, :], op0=ALU.mult, op1=ALU.mult), [pO, onorm_t, og], [og])
                oz = ogz.next()
                P.dve(lambda e: e.tensor_tensor(out=oz[:, :, :], in0=og[:, :, :], in1=zT[:, :, :], op=ALU.mult), [og, zT], [oz])
                P.dma(lambda e: e.dma_start(out=OTD[0:4, :, row0:row0 + C].rearrange("h p t -> p h t"), in_=oz[:, :, 0:C]), [oz], [("OTD", ti)], ("ogz", oz.ri))

            loads(0)
            for i in range(len(chunks)):
                if i + 1 < len(chunks):
                    loads(i + 1)
                compute(i)
                if i == 0:
                    P.dma(lambda e: e.dma_start(out=o_sS.rearrange("h k v -> k h v"), in_=Ss["s"][:, :, :]), [Ss["s"]], ["o_sS"], "so")
            P.dma(lambda e: e.dma_start(out=o_pS.rearrange("h k v -> k h v"), in_=Ss["p"][:, :, :]), [Ss["p"]], ["o_pS"], "so")

        def a_pass():
            AR.reset()
            NKT = 4 * NT + 2 + 8
            hd = [dict(K=AR.alloc([NK], BF16), Q=AR.alloc([NTOK], BF16), V=AR.alloc([NKT, 65], BF16)) for _ in range(2)]
            ptr = Ring([AR.alloc([512], BF16) for _ in range(4)])
            amask = AR.alloc([4, 512], BF16)
            osb = Ring([AR.alloc([512], F32) for _ in range(2)])
            rrow = Ring([AR.alloc([512], F32) for _ in range(2)])
            onr = Ring([AR.alloc([512], BF16) for _ in range(2)])
            P.dma(lambda e: e.dma_start(out=amask[:, :, :], in_=c_amask[:, :, :]), [], [amask], "c1", q="pool")
            psS = Ring(PS[0:5]); psO = Ring(PS[5:7]); psB = PS[7]
            allkv = [("KV", ti) for ti in range(NTL)] + [("KV", "c0"), ("KV", "c1")]
            allq = [("QTD", ti) for ti in range(NTL)]
            SC = 96.0 ** -0.5

            def hloads(hh):
                d = hd[hh % 2]
                P.dma(lambda e: e.dma_start(out=d["K"][0:96, :], in_=KTD[hh, :, :]), allkv, [d["K"]], ("aK", hh % 2))
                P.dma(lambda e: e.dma_start(out=d["Q"][0:96, :], in_=QTD[hh, :, :]), allq, [d["Q"]], ("aQ", hh % 2))
                for t in range(NT):
                    P.dma(lambda e, t=t: e.dma_start(out=d["V"][:, 4 * t:4 * t + 4, :], in_=VSD[512 * t:512 * t + 512, hh, :].rearrange("(k p) d -> p k d", p=128)), allkv, [d["V"]], ("aV", hh % 2, t % 2))
                P.dma(lambda e: e.dma_start(out=d["V"][0:16, 4 * NT, :], in_=VSD[NF:NF + 16, hh, :]), allkv, [d["V"]], ("aV", hh % 2, 0))
                P.dma(lambda e: e.dma_start(out=d["V"][0:64, 4 * NT + 1, :], in_=VSD[NF + 16:NF + 80, hh, :]), allkv, [d["V"]], ("aV", hh % 2, 1))
                for t in range(2):
                    P.dma(lambda e, t=t: e.dma_start(out=d["V"][:, 4 * NT + 2 + 4 * t:4 * NT + 6 + 4 * t, :], in_=VSD[NTOK + 512 * t:NTOK + 512 * t + 512, hh, :].rearrange("(k p) d -> p k d", p=128)), allkv, [d["V"]], ("aV", hh % 2, t % 2))

            pend = []

            def attend(hh, d, q0, nq, keys, ti):
                pO = psO.next()
                pSs = {}

                def emitS(ki):
                    kc, nk, vti, mk = keys[ki]
                    pS = psS.next()
                    pSs[ki] = pS
                    P.pe(lambda e: e.matmul(pS[0:nk, 0:nq], d["K"][0:96, kc:kc + nk], d["Q"][0:96, q0:q0 + nq], start=True, stop=True), [d["K"], d["Q"]], [pS])

                LA = 4
                for ki in range(min(LA, len(keys))):
                    emitS(ki)
                for ki, (kc, nk, vti, mk) in enumerate(keys):
                    pS = pSs.pop(ki); pt = ptr.next()
                    P.act(lambda e, pS=pS, pt=pt, nk=nk: e.activation(out=pt[0:nk, 0:nq], in_=pS[0:nk, 0:nq], func=AF.Exp, scale=SC), [pS], [pt])
                    if mk is not None:
                        P.dve(lambda e, pt=pt, mk=mk, nk=nk: e.tensor_tensor(out=pt[0:nk, 0:nq], in0=pt[0:nk, 0:nq], in1=amask[0:nk, mk, 0:nq], op=ALU.mult), [pt, amask], [pt])
                    if ki + LA < len(keys):
                        emitS(ki + LA)
                    P.pe(lambda e, pt=pt, nk=nk, vti=vti, ki=ki: e.matmul(pO[0:65, 0:nq], d["V"][0:nk, vti, :], pt[0:nk, 0:nq], start=(ki == 0), stop=(ki == len(keys) - 1)), [d["V"], pt], [pO])
                    if ki == 2 and pend:
                        pend.pop()()
                if pend:
                    pend.pop()()
                rr = rrow.next(); ob = osb.next(); on = onr.next()
                P.dve(lambda e: e.reciprocal(out=rr[64:65, 0:nq], in_=pO[64:65, 0:nq]), [pO], [rr])
                P.act(lambda e: e.copy(out=ob[0:64, 0:nq], in_=pO[0:64, 0:nq]), [pO], [ob])

                def fin_b():
                    P.pe(lambda e: e.matmul(psB[0:64, 0:nq], onesf[64:65, 0:64], rr[64:65, 0:nq], start=True, stop=True), [onesf, rr], [psB])
                    P.dve(lambda e: e.tensor_tensor(out=on[0:64, 0:nq], in0=ob[0:64, 0:nq], in1=psB[0:64, 0:nq], op=ALU.mult), [ob, psB], [on])
                    P.dma(lambda e: e.dma_start(out=OTD[4 + hh // 2, (hh % 2) * 64:(hh % 2) * 64 + 64, q0:q0 + nq], in_=on[0:64, 0:nq]), [on], [("OTD", ti)], ("aO", on.ri))
                pend.append(fin_b)

            hloads(0)
            for hh in range(8):
                if hh + 1 < 8:
                    hloads(hh + 1)
                d = hd[hh % 2]
                meta_k = (NF, 16, 4 * NT, None)
                attend(hh, d, NF, 16, [meta_k], NTL - 1)
                attend(hh, d, NF + 16, 64, [(NTOK + 128 * j, 128, 4 * NT + 2 + j, None) for j in range(8)] + [(NF + 16, 64, 4 * NT + 1, None)], NTL - 1)
                for t in range(NT):
                    keys = [meta_k] + [(128 * k, 128, k, (k - 4 * t) if k >= 4 * t else None) for k in range(4 * t + 4)]
                    attend(hh, d, 512 * t, 512, keys, t)
            while pend:
                pend.pop()()

        def o_pass(wts, src, dst, key_src, key_dst):
            w_out_t = wts[3]
            AR.reset()
            xt = Ring([AR.alloc([1024], F32) for _ in range(4)])
            ot = Ring([[AR.alloc([512], BF16) for _ in range(8)] for _ in range(2)])
            oi = 0
            for ti, (r0, n) in enumerate(tiles):
                o = ot.next(); oi += 1
                for c in range(8):
                    P.dma(lambda e, c=c, o=o, r0=r0, n=n: e.dma_start(out=o[c][:, 0:n], in_=OTD[c, :, r0:r0 + n]), [("OTD", ti)], [o[c]], ("oO", oi % 2, c % 4))
                for s, r in subs(n):
                    y = xt.next()
                    for (p0, cnt, sap) in xrows(src[1], src[0], r0 + 128 * s, r):
                        P.dma(lambda e, y=y, p0=p0, cnt=cnt, sap=sap: e.dma_start(out=y[p0:p0 + cnt, :], in_=sap),
                              [(key_src, ti)], [y], ("xt", y.ri))
                    for dh in range(2):
                        pd = psr.next()
                        for c in range(8):
                            P.pe(lambda e, c=c, pd=pd, s=s, r=r, dh=dh, o=o: e.matmul(pd[0:r, :], o[c][:, s * 128:s * 128 + r], w_out_t[:, c, dh * 512:(dh + 1) * 512], start=(c == 0), stop=(c == 7)), [w_out_t, o[c]], [pd])
                        P.dve(lambda e, pd=pd, y=y, r=r, dh=dh: e.tensor_tensor(out=y[0:r, dh * 512:(dh + 1) * 512], in0=pd[0:r, :], in1=y[0:r, dh * 512:(dh + 1) * 512], op=ALU.add), [pd, y], [y])
                    for (p0, cnt, dap) in xrows(dst[1], dst[0], r0 + 128 * s, r):
                        P.dma(lambda e, y=y, p0=p0, cnt=cnt, dap=dap: e.dma_start(out=dap, in_=y[p0:p0 + cnt, :]),
                              [y], [(key_dst, ti)], ("xo", y.ri))

        if sched == "ffn":
            W0 = load_ffn(0, 1, 0, 0)
            W1 = load_ffn(1, 1, 0, 1)
            ffn_pass(W0, 0, f1n[0], ("in", None), ("x", XA), "IN", "XA")
            ffn_pass(W1, 1, None, ("x", XA), ("out", None), "XA", "OUT")
        elif sched == "mix0":
            Wm = load_mix0(0)
            p1_pass(Wm, ("in", None), "IN")
            p2_pass(Wm)
            g_pass()
            a_pass()
            o_pass(Wm, ("in", None), ("out", None), "IN", "OUT")
        elif sched == "full":
            Wa = load_ffn(0, 1, 0, 0)
            Wb = load_ffn(1, 1, 0, 1)
            ffn_pass(Wa, 0, f1n[0], ("in", None), ("x", XA), "IN", "XA")
            Wm = load_mix0(0)
            ffn_pass(Wb, 1, None, ("x", XA), ("x", XB), "XA", "XB")
            Wa = load_ffn(1, 2, 0, 0)
            p1_pass(Wm, ("x", XB), "XB")
            p2_pass(Wm)
            g_pass()
            a_pass()
            o_pass(Wm, ("x", XB), ("x", XA), "XB", "XA")
            Wb = load_ffn(0, 2, 0, 1)
            ffn_pass(Wa, 0, f2n[0], ("x", XA), ("x", XB), "XA", "XB")
            Wa = load_ffn(1, 1, 1, 0)
            ffn_pass(Wb, 1, None, ("x", XB), ("x", XA), "XB", "XA")
            Wb = load_ffn(0, 1, 1, 1)
            ffn_pass(Wa, 0, f1n[1], ("x", XA), ("x", XB), "XA", "XB")
            Ws = load_sc(1)
            ffn_pass(Wb, 1, None, ("x", XB), ("x", XA), "XB", "XA")
            Wa = load_ffn(0, 2, 1, 0)
            sconv_pass(Ws, ("x", XA), ("x", XB), "XA", "XB")
            Wb = load_ffn(1, 2, 1, 1)
            ffn_pass(Wa, 0, f2n[1], ("x", XB), ("x", XA), "XB", "XA")
            ffn_pass(Wb, 1, None, ("x", XA), ("out", None), "XA", "OUT")
        elif sched == "p1g":
            Wm = load_mix0(0)
            p1_pass(Wm, ("in", None), "IN")
            g_pass()
        elif sched in ("p12", "p1", "p2"):
            Wm = load_mix0(0)
            if sched != "p2":
                p1_pass(Wm, ("in", None), "IN")
            if sched != "p1":
                p2_pass(Wm)
        elif sched == "sconv":
            W0 = load_sc(0)
            sconv_pass(W0, ("in", None), ("out", None), "IN", "OUT")
        P.finish(st)
        LAST['P'] = P
    return nc


def _consts(NT):
    NF = NT * 512
    NTOK = NF + 80
    c = {}
    c["c_ident"] = np.eye(128, dtype=np.float32)
    j = np.arange(128)
    c["c_tri"] = (j[:, None] <= j[None, :]).astype(np.float32)
    c["c_mgt"] = (j[None, :] > j[:, None]).astype(np.float32)
    c["c_mlt"] = (j[None, :] < j[:, None]).astype(np.float32)
    am = np.zeros((128, 4, 512), np.float32)
    p = np.arange(128)[:, None]; f = np.arange(512)[None, :]
    for d in range(4):
        am[:, d, :] = ((2 * d + p // 64) <= (f // 64)).astype(np.float32)
    c["c_amask"] = am
    b = np.zeros((96, 96), np.float32); b[:64, :64] = 1.0 / 64; b[64:, 64:] = 1.0 / 32
    c["c_b96"] = b
    Pm = np.zeros((96, 96), np.float32)
    for a in range(16):
        Pm[64 + a, 64 + 16 + a] = -1.0
        Pm[64 + 16 + a, 64 + a] = 1.0
    c["c_pT"] = np.ascontiguousarray(Pm.T)
    pos = np.concatenate([16 + np.arange(NF), np.arange(16), 1024 + np.arange(64)]).astype(np.float32)
    inv = (np.float32(10000.0) ** (-np.arange(16, dtype=np.float32) / np.float32(16))).astype(np.float32)
    ang = (pos[:, None] * inv[None, :]).astype(np.float32)
    cs = np.cos(ang.astype(np.float64)).astype(np.float32); sn = np.sin(ang.astype(np.float64)).astype(np.float32)
    c["c_cs"] = np.ascontiguousarray(np.concatenate([cs, sn], 1))
    C96 = np.ones((96, NTOK), np.float32); S96 = np.zeros((96, NTOK), np.float32)
    C96[64:80] = cs.T; C96[80:96] = cs.T; S96[64:80] = sn.T; S96[80:96] = sn.T
    c["c_C96"] = C96; c["c_S96"] = S96
    return c


_CACHE = {}
P2PARTS = 'abqc'
A_LVL = 9
DEBUG = False
LAST = {}
SCHED = 'full'


def kernel(**inp):
    f = lambda a: np.ascontiguousarray(np.asarray(a, dtype=np.float32))
    NT = inp["x_prompt"].shape[1] // 512
    NF = NT * 512
    if NT not in _CACHE:
        _CACHE[NT] = build(NT, SCHED)
    nc = _CACHE[NT]
    cs = _consts(NT)
    shared = dict(cs)
    shared.update(meta=f(inp["meta_tokens"]), f1n=f(inp["ffn1_norm"]), f2n=f(inp["ffn2_norm"]), mixn=f(inp["mix_norm"]),
                  f1g=f(inp["ffn1_w_gate"]), f1u=f(inp["ffn1_w_up"]), f1d=f(inp["ffn1_w_down"]),
                  f2g=f(inp["ffn2_w_gate"]), f2u=f(inp["ffn2_w_up"]), f2d=f(inp["ffn2_w_down"]),
                  w_in=f(inp["ab_w_in"][0]), w_out=f(inp["ab_w_out"][0]), convw=f(inp["gdn_conv_w"][0]),
                  alog=f(inp["gdn_A_log"]), dtb=f(inp["gdn_dt_bias"]), onorm=f(inp["gdn_o_norm"]),
                  qnorm=f(inp["mla_q_norm"]), wuq=f(inp["mla_w_uq"][0]), kvnorm=f(inp["mla_kv_norm"]),
                  wukv=f(inp["mla_w_ukv"][0]), qnn=f(inp["mla_qn_norm"]), qrn=f(inp["mla_qr_norm"]),
                  knn=f(inp["mla_kn_norm"]), krn=f(inp["mla_kr_norm"]),
                  scin=f(inp["sc_w_in"][0]), sccw=f(inp["sc_conv_w"][0]), scout=f(inp["sc_w_out"][0]))
    in_maps = []
    for b in range(8):
        m = dict(shared)
        m.update(xp=f(inp["x_prompt"][b]), xs=f(inp["x_sample"][b]), cckv=f(inp["cache_mla_ckv"][0, b]),
                 ckr=f(inp["cache_mla_krope"][0, b]), gS=f(inp["state_gdn_S"][0, b]),
                 gconv=f(inp["state_gdn_conv"][0, b]), sconv=f(inp["state_sconv"][0, b]))
        in_maps.append(m)
    res = run_bass_kernel_spmd(nc, in_maps, core_ids=list(range(8))).results
    LAST["res"] = res
    g = lambda k: np.stack([np.asarray(res[b][k], dtype=np.float32) for b in range(8)])
    pck = g("o_pckv"); pkr = g("o_pkr")
    pck = np.concatenate([pck[:, NF:], pck[:, :NF]], 1); pkr = np.concatenate([pkr[:, NF:], pkr[:, :NF]], 1)
    return (g("y_p"), g("y_s"), pck[None], pkr[None], g("o_pS")[None], g("o_pconv")[None], g("o_psc")[None],
            g("o_sckv")[None], g("o_skr")[None], g("o_sS")[None], g("o_sconv")[None], g("o_ssc")[None])
```

```python
import numpy as np
from contextlib import ExitStack
import concourse.bass as bass
import concourse.mybir as mybir
from concourse.bass_utils import run_bass_kernel_spmd

F32 = mybir.dt.float32
BF16 = mybir.dt.bfloat16
U8 = mybir.dt.uint8
AF = mybir.ActivationFunctionType
ALU = mybir.AluOpType
EPS = 1e-6
DEBUG = False
DFF = 2816
HFF = 1408
NCH = 11
SAME_ENG_SYNC = True


class Tile:
    def __init__(self, ap, res):
        self.ap = ap
        self.res = tuple(res)

    def __getitem__(self, k):
        return self.ap[k]


def _flat(xs):
    out = []
    for x in xs:
        if isinstance(x, Tile):
            out.extend(x.res)
        elif isinstance(x, (list, tuple)) and x and isinstance(x[0], (Tile, list, tuple)):
            out.extend(_flat(x))
        else:
            out.append(x)
    return out


class Op:
    __slots__ = ("eng", "fn", "R", "W", "dma", "deps", "sig", "val", "sem", "tag")


class Prog:
    ENG = ("pe", "act", "dve", "pool", "sp")

    def __init__(self, nc):
        self.nc = nc
        self.ops = []
        self.dcount = {}
        self._tag = None
        self._g0 = None

    def add(self, eng, fn, R=(), W=(), dma=None):
        o = Op()
        o.eng, o.fn, o.R, o.W, o.dma = eng, fn, _flat(R), _flat(W), dma
        for r in o.R:
            if isinstance(r, tuple) and r and r[0] == "PS" and r not in o.W:
                o.W.append(r)
        o.deps = ()
        o.sig = dma is not None
        o.tag = self._tag
        self.ops.append(o)
        return o

    def begin_group(self):
        self._g0 = len(self.ops)

    def stage(self, item, k):
        self._tag = (item, k)

    def end_group(self):
        seg = self.ops[self._g0:]
        assert all(o.tag is not None for o in seg)
        order = sorted(range(len(seg)), key=lambda i: (seg[i].tag[0] + seg[i].tag[1], -seg[i].tag[1], i))
        self.ops[self._g0:] = [seg[i] for i in order]
        self._tag = None
        self._g0 = None

    def pe(self, fn, R=(), W=()):
        return self.add("pe", fn, R, W)

    def act(self, fn, R=(), W=()):
        return self.add("act", fn, R, W)

    def dve(self, fn, R=(), W=()):
        return self.add("dve", fn, R, W)

    DPOOL = {"sp": 32, "pool": 12}

    def dma(self, fn, R, W, sem, q="sp"):
        c = self.dcount.get(q, 0)
        self.dcount[q] = c + 1
        return self.add(q, fn, R, W, dma=(q, c % self.DPOOL[q]))

    def finish(self, stack):
        nc = self.nc
        ops = self.ops
        lastw = {}
        readers = {}
        lastdma = {}
        for i, o in enumerate(ops):
            d = set()
            if o.dma is not None:
                if o.dma in lastdma:
                    d.add(lastdma[o.dma])
                lastdma[o.dma] = i
            for r in o.R:
                if r in lastw:
                    d.add(lastw[r])
            for r in o.W:
                if r in lastw:
                    d.add(lastw[r])
                rd = readers.get(r)
                if rd:
                    d.update(rd.values())
            d.discard(i)
            o.deps = d
            for r in o.R:
                readers.setdefault(r, {})[o.eng if o.dma is None else ("dma", i)] = i
            for r in o.W:
                lastw[r] = i
                readers[r] = {}
        fin = Op()
        fin.eng, fin.fn, fin.R, fin.W, fin.dma, fin.sig = "sp", None, [], [], None, False
        fin.deps = set(i for i, o in enumerate(ops) if o.dma is not None)
        ops.append(fin)
        for o in ops:
            for j in o.deps:
                y = ops[j]
                if y.dma is None and not (y.eng == o.eng and o.dma is None and (o.eng == "pe" or not SAME_ENG_SYNC)):
                    y.sig = True
        esem = {e: stack.enter_context(nc.semaphore("s_" + e)) for e in self.ENG}
        dsem = {}
        ecnt = {e: 0 for e in self.ENG}
        dcnt = {}
        for o in ops:
            if o.dma is not None:
                if o.dma not in dsem:
                    dsem[o.dma] = stack.enter_context(nc.semaphore("d_%d" % len(dsem)))
                    dcnt[o.dma] = 0
                dcnt[o.dma] += 16
                o.sem, o.val = dsem[o.dma], dcnt[o.dma]
            elif o.sig:
                ecnt[o.eng] += 1
                o.sem, o.val = esem[o.eng], ecnt[o.eng]
        per = {e: [] for e in self.ENG}
        for o in ops:
            per[o.eng].append(o)

        def run(eng_name, e):
            waited = {}
            for o in per[eng_name]:
                need = {}
                for j in o.deps:
                    y = ops[j]
                    if y.dma is None and y.eng == eng_name and o.dma is None and (eng_name == "pe" or not SAME_ENG_SYNC):
                        continue
                    k = id(y.sem)
                    if waited.get(k, 0) >= y.val:
                        continue
                    if k not in need or need[k][1] < y.val:
                        need[k] = (y.sem, y.val)
                for k, (s, v) in need.items():
                    e.wait_ge(s, v)
                    waited[k] = v
                if o.fn is None:
                    continue
                ins = o.fn(e)
                if o.dma is not None:
                    ins.then_inc(o.sem, 16)
                elif o.sig:
                    ins.then_inc(o.sem, 1)

        with nc.Block() as block:
            @block.sync
            def _(e):
                run("sp", e)

            @block.tensor
            def _(e):
                run("pe", e)

            @block.scalar
            def _(e):
                run("act", e)

            @block.vector
            def _(e):
                run("dve", e)

            @block.gpsimd
            def _(e):
                run("pool", e)


class Arena:
    GRAN = 256

    def __init__(self, t):
        self.t = t
        self.off = 0

    def reset(self):
        self.off = 0

    def alloc(self, shape, dtype):
        esz = 4 if dtype == F32 else 2
        n = int(np.prod(shape)) * esz
        off = (self.off + self.GRAN - 1) // self.GRAN * self.GRAN
        self.off = off + n
        assert self.off <= self.t.shape[1], ("arena overflow", self.off)
        ap = self.t[:, off:off + n].bitcast(dtype)
        if len(shape) == 2:
            ap = ap.rearrange("p (a b) -> p a b", b=shape[1])
        elif len(shape) == 3:
            ap = ap.rearrange("p (a b c) -> p a b c", b=shape[1], c=shape[2])
        return Tile(ap, [("AR", g) for g in range(off // self.GRAN, (off + n - 1) // self.GRAN + 1)])


class Ring:
    def __init__(self, tiles):
        self.tiles = tiles
        self.i = 0
        for j, t in enumerate(tiles):
            if isinstance(t, Tile):
                t.ri = j

    def next(self):
        t = self.tiles[self.i % len(self.tiles)]
        self.i += 1
        return t


def build(NT, sched='full'):
    NF = NT * 512
    NTOK = NF + 80
    NK = NTOK + 1024
    nc = bass.Bass("TRN2", target_bir_lowering=False)
    P = Prog(nc)
    D = {}

    def din(name, shape, dt=F32):
        D[name] = nc.dram_tensor(name, list(shape), dt, kind="ExternalInput").ap()
        return D[name]

    def dout(name, shape):
        D[name] = nc.dram_tensor(name, list(shape), F32, kind="ExternalOutput").ap()
        return D[name]

    def dint(name, shape, dt):
        D[name] = nc.dram_tensor(name, list(shape), dt, kind=("ExternalOutput" if DEBUG else "Internal")).ap()
        return D[name]

    xp = din("xp", [NF, 1024]); xs = din("xs", [64, 1024]); meta = din("meta", [16, 1024])
    cckv = din("cckv", [1024, 256]); ckr = din("ckr", [1024, 32]); gS = din("gS", [4, 128, 128])
    gconv = din("gconv", [3, 1536]); sconv = din("sconv", [2, 1024])
    f1n = din("f1n", [2, 1024]); f2n = din("f2n", [2, 1024]); mixn = din("mixn", [2, 1024])
    fw = {}
    for k in ("f1g", "f1u", "f2g", "f2u"):
        fw[k] = din(k, [2, 1024, DFF])
    for k in ("f1d", "f2d"):
        fw[k] = din(k, [2, DFF, 1024])
    w_in = din("w_in", [1024, 2728]); w_out = din("w_out", [1024, 1024]); convw = din("convw", [4, 1536])
    alog = din("alog", [1, 4]); dtb = din("dtb", [1, 4]); onorm = din("onorm", [1, 128])
    qnorm = din("qnorm", [1, 384]); wuq = din("wuq", [384, 768]); kvnorm = din("kvnorm", [1, 256])
    wukv = din("wukv", [256, 1024]); qnn = din("qnn", [1, 64]); qrn = din("qrn", [1, 32])
    knn = din("knn", [1, 64]); krn = din("krn", [1, 32])
    scin = din("scin", [1024, 3072]); sccw = din("sccw", [3, 1024]); scout = din("scout", [1024, 1024])
    c_ident = din("c_ident", [128, 128]); c_tri = din("c_tri", [128, 128]); c_mgt = din("c_mgt", [128, 128])
    c_mlt = din("c_mlt", [128, 128]); c_amask = din("c_amask", [128, 4, 512]); c_b96 = din("c_b96", [96, 96])
    c_pT = din("c_pT", [96, 96]); c_cs = din("c_cs", [NTOK, 32]); c_C96 = din("c_C96", [96, NTOK])
    c_S96 = din("c_S96", [96, NTOK])
    y_p = dout("y_p", [NF, 1024]); y_s = dout("y_s", [64, 1024])
    o_pckv = dout("o_pckv", [NF + 16, 256]); o_pkr = dout("o_pkr", [NF + 16, 32])
    o_pS = dout("o_pS", [4, 128, 128]); o_pconv = dout("o_pconv", [3, 1536]); o_psc = dout("o_psc", [2, 1024])
    o_sckv = dout("o_sckv", [64, 256]); o_skr = dout("o_skr", [64, 32]); o_sS = dout("o_sS", [4, 128, 128])
    o_sconv = dout("o_sconv", [3, 1536]); o_ssc = dout("o_ssc", [2, 1024])
    XA = dint("XA", [NTOK, 1024], F32); XB = dint("XB", [NTOK, 1024], F32)
    HTD = dint("HTD", [8, 128, NTOK], BF16)
    QKVD = dint("QKVD", [12, 128, NTOK], F32); ZTD = dint("ZTD", [4, 128, NTOK], BF16)
    GBD = dint("GBD", [NTOK, 8], F32)
    KTD = dint("KTD", [8, 96, NK], BF16); QTD = dint("QTD", [8, 96, NTOK], BF16)
    VSD = dint("VSD", [NK, 8, 65], BF16); OTD = dint("OTD", [8, 128, NTOK], BF16)

    tiles = [(512 * t, 512) for t in range(NT)] + [(NF, 80)]
    NTL = len(tiles)

    def subs(n):
        return [(s, min(128, n - 128 * s)) for s in range((n + 127) // 128)]

    with ExitStack() as st:
        SLOT_EL = 34368
        slots = [st.enter_context(nc.sbuf_tensor("wslot%d" % i, [128, SLOT_EL], BF16)) for i in range(2)]
        ARB = 69 * 1024
        art = st.enter_context(nc.sbuf_tensor("arena", [128, ARB], U8))
        AR = Arena(art)
        cst = st.enter_context(nc.sbuf_tensor("consts", [128, 6 * 128 + 96 * 2], F32))
        cstb = st.enter_context(nc.sbuf_tensor("constsb", [128, 128 + 96 * 2], BF16))
        psb = [st.enter_context(nc.psum_tensor("psb%d" % i, [128, 512], F32)) for i in range(8)]
        PS = [Tile(psb[i][:], [("PS", i)]) for i in range(8)]
        psr = Ring(PS)
        identf = Tile(cst[:, 0:128], ["c_identf"])
        onesf = Tile(cst[:, 128:256], ["c_onesf"])
        tri = Tile(cst[:, 256:384], ["c_tri"])
        mgt = Tile(cst[:, 384:512], ["c_mgt"])
        mlt = Tile(cst[:, 512:640], ["c_mlt"])
        identb = Tile(cstb[:, 0:128], ["c_identb"])
        b96 = Tile(cstb[:, 128:224], ["c_b96"])
        pT96 = Tile(cstb[:, 224:320], ["c_pT"])
        onesb = Tile(cst[:, 640:768].bitcast(BF16)[:, 0:128], ["c_onesb"])

        P.dma(lambda e: e.dma_start(out=identf[:], in_=c_ident[:, :]), [], [identf], "c0")
        P.dma(lambda e: e.dma_start(out=tri[:], in_=c_tri[:, :]), [], [tri], "c0")
        P.dma(lambda e: e.dma_start(out=mgt[:], in_=c_mgt[:, :]), [], [mgt], "c0")
        P.dma(lambda e: e.dma_start(out=mlt[:], in_=c_mlt[:, :]), [], [mlt], "c0")
        P.dma(lambda e: e.dma_start(out=identb[:], in_=c_ident[:, :]), [], [identb], "c1", q="pool")
        P.dma(lambda e: e.dma_start(out=b96[0:96, :], in_=c_b96[:, :]), [], [b96], "c1", q="pool")
        P.dma(lambda e: e.dma_start(out=pT96[0:96, :], in_=c_pT[:, :]), [], [pT96], "c1", q="pool")
        P.dve(lambda e: e.memset(onesf[:], 1.0), [], [onesf])
        P.dve(lambda e: e.memset(onesb[:], 1.0), [], [onesb])

        def wview(slot, off, kc, ncol):
            return slots[slot][:, off:off + kc * ncol].rearrange("p (k n) -> p k n", n=ncol)

        def wload(slot, part, off, src, kc, ncol):
            dst = wview(slot, off, kc, ncol)
            res = ("WS", slot)
            for k in range(kc):
                P.dma(lambda e, k=k: e.dma_start(out=dst[:, k, :], in_=src[k * 128:(k + 1) * 128, :]),
                      [], [res], ("w", slot, part, k % 3), q="pool")
            return Tile(dst, [res])

        def load_ffn(slot, which, l, half):
            g = fw["f%dg" % which][l]; u = fw["f%du" % which][l]; d = fw["f%dd" % which][l]
            c0 = half * HFF
            wg = wload(slot, 0, 0, g[:, c0:c0 + HFF], 8, HFF)
            wu = wload(slot, 1, 8 * HFF, u[:, c0:c0 + HFF], 8, HFF)
            wd = wload(slot, 2, 16 * HFF, d[c0:c0 + HFF, :], NCH, 1024)
            return wg, wu, wd

        def load_mix0(slot):
            a = wload(slot, 0, 0, w_in, 8, 2728)
            b = wload(slot, 1, 21824, wuq, 3, 768)
            c = wload(slot, 2, 24128, wukv, 2, 1024)
            d = wload(slot, 3, 26176, w_out, 8, 1024)
            return a, b, c, d

        def load_sc(slot):
            a = wload(slot, 0, 0, scin, 8, 3072)
            b = wload(slot, 1, 24576, scout, 8, 1024)
            return a, b

        def xrows(X, kind, r0, r):
            if kind == "in":
                if r0 < NF:
                    return [(0, r, xp[r0:r0 + r, :])]
                return [(0, 16, meta[:, :]), (16, 64, xs[:, :])]
            if kind == "out":
                if r0 < NF:
                    return [(0, r, y_p[r0:r0 + r, :])]
                return [(16, 64, y_s[:, :])]
            return [(0, r, X[r0:r0 + r, :])]

        def norm_T(h, gt, xt, hn, stat, src, key_src, ti, r0, n):
            for s, r in subs(n):
                x = xt.next()
                for (p0, cnt, sap) in xrows(src[1], src[0], r0 + 128 * s, r):
                    P.dma(lambda e, x=x, p0=p0, cnt=cnt, sap=sap: e.dma_start(out=x[p0:p0 + cnt, :], in_=sap),
                          [(key_src, ti)], [x], ("xt", x.ri))
                hb = hn.next(); s1 = stat.next(); s2 = stat.next()
                P.act(lambda e, x=x, hb=hb, s1=s1, r=r: e.activation(out=hb[0:r, :], in_=x[0:r, :], func=AF.Square, accum_out=s1[0:r, 0:1]), [x], [hb, s1])
                P.act(lambda e, s1=s1, s2=s2, r=r: e.activation(out=s2[0:r, 0:1], in_=s1[0:r, 0:1], func=AF.Ln, bias=EPS, scale=1.0 / 1024), [s1], [s2])
                P.act(lambda e, s2=s2, r=r: e.activation(out=s2[0:r, 1:2], in_=s2[0:r, 0:1], func=AF.Exp, scale=-0.5), [s2], [s2])
                P.dve(lambda e, x=x, hb=hb, s2=s2, r=r: e.scalar_tensor_tensor(out=hb[0:r, :], in0=x[0:r, :], scalar=s2[0:r, 1:2], in1=gt[0:r, :], op0=ALU.mult, op1=ALU.mult), [x, s2, gt], [hb])
                pt = psr.next()
                ptb = pt.ap.bitcast(BF16).rearrange("p (k t) -> p k t", t=128)
                for k in range(8):
                    P.pe(lambda e, k=k, hb=hb, ptb=ptb, r=r: e.transpose(out=ptb[:, k, 0:r], in_=hb[0:r, k * 128:(k + 1) * 128], identity=identb[0:r, 0:r]), [hb, identb], [pt])
                if s % 2 == 0:
                    P.act(lambda e, h=h, ptb=ptb, s=s, r=r: e.copy(out=h[:, :, s * 128:s * 128 + r], in_=ptb[:, :, 0:r]), [pt], [h])
                else:
                    P.dve(lambda e, h=h, ptb=ptb, s=s, r=r: e.tensor_copy(out=h[:, :, s * 128:s * 128 + r], in_=ptb[:, :, 0:r]), [pt], [h])

        def load_gt(gt, norm_ap):
            P.dma(lambda e: e.dma_start(out=gt[:], in_=norm_ap.partition_broadcast(128)), [], [gt], "gt")

        def ffn_pass(wts, half, norm_ap, src, dst, ti_key_src, ti_key_dst):
            wg, wu, wd = wts
            AR.reset()
            xt = Ring([AR.alloc([1024], F32) for _ in range(6)])
            hT = Ring([AR.alloc([8, 512], BF16) for _ in range(2)])
            actb = [AR.alloc([512], BF16) for _ in range(NCH)]
            hn = Ring([AR.alloc([1024], BF16) for _ in range(2)])
            sg = Ring([AR.alloc([512], BF16) for _ in range(2)])
            gt = AR.alloc([1024], F32)
            stat = Ring([AR.alloc([4], F32) for _ in range(4)])
            if half == 0:
                load_gt(gt, norm_ap)
            def hload(ti):
                r0, n = tiles[ti]
                h = hT.next()
                P.dma(lambda e, h=h, r0=r0, n=n: e.dma_start(out=h[:, :, 0:n], in_=HTD[:, :, r0:r0 + n].rearrange("k p t -> p k t")),
                      [("HTD", ti)], [h], ("hld", h.ri))
                return h

            hnext = hload(0) if half == 1 else None
            for ti, (r0, n) in enumerate(tiles):
                SB = subs(n)
                if half == 0:
                    h = hT.next()
                    norm_T(h, gt, xt, hn, stat, src, ti_key_src, ti, r0, n)
                    P.dma(lambda e, h=h, r0=r0, n=n: e.dma_start(out=HTD[:, :, r0:r0 + n].rearrange("k p t -> p k t"), in_=h[:, :, 0:n]),
                          [h], [("HTD", ti)], ("hst", h.ri))
                else:
                    h = hnext
                ys = []
                for s, r in SB:
                    y = xt.next()
                    ys.append(y)
                    for (p0, cnt, sap) in xrows(src[1], src[0], r0 + 128 * s, r):
                        P.dma(lambda e, y=y, p0=p0, cnt=cnt, sap=sap: e.dma_start(out=y[p0:p0 + cnt, :], in_=sap),
                              [(ti_key_src, ti)], [y], ("xt", y.ri))
                for c in range(NCH):
                    pg = psr.next(); pu = psr.next()
                    for k in range(8):
                        P.pe(lambda e, c=c, k=k, pg=pg, h=h, n=n: e.matmul(pg[:, 0:n], wg[:, k, c * 128:(c + 1) * 128], h[:, k, 0:n], start=(k == 0), stop=(k == 7)), [wg, h], [pg])
                    for k in range(8):
                        P.pe(lambda e, c=c, k=k, pu=pu, h=h, n=n: e.matmul(pu[:, 0:n], wu[:, k, c * 128:(c + 1) * 128], h[:, k, 0:n], start=(k == 0), stop=(k == 7)), [wu, h], [pu])
                    sgt = sg.next()
                    P.act(lambda e, sgt=sgt, pg=pg, n=n: e.activation(out=sgt[:, 0:n], in_=pg[:, 0:n], func=AF.Silu), [pg], [sgt])
                    P.dve(lambda e, c=c, sgt=sgt, pu=pu, n=n: e.tensor_tensor(out=actb[c][:, 0:n], in0=sgt[:, 0:n], in1=pu[:, 0:n], op=ALU.mult), [sgt, pu], [actb[c]])
                if half == 1 and ti + 1 < NTL:
                    hnext = hload(ti + 1)
                for s, r in SB:
                    y = ys[s]
                    for dh in range(2):
                        pd = psr.next()
                        for c in range(NCH):
                            P.pe(lambda e, c=c, pd=pd, s=s, r=r, dh=dh: e.matmul(pd[0:r, :], actb[c][:, s * 128:s * 128 + r], wd[:, c, dh * 512:(dh + 1) * 512], start=(c == 0), stop=(c == NCH - 1)), [wd, actb[c]], [pd])
                        P.dve(lambda e, pd=pd, y=y, r=r, dh=dh: e.scalar_tensor_tensor(out=y[0:r, dh * 512:(dh + 1) * 512], in0=pd[0:r, :], scalar=0.5, in1=y[0:r, dh * 512:(dh + 1) * 512], op0=ALU.mult, op1=ALU.add), [pd, y], [y])
                    for (p0, cnt, dap) in xrows(dst[1], dst[0], r0 + 128 * s, r):
                        P.dma(lambda e, y=y, p0=p0, cnt=cnt, dap=dap: e.dma_start(out=dap, in_=y[p0:p0 + cnt, :]),
                              [y], [(ti_key_dst, ti)], ("xo", y.ri))

        def sconv_pass(wts, src, dst, key_src, key_dst):
            wsi, wso = wts
            AR.reset()
            xt = Ring([AR.alloc([1024], F32) for _ in range(3)])
            hT = Ring([AR.alloc([8, 512], BF16) for _ in range(2)])
            hn = Ring([AR.alloc([1024], BF16) for _ in range(2)])
            gt = AR.alloc([1024], F32)
            stat = Ring([AR.alloc([4], F32) for _ in range(4)])
            gT = Ring([[AR.alloc([512], BF16) for _ in range(8)] for _ in range(2)])
            pcr = Ring([AR.alloc([516], F32) for _ in range(2)])
            tmp = Ring([AR.alloc([512], F32) for _ in range(2)])
            yb = Ring([AR.alloc([512], F32) for _ in range(2)])
            hist = {"p": AR.alloc([8, 2], F32), "s": AR.alloc([8, 2], F32)}
            cw = AR.alloc([3, 8], F32)
            load_gt(gt, mixn[1])
            for i in range(3):
                P.dma(lambda e, i=i: e.dma_start(out=cw[:, i, :], in_=sccw[i].rearrange("(c p) -> p c", p=128), allow_slow_non_contiguous=True), [], [cw], "cw")
            for t_ in range(2):
                P.dma(lambda e, t_=t_: e.dma_start(out=hist["s"][:, :, t_], in_=sconv[t_].rearrange("(c p) -> p c", p=128), allow_slow_non_contiguous=True), [], [hist["s"]], "hs")
            P.dve(lambda e: e.memset(hist["p"][:], 0.0), [], [hist["p"]])
            order = [NTL - 1] + list(range(NTL - 1))
            for ti in order:
                r0, n = tiles[ti]
                segs = [(0, 16, "p"), (16, 64, "s")] if ti == NTL - 1 else [(0, n, "p")]
                h = hT.next()
                norm_T(h, gt, xt, hn, stat, src, key_src, ti, r0, n)
                g = gT.next()
                for c in range(8):
                    pcg = psr.next(); pxi = psr.next(); pbg = psr.next()
                    for (pp, base) in ((pcg, 1024), (pxi, 2048), (pbg, 0)):
                        for k in range(8):
                            P.pe(lambda e, pp=pp, base=base, c=c, k=k, h=h, n=n: e.matmul(pp[:, 0:n], wsi[:, k, base + c * 128:base + (c + 1) * 128], h[:, k, 0:n], start=(k == 0), stop=(k == 7)), [wsi, h], [pp])
                    tm = tmp.next()
                    P.act(lambda e, tm=tm, pcg=pcg, n=n: e.copy(out=tm[:, 0:n], in_=pcg[:, 0:n]), [pcg], [tm])
                    for (c0, ns, sq) in segs:
                        pc = pcr.next(); y2 = yb.next(); hs = hist[sq]
                        P.act(lambda e, pc=pc, hs=hs, c=c: e.copy(out=pc[:, 0:2], in_=hs[:, c, :]), [hs], [pc])
                        P.dve(lambda e, pc=pc, tm=tm, pxi=pxi, c0=c0, ns=ns: e.tensor_tensor(out=pc[:, 2:2 + ns], in0=tm[:, c0:c0 + ns], in1=pxi[:, c0:c0 + ns], op=ALU.mult), [tm, pxi], [pc])
                        P.act(lambda e, pc=pc, hs=hs, c=c, ns=ns: e.copy(out=hs[:, c, :], in_=pc[:, ns:ns + 2]), [pc], [hs])
                        P.dve(lambda e, pc=pc, y2=y2, c=c, ns=ns: e.tensor_scalar(out=y2[:, 0:ns], in0=pc[:, 0:ns], scalar1=cw[:, 0, c:c + 1], scalar2=None, op0=ALU.mult), [pc, cw], [y2])
                        for i in (1, 2):
                            P.dve(lambda e, pc=pc, y2=y2, c=c, ns=ns, i=i: e.scalar_tensor_tensor(out=y2[:, 0:ns], in0=pc[:, i:i + ns], scalar=cw[:, i, c:c + 1], in1=y2[:, 0:ns], op0=ALU.mult, op1=ALU.add), [pc, cw, y2], [y2])
                        P.dve(lambda e, g=g, c=c, pbg=pbg, y2=y2, c0=c0, ns=ns: e.tensor_tensor(out=g[c][:, c0:c0 + ns], in0=pbg[:, c0:c0 + ns], in1=y2[:, 0:ns], op=ALU.mult), [pbg, y2], [g[c]])
                for s, r in subs(n):
                    y = xt.next()
                    for (p0, cnt, sap) in xrows(src[1], src[0], r0 + 128 * s, r):
                        P.dma(lambda e, y=y, p0=p0, cnt=cnt, sap=sap: e.dma_start(out=y[p0:p0 + cnt, :], in_=sap),
                              [(key_src, ti)], [y], ("xt", y.ri))
                    for dh in range(2):
                        pd = psr.next()
                        for c in range(8):
                            P.pe(lambda e, c=c, pd=pd, s=s, r=r, dh=dh, g=g: e.matmul(pd[0:r, :], g[c][:, s * 128:s * 128 + r], wso[:, c, dh * 512:(dh + 1) * 512], start=(c == 0), stop=(c == 7)), [wso, g[c]], [pd])
                        P.dve(lambda e, pd=pd, y=y, r=r, dh=dh: e.tensor_tensor(out=y[0:r, dh * 512:(dh + 1) * 512], in0=pd[0:r, :], in1=y[0:r, dh * 512:(dh + 1) * 512], op=ALU.add), [pd, y], [y])
                    for (p0, cnt, dap) in xrows(dst[1], dst[0], r0 + 128 * s, r):
                        P.dma(lambda e, y=y, p0=p0, cnt=cnt, dap=dap: e.dma_start(out=dap, in_=y[p0:p0 + cnt, :]),
                              [y], [(key_dst, ti)], ("xo", y.ri))
            for t_ in range(2):
                P.dma(lambda e, t_=t_: e.dma_start(out=o_psc[t_].rearrange("(c p) -> p c", p=128), in_=hist["p"][:, :, t_], allow_slow_non_contiguous=True), [hist["p"]], ["o_psc"], "ho")
                P.dma(lambda e, t_=t_: e.dma_start(out=o_ssc[t_].rearrange("(c p) -> p c", p=128), in_=hist["s"][:, :, t_], allow_slow_non_contiguous=True), [hist["s"]], ["o_ssc"], "ho")

        def p1_pass(wts, src, key_src):
            w_in_t = wts[0]
            AR.reset()
            xt = Ring([AR.alloc([1024], F32) for _ in range(2)])
            hT = Ring([AR.alloc([8, 512], BF16) for _ in range(2)])
            hn = Ring([AR.alloc([1024], BF16) for _ in range(2)])
            gt = AR.alloc([1024], F32)
            stat = Ring([AR.alloc([4], F32) for _ in range(4)])
            f5 = Ring([AR.alloc([512], F32) for _ in range(6)])
            b5 = Ring([AR.alloc([512], BF16) for _ in range(4)])
            xcr = Ring([AR.alloc([516], BF16) for _ in range(3)])
            dgr = Ring([[AR.alloc([128], BF16) for _ in range(4)] for _ in range(2)])
            histb = {"p": AR.alloc([12, 3], BF16), "s": AR.alloc([12, 3], BF16)}
            hc32 = {"p": AR.alloc([12, 3], F32), "s": AR.alloc([12, 3], F32)}
            cwg = AR.alloc([4, 12], F32)
            dtb_t = AR.alloc([4], F32); negA = AR.alloc([4], F32)
            sm = Ring([AR.alloc([8], F32) for _ in range(6)])
            load_gt(gt, mixn[0])
            for i in range(4):
                P.dma(lambda e, i=i: e.dma_start(out=cwg[:, i, :], in_=convw[i].rearrange("(c p) -> p c", p=128), allow_slow_non_contiguous=True), [], [cwg], "cw")
            for t_ in range(3):
                P.dma(lambda e, t_=t_: e.dma_start(out=hc32["s"][:, :, t_], in_=gconv[t_].rearrange("(c p) -> p c", p=128), allow_slow_non_contiguous=True), [], [hc32["s"]], "hs")
            P.act(lambda e: e.copy(out=histb["s"][:], in_=hc32["s"][:]), [hc32["s"]], [histb["s"]])
            P.dve(lambda e: e.memset(histb["p"][:], 0.0), [], [histb["p"]])
            P.dma(lambda e: e.dma_start(out=dtb_t[:], in_=dtb[0].partition_broadcast(128)), [], [dtb_t], "pv")
            P.dma(lambda e: e.dma_start(out=negA[:], in_=alog[0].partition_broadcast(128)), [], [negA], "pv2")
            P.act(lambda e: e.activation(out=negA[:], in_=negA[:], func=AF.Exp), [negA], [negA])
            P.dve(lambda e: e.tensor_scalar(out=negA[:], in0=negA[:], scalar1=-1.0, scalar2=None, op0=ALU.mult), [negA], [negA])
            pp_r = Ring(PS[0:3]); pc_r = Ring(PS[3:6]); pm_r = Ring(PS[6:8])
            order = [NTL - 1] + list(range(NTL - 1))
            for ti in order:
                r0, n = tiles[ti]
                segs = [(0, 16, "p"), (16, 64, "s")] if ti == NTL - 1 else [(0, n, "p")]
                h = hT.next()
                norm_T(h, gt, xt, hn, stat, src, key_src, ti, r0, n)
                P.dma(lambda e, h=h, r0=r0, n=n: e.dma_start(out=HTD[:, :, r0:r0 + n].rearrange("k p t -> p k t"), in_=h[:, :, 0:n]),
                      [h], [("HTD", ti)], ("hst", h.ri))
                SK = (n == 512)
                if SK:
                    P.begin_group()
                for c in range(12):
                    if SK:
                        P.stage(c, 0)
                    pp = pp_r.next()
                    for k in range(8):
                        P.pe(lambda e, pp=pp, c=c, k=k, h=h, n=n: e.matmul(pp[:, 0:n], w_in_t[:, k, c * 128:(c + 1) * 128], h[:, k, 0:n], start=(k == 0), stop=(k == 7)), [w_in_t, h], [pp])
                    if SK:
                        P.stage(c, 1)
                    dg = dgr.next()
                    for i in range(4):
                        P.dve(lambda e, dg=dg, i=i, c=c: e.tensor_scalar(out=dg[i][:], in0=identb[:], scalar1=cwg[:, i, c:c + 1], scalar2=None, op0=ALU.mult), [identb, cwg], [dg[i]])
                    for (c0, ns, sq_) in segs:
                        xc = xcr.next(); hb = histb[sq_]
                        P.act(lambda e, xc=xc, hb=hb, c=c: e.copy(out=xc[:, 0:3], in_=hb[:, c, :]), [hb], [xc])
                        P.act(lambda e, xc=xc, pp=pp, c0=c0, ns=ns: e.copy(out=xc[:, 3:3 + ns], in_=pp[:, c0:c0 + ns]), [pp], [xc])
                        P.dve(lambda e, xc=xc, hb=hb, c=c, ns=ns: e.tensor_copy(out=hb[:, c, :], in_=xc[:, ns:ns + 3]), [xc], [hb])
                        if sq_ == "s" or ti == NT - 1:
                            P.dve(lambda e, pp=pp, c=c, c0=c0, ns=ns, sq_=sq_: e.tensor_copy(out=hc32[sq_][:, c, :], in_=pp[:, c0 + ns - 3:c0 + ns]), [pp], [hc32[sq_]])
                        if SK:
                            P.stage(c, 2)
                        pc2 = pc_r.next()
                        for i in range(4):
                            P.pe(lambda e, pc2=pc2, dg=dg, i=i, xc=xc, ns=ns: e.matmul(pc2[:, 0:ns], dg[i][:], xc[:, i:i + ns], start=(i == 0), stop=(i == 3)), [dg[i], xc], [pc2])
                        if SK:
                            P.stage(c, 3)
                        so = f5.next()
                        P.act(lambda e, so=so, pc2=pc2, ns=ns: e.activation(out=so[:, 0:ns], in_=pc2[:, 0:ns], func=AF.Exp, scale=-1.0), [pc2], [so])
                        P.act(lambda e, so=so, ns=ns: e.activation(out=so[:, 0:ns], in_=so[:, 0:ns], func=AF.Ln, bias=1.0), [so], [so])
                        P.act(lambda e, so=so, ns=ns: e.activation(out=so[:, 0:ns], in_=so[:, 0:ns], func=AF.Exp, scale=-1.0), [so], [so])
                        P.dve(lambda e, so=so, pc2=pc2, ns=ns: e.tensor_tensor(out=so[:, 0:ns], in0=so[:, 0:ns], in1=pc2[:, 0:ns], op=ALU.mult), [so, pc2], [so])
                        if c < 8:
                            sq = b5.next()
                            P.act(lambda e, sq=sq, so=so, ns=ns: e.activation(out=sq[:, 0:ns], in_=so[:, 0:ns], func=AF.Square), [so], [sq])
                            if SK:
                                P.stage(c, 4)
                            pm = pm_r.next()
                            P.pe(lambda e, pm=pm, sq=sq, ns=ns: e.matmul(pm[:, 0:ns], onesb[:], sq[:, 0:ns], start=True, stop=True), [onesb, sq], [pm])
                            if SK:
                                P.stage(c, 5)
                            t1 = f5.next()
                            P.act(lambda e, t1=t1, pm=pm, ns=ns: e.activation(out=t1[:, 0:ns], in_=pm[:, 0:ns], func=AF.Ln, bias=EPS), [pm], [t1])
                            P.act(lambda e, t1=t1, ns=ns: e.activation(out=t1[:, 0:ns], in_=t1[:, 0:ns], func=AF.Exp, scale=-0.5), [t1], [t1])
                            oq = f5.next()
                            P.dve(lambda e, oq=oq, so=so, t1=t1, ns=ns, c=c: e.scalar_tensor_tensor(out=oq[:, 0:ns], in0=so[:, 0:ns], scalar=(128.0 ** -0.5 if c < 4 else 1.0), in1=t1[:, 0:ns], op0=ALU.mult, op1=ALU.mult), [so, t1], [oq])
                        else:
                            oq = so
                        if SK:
                            P.stage(c, 5)
                        P.dma(lambda e, oq=oq, c=c, r0=r0, c0=c0, ns=ns: e.dma_start(out=QKVD[c, :, r0 + c0:r0 + c0 + ns], in_=oq[:, 0:ns]), [oq], [("QKVD", ti)], ("f5", oq.ri))
                for c in range(4):
                    if SK:
                        P.stage(12 + c, 0)
                    pp = pp_r.next()
                    for k in range(8):
                        P.pe(lambda e, pp=pp, c=c, k=k, h=h, n=n: e.matmul(pp[:, 0:n], w_in_t[:, k, 1536 + c * 128:1536 + (c + 1) * 128], h[:, k, 0:n], start=(k == 0), stop=(k == 7)), [w_in_t, h], [pp])
                    if SK:
                        P.stage(12 + c, 1)
                    zb = b5.next(); zt = f5.next()
                    P.act(lambda e, zt=zt, pp=pp, n=n: e.activation(out=zt[:, 0:n], in_=pp[:, 0:n], func=AF.Exp, scale=-1.0), [pp], [zt])
                    P.act(lambda e, zt=zt, n=n: e.activation(out=zt[:, 0:n], in_=zt[:, 0:n], func=AF.Ln, bias=1.0), [zt], [zt])
                    P.act(lambda e, zt=zt, n=n: e.activation(out=zt[:, 0:n], in_=zt[:, 0:n], func=AF.Exp, scale=-1.0), [zt], [zt])
                    P.dve(lambda e, zb=zb, zt=zt, pp=pp, n=n: e.tensor_tensor(out=zb[:, 0:n], in0=zt[:, 0:n], in1=pp[:, 0:n], op=ALU.mult), [zt, pp], [zb])
                    P.dma(lambda e, zb=zb, c=c, r0=r0, n=n: e.dma_start(out=ZTD[c, :, r0:r0 + n], in_=zb[:, 0:n]), [zb], [("ZTD", ti)], ("b5", zb.ri))
                for s, r in subs(n):
                    if SK:
                        P.stage(16 + s, 0)
                    pp = pp_r.next()
                    for k in range(8):
                        P.pe(lambda e, pp=pp, k=k, h=h, s=s, r=r: e.matmul(pp[0:r, 0:8], h[:, k, s * 128:s * 128 + r], w_in_t[:, k, 2048:2056], start=(k == 0), stop=(k == 7)), [w_in_t, h], [pp])
                    if SK:
                        P.stage(16 + s, 1)
                    ta = sm.next(); tb = sm.next(); gb = sm.next()
                    P.dve(lambda e, ta=ta, pp=pp, r=r: e.tensor_tensor(out=ta[0:r, 0:4], in0=pp[0:r, 0:4], in1=dtb_t[0:r, :], op=ALU.add), [pp, dtb_t], [ta])
                    P.dve(lambda e, ta=ta, tb=tb, r=r: e.tensor_scalar(out=tb[0:r, 0:4], in0=ta[0:r, 0:4], scalar1=-1.0, scalar2=None, op0=ALU.mult), [ta], [tb])
                    P.dve(lambda e, ta=ta, tb=tb, r=r: e.tensor_tensor(out=tb[0:r, 0:4], in0=tb[0:r, 0:4], in1=ta[0:r, 0:4], op=ALU.min), [ta, tb], [tb])
                    P.act(lambda e, tb=tb, r=r: e.activation(out=tb[0:r, 0:4], in_=tb[0:r, 0:4], func=AF.Exp), [tb], [tb])
                    P.act(lambda e, tb=tb, r=r: e.activation(out=tb[0:r, 0:4], in_=tb[0:r, 0:4], func=AF.Ln, bias=1.0), [tb], [tb])
                    P.dve(lambda e, ta=ta, tb=tb, r=r: e.scalar_tensor_tensor(out=ta[0:r, 0:4], in0=ta[0:r, 0:4], scalar=0.0, in1=tb[0:r, 0:4], op0=ALU.max, op1=ALU.add), [ta, tb], [ta])
                    P.dve(lambda e, ta=ta, gb=gb, r=r: e.tensor_tensor(out=gb[0:r, 0:4], in0=ta[0:r, 0:4], in1=negA[0:r, :], op=ALU.mult), [ta, negA], [gb])
                    P.act(lambda e, gb=gb, pp=pp, r=r: e.activation(out=gb[0:r, 4:8], in_=pp[0:r, 4:8], func=AF.Exp, scale=-1.0), [pp], [gb])
                    P.act(lambda e, gb=gb, r=r: e.activation(out=gb[0:r, 4:8], in_=gb[0:r, 4:8], func=AF.Ln, bias=1.0), [gb], [gb])
                    P.act(lambda e, gb=gb, r=r: e.activation(out=gb[0:r, 4:8], in_=gb[0:r, 4:8], func=AF.Exp, scale=-1.0), [gb], [gb])
                    P.dma(lambda e, gb=gb, r0=r0, s=s, r=r: e.dma_start(out=GBD[r0 + s * 128:r0 + s * 128 + r, :], in_=gb[0:r, :]), [gb], [("GBD", ti)], ("sm", gb.ri))
                if SK:
                    P.end_group()
            for t_ in range(3):
                P.dma(lambda e, t_=t_: e.dma_start(out=o_pconv[t_].rearrange("(c p) -> p c", p=128), in_=hc32["p"][:, :, t_], allow_slow_non_contiguous=True), [hc32["p"]], ["o_pconv"], "ho")
                P.dma(lambda e, t_=t_: e.dma_start(out=o_sconv[t_].rearrange("(c p) -> p c", p=128), in_=hc32["s"][:, :, t_], allow_slow_non_contiguous=True), [hc32["s"]], ["o_sconv"], "ho")

        def p2_pass(wts):
            w_in_t, wuq_t, wukv_t = wts[0], wts[1], wts[2]
            wukv_v = wukv_t.ap.rearrange("p k (h t d) -> p k h t d", h=8, t=2)
            AR.reset()
            hT = Ring([AR.alloc([8, 512], BF16) for _ in range(1)])
            f5 = Ring([AR.alloc([512], F32) for _ in range(5)])
            sqr = Ring([AR.alloc([512], BF16) for _ in range(3)]); qnr = Ring([AR.alloc([512], BF16) for _ in range(4)]); o2r = Ring([AR.alloc([512], BF16) for _ in range(2)])
            pA = Ring(PS[0:4]); pB = Ring(PS[4:6]); pC = Ring(PS[6:8])
            c96 = Ring([AR.alloc([512], F32) for _ in range(1)]); s96 = Ring([AR.alloc([512], F32) for _ in range(1)])
            cst_r = Ring([AR.alloc([4, 32], F32) for _ in range(2)])
            cqs = [AR.alloc([512], F32) for _ in range(3)]
            cqn = [AR.alloc([512], BF16) for _ in range(3)]
            sqc = [AR.alloc([512], BF16) for _ in range(3)]
            ckvnT = Ring([[AR.alloc([512], BF16) for _ in range(2)] for _ in range(2)])
            krT = Ring([AR.alloc([512], BF16) for _ in range(2)])
            kvn = Ring([AR.alloc([288], F32) for _ in range(3)])
            gain_t = AR.alloc([288], F32)
            vt = Ring([AR.alloc([8, 65], BF16) for _ in range(2)])
            st = Ring([AR.alloc([4], F32) for _ in range(3)])
            r16 = Ring([AR.alloc([16], F32) for _ in range(4)])
            inv_t = AR.alloc([2], F32); gain96 = AR.alloc([1], F32); knn_t = AR.alloc([1], F32); qnorm_t = AR.alloc([3], F32)
            xck = Ring([AR.alloc([288], F32) for _ in range(4)])
            P.dma(lambda e: e.dma_start(out=gain_t[:, 0:256], in_=kvnorm[0].partition_broadcast(128)), [], [gain_t], "pv")
            P.dma(lambda e: e.dma_start(out=gain_t[:, 256:288], in_=krn[0].partition_broadcast(128)), [], [gain_t], "pv2")
            P.dma(lambda e: e.dma_start(out=gain96[0:64, :], in_=qnn[0].rearrange("(p o) -> p o", o=1)), [], [gain96], "pv3")
            P.dma(lambda e: e.dma_start(out=gain96[64:96, :], in_=qrn[0].rearrange("(p o) -> p o", o=1)), [], [gain96], "pv4")
            P.dma(lambda e: e.dma_start(out=knn_t[0:64, :], in_=knn[0].rearrange("(p o) -> p o", o=1)), [], [knn_t], "pv5")
            P.dma(lambda e: e.dma_start(out=qnorm_t[:], in_=qnorm[0].rearrange("(c p) -> p c", p=128), allow_slow_non_contiguous=True), [], [qnorm_t], "pv6")
            P.dve(lambda e: e.memset(inv_t[:, 0:1], 1.0 / 256), [], [inv_t])
            P.dve(lambda e: e.memset(inv_t[:, 1:2], 1.0 / 32), [], [inv_t])
            wv = AR.alloc([2, 512], BF16)
            for k in range(2):
                P.dve(lambda e, k=k: e.tensor_copy(out=wv[:, k, :].rearrange("p (h d) -> p h d", d=64), in_=wukv_v[:, k, :, 1, :]), [wukv_t], [wv])
            kvb = Ring([AR.alloc([384], BF16) for _ in range(2)])
            for v_ in vt.tiles:
                P.dve(lambda e, v_=v_: e.memset(v_[:], 1.0), [], [v_])
            for kv_ in kvb.tiles:
                P.dve(lambda e, kv_=kv_: e.memset(kv_[:], 0.0), [], [kv_])
            for kv_ in kvn.tiles:
                P.dve(lambda e, kv_=kv_: e.memset(kv_[:], 0.0), [], [kv_])

            def kv_expand(ck, kr_, n, kcol, vrow, key):
                P.begin_group()
                for hh in range(8):
                    P.stage(hh, 0)
                    pk = pA.next()
                    for k in range(2):
                        P.pe(lambda e, pk=pk, k=k, hh=hh, ck=ck, n=n: e.matmul(pk[0:64, 0:n], wukv_t[:, k, hh * 128:hh * 128 + 64], ck[k][:, 0:n], start=(k == 0), stop=(k == 1)), [wukv_t, ck[k]], [pk])
                    P.stage(hh, 1)
                    sq = sqr.next()
                    P.act(lambda e, sq=sq, pk=pk, n=n: e.activation(out=sq[0:64, 0:n], in_=pk[0:64, 0:n], func=AF.Square), [pk], [sq])
                    P.stage(hh, 2)
                    pm = pB.next()
                    P.pe(lambda e, pm=pm, sq=sq, n=n: e.matmul(pm[0:64, 0:n], b96[0:64, 0:64], sq[0:64, 0:n], start=True, stop=True), [b96, sq], [pm])
                    P.stage(hh, 3)
                    t1 = f5.next()
                    P.act(lambda e, t1=t1, pm=pm, n=n: e.activation(out=t1[0:64, 0:n], in_=pm[0:64, 0:n], func=AF.Ln, bias=EPS), [pm], [t1])
                    P.act(lambda e, t1=t1, n=n: e.activation(out=t1[0:64, 0:n], in_=t1[0:64, 0:n], func=AF.Exp, scale=-0.5), [t1], [t1])
                    kn = o2r.next()
                    P.dve(lambda e, kn=kn, pk=pk, t1=t1, n=n: e.scalar_tensor_tensor(out=kn[0:64, 0:n], in0=pk[0:64, 0:n], scalar=knn_t[0:64, 0:1], in1=t1[0:64, 0:n], op0=ALU.mult, op1=ALU.mult), [pk, knn_t, t1], [kn])
                    P.dma(lambda e, kn=kn, hh=hh, n=n: e.dma_start(out=KTD[hh, 0:64, kcol:kcol + n], in_=kn[0:64, 0:n]), [kn], [key], ("b5", kn.ri))
                    P.dma(lambda e, kr_=kr_, hh=hh, n=n: e.dma_start(out=KTD[hh, 64:96, kcol:kcol + n], in_=kr_[0:32, 0:n]), [kr_], [key], ("krd", hh % 4))
                for s, r in subs(n):
                    P.stage(8 + s, 0)
                    pv = pC.next()
                    pvv = pv.ap.rearrange("p (h d) -> p h d", d=64)
                    for k in range(2):
                        P.pe(lambda e, pv=pv, k=k, s=s, r=r, ck=ck: e.matmul(pv[0:r, :], ck[k][:, s * 128:s * 128 + r], wv[:, k, :], start=(k == 0), stop=(k == 1)), [wv, ck[k]], [pv])
                    P.stage(8 + s, 1)
                    v_ = vt.next()
                    P.act(lambda e, v_=v_, pvv=pvv, r=r: e.copy(out=v_[0:r, :, 0:64], in_=pvv[0:r, :, :]), [pv], [v_])
                    P.dma(lambda e, v_=v_, s=s, r=r: e.dma_start(out=VSD[vrow + s * 128:vrow + s * 128 + r, :, :], in_=v_[0:r, :, :]), [v_], [key], ("vt", v_.ri))
                P.end_group()

            for ti, (r0, n) in enumerate(tiles):
                h = hT.next()
                P.dma(lambda e, h=h, r0=r0, n=n: e.dma_start(out=h[:, :, 0:n], in_=HTD[:, :, r0:r0 + n].rearrange("k p t -> p k t")),
                      [("HTD", ti)], [h], ("hld", h.ri))
                cs_t = cst_r.next(); Ct = c96.next(); St = s96.next()
                if n == 512:
                    P.dma(lambda e, cs_t=cs_t, r0=r0: e.dma_start(out=cs_t[:, :, :], in_=c_cs[r0:r0 + 512, :].rearrange("(s p) c -> p s c", p=128)), [], [cs_t], ("cst", cs_t.ri))
                else:
                    P.dma(lambda e, cs_t=cs_t, r0=r0, n=n: e.dma_start(out=cs_t[0:n, 0, :], in_=c_cs[r0:r0 + n, :]), [], [cs_t], ("cst", cs_t.ri))
                P.dma(lambda e, Ct=Ct, r0=r0, n=n: e.dma_start(out=Ct[0:96, 0:n], in_=c_C96[:, r0:r0 + n]), [], [Ct], ("c96", Ct.ri))
                P.dma(lambda e, St=St, r0=r0, n=n: e.dma_start(out=St[0:96, 0:n], in_=c_S96[:, r0:r0 + n]), [], [St], ("s96", St.ri))
                ck = ckvnT.next(); kr_ = krT.next()
                for s, r in (subs(n) if 'a' in P2PARTS else []):
                    pp = psr.next()
                    for k in range(8):
                        P.pe(lambda e, pp=pp, k=k, h=h, s=s, r=r: e.matmul(pp[0:r, 0:288], h[:, k, s * 128:s * 128 + r], w_in_t[:, k, 2440:2728], start=(k == 0), stop=(k == 7)), [w_in_t, h], [pp])
                    s1 = st.next(); s1b = st.next(); s2 = st.next(); jk = xck.next(); kv = kvn.next(); kraw = xck.next()
                    P.act(lambda e, kraw=kraw, pp=pp, r=r: e.copy(out=kraw[0:r, 0:288], in_=pp[0:r, 0:288]), [pp], [kraw])
                    P.act(lambda e, jk=jk, kraw=kraw, s1=s1, r=r: e.activation(out=jk[0:r, 0:256], in_=kraw[0:r, 0:256], func=AF.Square, accum_out=s1[0:r, 0:1]), [kraw], [jk, s1])
                    P.act(lambda e, jk=jk, kraw=kraw, s1b=s1b, r=r: e.activation(out=jk[0:r, 256:288], in_=kraw[0:r, 256:288], func=AF.Square, accum_out=s1b[0:r, 0:1]), [kraw], [jk, s1b])
                    P.act(lambda e, s1=s1, s2=s2, r=r: e.activation(out=s2[0:r, 0:1], in_=s1[0:r, 0:1], func=AF.Ln, bias=EPS, scale=1.0 / 256), [s1], [s2])
                    P.act(lambda e, s1b=s1b, s2=s2, r=r: e.activation(out=s2[0:r, 1:2], in_=s1b[0:r, 0:1], func=AF.Ln, bias=EPS, scale=1.0 / 32), [s1b], [s2])
                    P.act(lambda e, s2=s2, r=r: e.activation(out=s2[0:r, 0:2], in_=s2[0:r, 0:2], func=AF.Exp, scale=-0.5), [s2], [s2])
                    P.dve(lambda e, kv=kv, kraw=kraw, s2=s2, r=r: e.scalar_tensor_tensor(out=kv[0:r, 0:256], in0=kraw[0:r, 0:256], scalar=s2[0:r, 0:1], in1=gain_t[0:r, 0:256], op0=ALU.mult, op1=ALU.mult), [kraw, s2, gain_t], [kv])
                    P.dve(lambda e, jk=jk, kraw=kraw, s2=s2, r=r: e.scalar_tensor_tensor(out=jk[0:r, 256:288], in0=kraw[0:r, 256:288], scalar=s2[0:r, 1:2], in1=gain_t[0:r, 256:288], op0=ALU.mult, op1=ALU.mult), [kraw, s2, gain_t], [jk])
                    if A_LVL < 2:
                        continue
                    cosv = cs_t[0:r, s, 0:16]; sinv = cs_t[0:r, s, 16:32]
                    a1 = r16.next(); a2 = r16.next(); a3 = r16.next(); a4 = r16.next()
                    P.dve(lambda e, a1=a1, jk=jk, cosv=cosv, r=r: e.tensor_tensor(out=a1[0:r, :], in0=jk[0:r, 256:272], in1=cosv, op=ALU.mult), [jk, cs_t], [a1])
                    P.dve(lambda e, a2=a2, jk=jk, sinv=sinv, r=r: e.tensor_tensor(out=a2[0:r, :], in0=jk[0:r, 272:288], in1=sinv, op=ALU.mult), [jk, cs_t], [a2])
                    P.dve(lambda e, a3=a3, jk=jk, cosv=cosv, r=r: e.tensor_tensor(out=a3[0:r, :], in0=jk[0:r, 272:288], in1=cosv, op=ALU.mult), [jk, cs_t], [a3])
                    P.dve(lambda e, a4=a4, jk=jk, sinv=sinv, r=r: e.tensor_tensor(out=a4[0:r, :], in0=jk[0:r, 256:272], in1=sinv, op=ALU.mult), [jk, cs_t], [a4])
                    P.dve(lambda e, kv=kv, a1=a1, a2=a2, r=r: e.tensor_tensor(out=kv[0:r, 256:272], in0=a1[0:r, :], in1=a2[0:r, :], op=ALU.subtract), [a1, a2], [kv])
                    P.dve(lambda e, kv=kv, a3=a3, a4=a4, r=r: e.tensor_tensor(out=kv[0:r, 272:288], in0=a3[0:r, :], in1=a4[0:r, :], op=ALU.add), [a3, a4], [kv])
                    if A_LVL < 3:
                        continue
                    if r0 < NF:
                        rr = r0 + s * 128
                        P.dma(lambda e, kv=kv, rr=rr, r=r: e.dma_start(out=o_pckv[rr:rr + r, :], in_=kv[0:r, 0:256]), [kv], ["o_pckv"], ("kvo", kv.ri))
                        P.dma(lambda e, kv=kv, rr=rr, r=r: e.dma_start(out=o_pkr[rr:rr + r, :], in_=kv[0:r, 256:288]), [kv], ["o_pkr"], ("kvo2", kv.ri))
                    else:
                        P.dma(lambda e, kv=kv: e.dma_start(out=o_pckv[NF:NF + 16, :], in_=kv[0:16, 0:256]), [kv], ["o_pckv"], ("kvo", kv.ri))
                        P.dma(lambda e, kv=kv: e.dma_start(out=o_pkr[NF:NF + 16, :], in_=kv[0:16, 256:288]), [kv], ["o_pkr"], ("kvo2", kv.ri))
                        P.dma(lambda e, kv=kv: e.dma_start(out=o_sckv[:, :], in_=kv[16:80, 0:256]), [kv], ["o_sckv"], ("kvo", kv.ri))
                        P.dma(lambda e, kv=kv: e.dma_start(out=o_skr[:, :], in_=kv[16:80, 256:288]), [kv], ["o_skr"], ("kvo2", kv.ri))
                    if A_LVL < 4:
                        continue
                    pt = psr.next()
                    ptv = pt.ap.bitcast(BF16).rearrange("p (j t) -> p j t", t=128)
                    kb = kvb.next()
                    P.act(lambda e, kb=kb, kv=kv, r=r: e.copy(out=kb[0:r, 0:288], in_=kv[0:r, 0:288]), [kv], [kb])
                    for j in range(3):
                        P.pe(lambda e, ptv=ptv, j=j, kb=kb, r=r: e.transpose(out=ptv[:, j, 0:r], in_=kb[0:r, j * 128:(j + 1) * 128], identity=identb[0:r, 0:r]), [kb, identb], [pt])
                    if A_LVL < 5:
                        continue
                    for j in range(2):
                        P.act(lambda e, ck=ck, ptv=ptv, j=j, s=s, r=r: e.copy(out=ck[j][:, s * 128:s * 128 + r], in_=ptv[:, j, 0:r]), [pt], [ck[j]])
                    if A_LVL < 6:
                        continue
                    P.act(lambda e, kr_=kr_, ptv=ptv, s=s, r=r: e.copy(out=kr_[0:32, s * 128:s * 128 + r], in_=ptv[0:32, 2, 0:r]), [pt], [kr_])
                if 'b' in P2PARTS:
                    kv_expand(ck, kr_, n, r0, r0, ("KV", ti))
                if 'q' not in P2PARTS:
                    continue
                pm = psr.next()
                for c in range(3):
                    pc_ = psr.next()
                    for k in range(8):
                        P.pe(lambda e, pc_=pc_, c=c, k=k, h=h, n=n: e.matmul(pc_[:, 0:n], w_in_t[:, k, 2056 + c * 128:2056 + (c + 1) * 128], h[:, k, 0:n], start=(k == 0), stop=(k == 7)), [w_in_t, h], [pc_])
                    P.act(lambda e, c=c, pc_=pc_, n=n: e.copy(out=cqs[c][:, 0:n], in_=pc_[:, 0:n]), [pc_], [cqs[c]])
                    P.act(lambda e, c=c, pc_=pc_, n=n: e.activation(out=sqc[c][:, 0:n], in_=pc_[:, 0:n], func=AF.Square), [pc_], [sqc[c]])
                for c in range(3):
                    P.pe(lambda e, pm=pm, c=c, n=n: e.matmul(pm[:, 0:n], onesb[:], sqc[c][:, 0:n], start=(c == 0), stop=(c == 2)), [onesb, sqc[c]], [pm])
                t0_ = f5.next()
                P.act(lambda e, t0_=t0_, pm=pm, n=n: e.activation(out=t0_[:, 0:n], in_=pm[:, 0:n], func=AF.Ln, bias=EPS, scale=1.0 / 384), [pm], [t0_])
                P.act(lambda e, t0_=t0_, n=n: e.activation(out=t0_[:, 0:n], in_=t0_[:, 0:n], func=AF.Exp, scale=-0.5), [t0_], [t0_])
                for c in range(3):
                    P.dve(lambda e, c=c, t0_=t0_, n=n: e.scalar_tensor_tensor(out=cqn[c][:, 0:n], in0=cqs[c][:, 0:n], scalar=qnorm_t[:, c:c + 1], in1=t0_[:, 0:n], op0=ALU.mult, op1=ALU.mult), [cqs[c], qnorm_t, t0_], [cqn[c]])
                P.begin_group()
                for hh in range(8):
                    P.stage(hh, 0)
                    pq = pA.next()
                    for c in range(3):
                        P.pe(lambda e, pq=pq, c=c, hh=hh, n=n: e.matmul(pq[0:96, 0:n], wuq_t[:, c, hh * 96:(hh + 1) * 96], cqn[c][:, 0:n], start=(c == 0), stop=(c == 2)), [wuq_t, cqn[c]], [pq])
                    P.stage(hh, 1)
                    sq = sqr.next()
                    P.act(lambda e, sq=sq, pq=pq, n=n: e.activation(out=sq[0:96, 0:n], in_=pq[0:96, 0:n], func=AF.Square), [pq], [sq])
                    P.stage(hh, 2)
                    pm2 = pB.next()
                    P.pe(lambda e, pm2=pm2, sq=sq, n=n: e.matmul(pm2[0:96, 0:n], b96[0:96, 0:96], sq[0:96, 0:n], start=True, stop=True), [b96, sq], [pm2])
                    P.stage(hh, 3)
                    t1 = f5.next()
                    P.act(lambda e, t1=t1, pm2=pm2, n=n: e.activation(out=t1[0:96, 0:n], in_=pm2[0:96, 0:n], func=AF.Ln, bias=EPS), [pm2], [t1])
                    P.act(lambda e, t1=t1, n=n: e.activation(out=t1[0:96, 0:n], in_=t1[0:96, 0:n], func=AF.Exp, scale=-0.5), [t1], [t1])
                    qn = qnr.next()
                    P.dve(lambda e, qn=qn, pq=pq, t1=t1, n=n: e.scalar_tensor_tensor(out=qn[0:96, 0:n], in0=pq[0:96, 0:n], scalar=gain96[0:96, 0:1], in1=t1[0:96, 0:n], op0=ALU.mult, op1=ALU.mult), [pq, gain96, t1], [qn])
                    P.stage(hh, 4)
                    pr = pC.next()
                    P.pe(lambda e, pr=pr, qn=qn, n=n: e.matmul(pr[0:96, 0:n], pT96[0:96, 0:96], qn[0:96, 0:n], start=True, stop=True), [pT96, qn], [pr])
                    P.stage(hh, 5)
                    u1 = f5.next(); u2 = f5.next(); qf = o2r.next()
                    P.dve(lambda e, u1=u1, qn=qn, Ct=Ct, n=n: e.tensor_tensor(out=u1[0:96, 0:n], in0=qn[0:96, 0:n], in1=Ct[0:96, 0:n], op=ALU.mult), [qn, Ct], [u1])
                    P.dve(lambda e, u2=u2, pr=pr, St=St, n=n: e.tensor_tensor(out=u2[0:96, 0:n], in0=pr[0:96, 0:n], in1=St[0:96, 0:n], op=ALU.mult), [pr, St], [u2])
                    P.dve(lambda e, qf=qf, u1=u1, u2=u2, n=n: e.tensor_tensor(out=qf[0:96, 0:n], in0=u1[0:96, 0:n], in1=u2[0:96, 0:n], op=ALU.add), [u1, u2], [qf])
                    P.dma(lambda e, qf=qf, hh=hh, r0=r0, n=n: e.dma_start(out=QTD[hh, :, r0:r0 + n], in_=qf[0:96, 0:n]), [qf], [("QTD", ti)], ("b5", qf.ri))
                P.end_group()
            for ct in (range(2) if 'c' in P2PARTS else []):
                ck = ckvnT.next(); kr_ = krT.next()
                for s in range(4):
                    rr = ct * 512 + s * 128
                    kv = kvn.next()
                    P.dma(lambda e, kv=kv, rr=rr: e.dma_start(out=kv[:, 0:256], in_=cckv[rr:rr + 128, :]), [], [kv], ("kvo", kv.ri))
                    P.dma(lambda e, kv=kv, rr=rr: e.dma_start(out=kv[:, 256:288], in_=ckr[rr:rr + 128, :]), [], [kv], ("kvo2", kv.ri))
                    pt = psr.next()
                    ptv = pt.ap.bitcast(BF16).rearrange("p (j t) -> p j t", t=128)
                    kb = kvb.next()
                    P.act(lambda e, kb=kb, kv=kv: e.copy(out=kb[:, 0:288], in_=kv[:, 0:288]), [kv], [kb])
                    for j in range(3):
                        P.pe(lambda e, ptv=ptv, j=j, kb=kb: e.transpose(out=ptv[:, j, :], in_=kb[:, j * 128:(j + 1) * 128], identity=identb[:]), [kb, identb], [pt])
                    for j in range(2):
                        P.act(lambda e, ck=ck, ptv=ptv, j=j, s=s: e.copy(out=ck[j][:, s * 128:(s + 1) * 128], in_=ptv[:, j, :]), [pt], [ck[j]])
                    P.act(lambda e, kr_=kr_, ptv=ptv, s=s: e.copy(out=kr_[0:32, s * 128:(s + 1) * 128], in_=ptv[0:32, 2, :]), [pt], [kr_])
                kv_expand(ck, kr_, 512, NTOK + ct * 512, NTOK + ct * 512, ("KV", "c%d" % ct))

        def g_pass():
            AR.reset()
            def T4():
                return AR.alloc([4, 128], F32)
            ld = [dict(q=T4(), k=T4(), v=T4(), z=AR.alloc([4, 128], BF16), gb=AR.alloc([8], F32)) for _ in range(2)]
            Rt = T4(); D1 = T4(); D2 = T4(); Ege = T4(); Egt = T4(); Elt = T4(); QKT = T4(); tmpM = T4()
            def T2():
                return AR.alloc([2, 128], F32)
            Pk = [[T2(), T2()], [T2(), T2()]]; Qk = [[T2(), T2()], [T2(), T2()]]; Rk = [[T2(), T2()], [T2(), T2()]]
            Vb = T4(); Kb = T4(); kt = T4(); nWkT = T4(); Wsb = T4(); qdT = T4(); og = T4()
            sqo = AR.alloc([4, 128], BF16); ogz = Ring([AR.alloc([4, 128], BF16) for _ in range(2)])
            Ss = {"p": T4(), "s": T4()}
            smr = Ring([AR.alloc([4], F32) for _ in range(12)])
            onorm_t = AR.alloc([1], F32)
            P.dma(lambda e: e.dma_start(out=onorm_t[:, :], in_=onorm[0].rearrange("(p o) -> p o", o=1)), [], [onorm_t], "pv")
            for d in ld:
                for nm in ("q", "k", "v", "gb", "z"):
                    P.dve(lambda e, t=d[nm]: e.memset(t[:], 0.0), [], [d[nm]])
            P.dve(lambda e: e.memset(Ss["p"][:], 0.0), [], [Ss["p"]])
            P.dma(lambda e: e.dma_start(out=Ss["s"][:, :, :], in_=gS.rearrange("h k v -> k h v")), [], [Ss["s"]], "pv2")
            chunks = [(NF + 16, 64, "s", NTL - 1), (NF, 16, "p", NTL - 1)] + [(128 * ci, 128, "p", ci // 4) for ci in range(4 * NT)]
            tri_b = tri.ap.unsqueeze(1).broadcast_to([128, 4, 128])
            mgt_b = mgt.ap.unsqueeze(1).broadcast_to([128, 4, 128])
            mlt_b = mlt.ap.unsqueeze(1).broadcast_to([128, 4, 128])
            id_b = identf.ap.unsqueeze(1).broadcast_to([128, 4, 128])

            def loads(i):
                row0, C, sq_, ti = chunks[i]
                d = ld[i % 2]
                for a, nm in enumerate(("q", "k", "v")):
                    P.dma(lambda e, a=a, t=d[nm], row0=row0, C=C: e.dma_start(out=t[:, :, 0:C], in_=QKVD[4 * a:4 * a + 4, :, row0:row0 + C].rearrange("h p t -> p h t")),
                          [("QKVD", ti)], [d[nm]], ("gl", nm, i % 2))
                P.dma(lambda e, t=d["z"], row0=row0, C=C: e.dma_start(out=t[:, :, 0:C], in_=ZTD[:, :, row0:row0 + C].rearrange("h p t -> p h t")),
                      [("ZTD", ti)], [d["z"]], ("gl", "z", i % 2))
                P.dma(lambda e, t=d["gb"], row0=row0, C=C: e.dma_start(out=t[0:C, :], in_=GBD[row0:row0 + C, :]),
                      [("GBD", ti)], [d["gb"]], ("gl", "gb", i % 2))

            def mm4(ps, lh, rh, start=True, stop=True, R=(), lh2=None):
                for hh in range(4):
                    P.pe(lambda e, hh=hh: e.matmul(ps[:, hh * 128:(hh + 1) * 128], lh[:, hh, :], rh[:, hh, :], start=start, stop=stop), list(R), [ps])

            def compute(i):
                row0, C, sq_, ti = chunks[i]
                d = ld[i % 2]
                qT, kT, vT, zT, gb = d["q"], d["k"], d["v"], d["z"], d["gb"]
                S = Ss[sq_]
                pG = psr.next(); pGL = psr.next()
                P.pe(lambda e: e.matmul(pG[:, 0:4], tri[:], gb[:, 0:4], start=True, stop=True), [tri, gb], [pG])
                P.pe(lambda e: e.matmul(pGL[:, 0:4], onesf[:], gb[:, 0:4], start=True, stop=True), [onesf, gb], [pGL])
                G = smr.next(); eGL = smr.next(); eG = smr.next(); bEG = smr.next(); eGm = smr.next()
                P.act(lambda e: e.copy(out=G[:, :], in_=pG[:, 0:4]), [pG], [G])
                P.act(lambda e: e.activation(out=eGL[:, :], in_=pGL[:, 0:4], func=AF.Exp), [pGL], [eGL])
                P.act(lambda e: e.activation(out=eG[:, :], in_=pG[:, 0:4], func=AF.Exp), [pG], [eG])
                P.dve(lambda e: e.tensor_tensor(out=bEG[:, :], in0=eG[:, :], in1=gb[:, 4:8], op=ALU.mult), [eG, gb], [bEG])
                P.dve(lambda e: e.tensor_tensor(out=eGm[:, :], in0=pGL[:, 0:4], in1=G[:, :], op=ALU.subtract), [pGL, G], [eGm])
                P.act(lambda e: e.activation(out=eGm[:, :], in_=eGm[:, :], func=AF.Exp), [eGm], [eGm])
                for hh in range(4):
                    P.dve(lambda e, hh=hh: e.tensor_scalar(out=Rt[:, hh, :], in0=tri[:], scalar1=gb[:, hh:hh + 1], scalar2=None, op0=ALU.mult), [tri, gb], [Rt])
                pGrow = psr.next()
                P.pe(lambda e: e.matmul(pGrow[:, :], onesf[:], Rt[:, :, :], start=True, stop=True), [onesf, Rt], [pGrow])
                pGr = pGrow.ap.rearrange("p (h t) -> p h t", t=128)
                for hh in range(4):
                    P.dve(lambda e, hh=hh: e.tensor_scalar(out=D1[:, hh, :], in0=pGr[:, hh, :], scalar1=G[:, hh:hh + 1], scalar2=0.0, op0=ALU.subtract, op1=ALU.min), [pGrow, G], [D1])
                    P.dve(lambda e, hh=hh: e.tensor_scalar(out=D2[:, hh, :], in0=pGr[:, hh, :], scalar1=G[:, hh:hh + 1], scalar2=0.0, op0=ALU.subtract, op1=ALU.max), [pGrow, G], [D2])
                P.act(lambda e: e.activation(out=D1[:, :, :], in_=D1[:, :, :], func=AF.Exp), [D1], [D1])
                P.act(lambda e: e.activation(out=D2[:, :, :], in_=D2[:, :, :], func=AF.Exp, scale=-1.0), [D2], [D2])
                P.act(lambda e: e.activation(out=qdT[:, :, :], in_=pGr[:, :, :], func=AF.Exp), [pGrow], [qdT])
                P.dve(lambda e: e.tensor_tensor(out=qdT[:, :, :], in0=qdT[:, :, :], in1=qT[:, :, :], op=ALU.mult), [qdT, qT], [qdT])
                P.dve(lambda e: e.tensor_tensor(out=Ege[:, :, :], in0=D1[:, :, :], in1=tri_b, op=ALU.mult), [D1, tri], [Ege])
                P.dve(lambda e: e.tensor_tensor(out=Egt[:, :, :], in0=D1[:, :, :], in1=mgt_b, op=ALU.mult), [D1, mgt], [Egt])
                P.dve(lambda e: e.tensor_tensor(out=Elt[:, :, :], in0=D2[:, :, :], in1=mlt_b, op=ALU.mult), [D2, mlt], [Elt])
                for hh in range(4):
                    P.dve(lambda e, hh=hh: e.tensor_scalar(out=Rt[:, hh, :], in0=identf[:], scalar1=gb[:, 4 + hh:5 + hh], scalar2=None, op0=ALU.mult), [identf, gb], [Rt])
                pBrow = psr.next()
                P.pe(lambda e: e.matmul(pBrow[:, :], onesf[:], Rt[:, :, :], start=True, stop=True), [onesf, Rt], [pBrow])
                pBr = pBrow.ap.rearrange("p (h t) -> p h t", t=128)
                pKK = psr.next(); pKQ = psr.next()
                mm4(pKK, kT, kT, R=[kT]); mm4(pKQ, kT, qT, R=[kT, qT])
                pKKv = pKK.ap.rearrange("p (h t) -> p h t", t=128); pKQv = pKQ.ap.rearrange("p (h t) -> p h t", t=128)
                P.dve(lambda e: e.tensor_tensor(out=QKT[:, :, :], in0=pKQv, in1=Ege[:, :, :], op=ALU.mult), [pKQ, Ege], [QKT])
                P.dve(lambda e: e.scalar_tensor_tensor(out=tmpM[:, :, :], in0=pKKv, scalar=-1.0, in1=Egt[:, :, :], op0=ALU.mult, op1=ALU.mult), [pKK, Egt], [tmpM])
                P0, Q0, R0 = Pk[0], Qk[0], Rk[0]
                id_b2 = identf.ap.unsqueeze(1).broadcast_to([128, 2, 128])
                for g in range(2):
                    P.dve(lambda e, g=g: e.tensor_tensor(out=P0[g][:, :, :], in0=tmpM[:, 2 * g:2 * g + 2, :], in1=pBr[:, 2 * g:2 * g + 2, :], op=ALU.mult), [tmpM, pBrow], [P0[g]])
                tmpQ = D1
                P.dve(lambda e: e.tensor_tensor(out=tmpQ[:, :, :], in0=pKKv, in1=Elt[:, :, :], op=ALU.mult), [pKK, Elt], [tmpQ])
                for hh in range(4):
                    P.dve(lambda e, hh=hh: e.tensor_scalar(out=Q0[hh // 2][:, hh % 2, :], in0=tmpQ[:, hh, :], scalar1=gb[:, 4 + hh:5 + hh], scalar2=-1.0, op0=ALU.mult, op1=ALU.mult), [tmpQ, gb], [Q0[hh // 2]])
                for g in range(2):
                    P.dve(lambda e, g=g: e.tensor_tensor(out=R0[g][:, :, :], in0=P0[g][:, :, :], in1=id_b2, op=ALU.add), [P0[g], identf], [R0[g]])
                cur = 0

                def mmg(ps, lh, rh, R=()):
                    for j in range(2):
                        P.pe(lambda e, j=j: e.matmul(ps[:, j * 128:(j + 1) * 128], lh[:, j, :], rh[:, j, :], start=True, stop=True), list(R), [ps])

                def pv2(ps):
                    return ps.ap[:, 0:256].rearrange("p (h t) -> p h t", t=128)

                for lvl in range(1, 7):
                    Pc, Qc, Rc = Pk[cur], Qk[cur], Rk[cur]
                    Pn, Qn, Rn = Pk[1 - cur], Qk[1 - cur], Rk[1 - cur]
                    pQ = [psr.next(), psr.next()]
                    pP = [psr.next(), psr.next()] if lvl < 6 else None
                    for g in range(2):
                        mmg(pQ[g], Pc[g], Qc[g], R=[Pc[g], Qc[g]])
                        if lvl < 6:
                            mmg(pP[g], Qc[g], Pc[g], R=[Pc[g], Qc[g]])
                    for g in range(2):
                        P.act(lambda e, g=g, Qn=Qn, pq=pQ[g]: e.copy(out=Qn[g][:, :, :], in_=pv2(pq)), [pQ[g]], [Qn[g]])
                        if lvl < 6:
                            P.dve(lambda e, g=g, Pn=Pn, pp_=pP[g]: e.tensor_copy(out=Pn[g][:, :, :], in_=pv2(pp_)), [pP[g]], [Pn[g]])
                    pR = [psr.next(), psr.next()]
                    for g in range(2):
                        mmg(pR[g], Qn[g], Rc[g], R=[Qn[g], Rc[g]])
                    for g in range(2):
                        P.dve(lambda e, g=g, Rn=Rn, Rc=Rc, pr=pR[g]: e.tensor_tensor(out=Rn[g][:, :, :], in0=Rc[g][:, :, :], in1=pv2(pr), op=ALU.add), [Rc[g], pR[g]], [Rn[g]])
                    cur = 1 - cur
                TT = Rk[cur]
                pK = psr.next(); pV = psr.next()
                for hh in range(4):
                    P.pe(lambda e, hh=hh: e.matmul(pK[:, hh * 128:(hh + 1) * 128], kT[:, hh, :], identf[:], start=True, stop=True), [kT, identf], [pK])
                for hh in range(4):
                    P.pe(lambda e, hh=hh: e.matmul(pV[:, hh * 128:(hh + 1) * 128], vT[:, hh, :], identf[:], start=True, stop=True), [vT, identf], [pV])
                for hh in range(4):
                    P.act(lambda e, hh=hh: e.activation(out=Vb[:, hh, :], in_=pV[:, hh * 128:(hh + 1) * 128], func=AF.Copy, scale=gb[:, 4 + hh:5 + hh]), [pV, gb], [Vb])
                    P.dve(lambda e, hh=hh: e.tensor_scalar(out=Kb[:, hh, :], in0=pK[:, hh * 128:(hh + 1) * 128], scalar1=bEG[:, hh:hh + 1], scalar2=None, op0=ALU.mult), [pK, bEG], [Kb])
                    P.act(lambda e, hh=hh: e.activation(out=kt[:, hh, :], in_=pK[:, hh * 128:(hh + 1) * 128], func=AF.Copy, scale=eGm[:, hh:hh + 1]), [pK, eGm], [kt])
                pWk = psr.next()
                for hh in range(4):
                    P.pe(lambda e, hh=hh: e.matmul(pWk[:, hh * 128:(hh + 1) * 128], Kb[:, hh, :], TT[hh // 2][:, hh % 2, :], start=True, stop=True), [Kb, TT[hh // 2]], [pWk])
                P.act(lambda e: e.mul(out=nWkT[:, :, :], in_=pWk.ap.rearrange("p (h t) -> p h t", t=128), mul=-1.0), [pWk], [nWkT])
                pW = psr.next()
                for hh in range(4):
                    P.pe(lambda e, hh=hh: e.matmul(pW[:, hh * 128:(hh + 1) * 128], TT[hh // 2][:, hh % 2, :], Vb[:, hh, :], start=True, stop=False), [TT[hh // 2], Vb], [pW])
                    P.pe(lambda e, hh=hh: e.matmul(pW[:, hh * 128:(hh + 1) * 128], nWkT[:, hh, :], S[:, hh, :], start=False, stop=True), [nWkT, S], [pW])
                P.act(lambda e: e.copy(out=Wsb[:, :, :], in_=pW.ap.rearrange("p (h t) -> p h t", t=128)), [pW], [Wsb])
                pO = psr.next()
                for hh in range(4):
                    P.pe(lambda e, hh=hh: e.matmul(pO[:, hh * 128:(hh + 1) * 128], S[:, hh, :], qdT[:, hh, :], start=True, stop=False), [S, qdT], [pO])
                    P.pe(lambda e, hh=hh: e.matmul(pO[:, hh * 128:(hh + 1) * 128], Wsb[:, hh, :], QKT[:, hh, :], start=False, stop=True), [Wsb, QKT], [pO])
                pS_ = psr.next()
                mm4(pS_, kt, Wsb, R=[kt, Wsb])
                for hh in range(4):
                    P.dve(lambda e, hh=hh: e.scalar_tensor_tensor(out=S[:, hh, :], in0=S[:, hh, :], scalar=eGL[:, hh:hh + 1], in1=pS_[:, hh * 128:(hh + 1) * 128], op0=ALU.mult, op1=ALU.add), [S, eGL, pS_], [S])
                P.act(lambda e: e.activation(out=sqo[:, :, :], in_=pO.ap.rearrange("p (h t) -> p h t", t=128), func=AF.Square), [pO], [sqo])
                pMS = psr.next()
                P.pe(lambda e: e.matmul(pMS[:, :], onesb[:], sqo[:, :, :], start=True, stop=True), [onesb, sqo], [pMS])
                P.act(lambda e: e.activation(out=og[:, :, :], in_=pMS.ap.rearrange("p (h t) -> p h t", t=128), func=AF.Ln, bias=EPS, scale=1.0 / 128), [pMS], [og])
                P.act(lambda e: e.activation(out=og[:, :, :], in_=og[:, :, :], func=AF.Exp, scale=-0.5), [og], [og])
                P.dve(lambda e: e.scalar_tensor_tensor(out=og[:, :, :], in0=pO.ap.rearrange("p (h t) -> p h t", t=128), scalar=onorm_t[:, 0:1], in1=og[:, :, :], op0=ALU.mult, op1=ALU.mult), [pO, onorm_t, og], [og])
                oz = ogz.next()
                P.dve(lambda e: e.tensor_tensor(out=oz[:, :, :], in0=og[:, :, :], in1=zT[:, :, :], op=ALU.mult), [og, zT], [oz])
                P.dma(lambda e: e.dma_start(out=OTD[0:4, :, row0:row0 + C].rearrange("h p t -> p h t"), in_=oz[:, :, 0:C]), [oz], [("OTD", ti)], ("ogz", oz.ri))

            loads(0)
            for i in range(len(chunks)):
                if i + 1 < len(chunks):
                    loads(i + 1)
                compute(i)
                if i == 0:
                    P.dma(lambda e: e.dma_start(out=o_sS.rearrange("h k v -> k h v"), in_=Ss["s"][:, :, :]), [Ss["s"]], ["o_sS"], "so")
            P.dma(lambda e: e.dma_start(out=o_pS.rearrange("h k v -> k h v"), in_=Ss["p"][:, :, :]), [Ss["p"]], ["o_pS"], "so")

        def a_pass():
            AR.reset()
            NKT = 4 * NT + 2 + 8
            hd = [dict(K=AR.alloc([NK], BF16), Q=AR.alloc([NTOK], BF16), V=AR.alloc([NKT, 65], BF16)) for _ in range(2)]
            ptr = Ring([AR.alloc([512], BF16) for _ in range(4)])
            amask = AR.alloc([4, 512], BF16)
            osb = Ring([AR.alloc([512], F32) for _ in range(2)])
            rrow = Ring([AR.alloc([512], F32) for _ in range(2)])
            onr = Ring([AR.alloc([512], BF16) for _ in range(2)])
            P.dma(lambda e: e.dma_start(out=amask[:, :, :], in_=c_amask[:, :, :]), [], [amask], "c1", q="pool")
            psS = Ring(PS[0:5]); psO = Ring(PS[5:7]); psB = PS[7]
            allkv = [("KV", ti) for ti in range(NTL)] + [("KV", "c0"), ("KV", "c1")]
            allq = [("QTD", ti) for ti in range(NTL)]
            SC = 96.0 ** -0.5

            def hloads(hh):
                d = hd[hh % 2]
                P.dma(lambda e: e.dma_start(out=d["K"][0:96, :], in_=KTD[hh, :, :]), allkv, [d["K"]], ("aK", hh % 2))
                P.dma(lambda e: e.dma_start(out=d["Q"][0:96, :], in_=QTD[hh, :, :]), allq, [d["Q"]], ("aQ", hh % 2))
                for t in range(NT):
                    P.dma(lambda e, t=t: e.dma_start(out=d["V"][:, 4 * t:4 * t + 4, :], in_=VSD[512 * t:512 * t + 512, hh, :].rearrange("(k p) d -> p k d", p=128)), allkv, [d["V"]], ("aV", hh % 2, t % 2))
                P.dma(lambda e: e.dma_start(out=d["V"][0:16, 4 * NT, :], in_=VSD[NF:NF + 16, hh, :]), allkv, [d["V"]], ("aV", hh % 2, 0))
                P.dma(lambda e: e.dma_start(out=d["V"][0:64, 4 * NT + 1, :], in_=VSD[NF + 16:NF + 80, hh, :]), allkv, [d["V"]], ("aV", hh % 2, 1))
                for t in range(2):
                    P.dma(lambda e, t=t: e.dma_start(out=d["V"][:, 4 * NT + 2 + 4 * t:4 * NT + 6 + 4 * t, :], in_=VSD[NTOK + 512 * t:NTOK + 512 * t + 512, hh, :].rearrange("(k p) d -> p k d", p=128)), allkv, [d["V"]], ("aV", hh % 2, t % 2))

            pend = []

            def attend(hh, d, q0, nq, keys, ti):
                pO = psO.next()
                pSs = {}

                def emitS(ki):
                    kc, nk, vti, mk = keys[ki]
                    pS = psS.next()
                    pSs[ki] = pS
                    P.pe(lambda e: e.matmul(pS[0:nk, 0:nq], d["K"][0:96, kc:kc + nk], d["Q"][0:96, q0:q0 + nq], start=True, stop=True), [d["K"], d["Q"]], [pS])

                LA = 4
                for ki in range(min(LA, len(keys))):
                    emitS(ki)
                for ki, (kc, nk, vti, mk) in enumerate(keys):
                    pS = pSs.pop(ki); pt = ptr.next()
                    P.act(lambda e, pS=pS, pt=pt, nk=nk: e.activation(out=pt[0:nk, 0:nq], in_=pS[0:nk, 0:nq], func=AF.Exp, scale=SC), [pS], [pt])
                    if mk is not None:
                        P.dve(lambda e, pt=pt, mk=mk, nk=nk: e.tensor_tensor(out=pt[0:nk, 0:nq], in0=pt[0:nk, 0:nq], in1=amask[0:nk, mk, 0:nq], op=ALU.mult), [pt, amask], [pt])
                    if ki + LA < len(keys):
                        emitS(ki + LA)
                    P.pe(lambda e, pt=pt, nk=nk, vti=vti, ki=ki: e.matmul(pO[0:65, 0:nq], d["V"][0:nk, vti, :], pt[0:nk, 0:nq], start=(ki == 0), stop=(ki == len(keys) - 1)), [d["V"], pt], [pO])
                    if ki == 2 and pend:
                        pend.pop()()
                if pend:
                    pend.pop()()
                rr = rrow.next(); ob = osb.next(); on = onr.next()
                P.dve(lambda e: e.reciprocal(out=rr[64:65, 0:nq], in_=pO[64:65, 0:nq]), [pO], [rr])
                P.act(lambda e: e.copy(out=ob[0:64, 0:nq], in_=pO[0:64, 0:nq]), [pO], [ob])

                def fin_b():
                    P.pe(lambda e: e.matmul(psB[0:64, 0:nq], onesf[64:65, 0:64], rr[64:65, 0:nq], start=True, stop=True), [onesf, rr], [psB])
                    P.dve(lambda e: e.tensor_tensor(out=on[0:64, 0:nq], in0=ob[0:64, 0:nq], in1=psB[0:64, 0:nq], op=ALU.mult), [ob, psB], [on])
                    P.dma(lambda e: e.dma_start(out=OTD[4 + hh // 2, (hh % 2) * 64:(hh % 2) * 64 + 64, q0:q0 + nq], in_=on[0:64, 0:nq]), [on], [("OTD", ti)], ("aO", on.ri))
                pend.append(fin_b)

            hloads(0)
            for hh in range(8):
                if hh + 1 < 8:
                    hloads(hh + 1)
                d = hd[hh % 2]
                meta_k = (NF, 16, 4 * NT, None)
                attend(hh, d, NF, 16, [meta_k], NTL - 1)
                attend(hh, d, NF + 16, 64, [(NTOK + 128 * j, 128, 4 * NT + 2 + j, None) for j in range(8)] + [(NF + 16, 64, 4 * NT + 1, None)], NTL - 1)
                for t in range(NT):
                    keys = [meta_k] + [(128 * k, 128, k, (k - 4 * t) if k >= 4 * t else None) for k in range(4 * t + 4)]
                    attend(hh, d, 512 * t, 512, keys, t)
            while pend:
                pend.pop()()

        def o_pass(wts, src, dst, key_src, key_dst):
            w_out_t = wts[3]
            AR.reset()
            xt = Ring([AR.alloc([1024], F32) for _ in range(4)])
            ot = Ring([[AR.alloc([512], BF16) for _ in range(8)] for _ in range(2)])
            oi = 0
            for ti, (r0, n) in enumerate(tiles):
                o = ot.next(); oi += 1
                for c in range(8):
                    P.dma(lambda e, c=c, o=o, r0=r0, n=n: e.dma_start(out=o[c][:, 0:n], in_=OTD[c, :, r0:r0 + n]), [("OTD", ti)], [o[c]], ("oO", oi % 2, c % 4))
                for s, r in subs(n):
                    y = xt.next()
                    for (p0, cnt, sap) in xrows(src[1], src[0], r0 + 128 * s, r):
                        P.dma(lambda e, y=y, p0=p0, cnt=cnt, sap=sap: e.dma_start(out=y[p0:p0 + cnt, :], in_=sap),
                              [(key_src, ti)], [y], ("xt", y.ri))
                    for dh in range(2):
                        pd = psr.next()
                        for c in range(8):
                            P.pe(lambda e, c=c, pd=pd, s=s, r=r, dh=dh, o=o: e.matmul(pd[0:r, :], o[c][:, s * 128:s * 128 + r], w_out_t[:, c, dh * 512:(dh + 1) * 512], start=(c == 0), stop=(c == 7)), [w_out_t, o[c]], [pd])
                        P.dve(lambda e, pd=pd, y=y, r=r, dh=dh: e.tensor_tensor(out=y[0:r, dh * 512:(dh + 1) * 512], in0=pd[0:r, :], in1=y[0:r, dh * 512:(dh + 1) * 512], op=ALU.add), [pd, y], [y])
                    for (p0, cnt, dap) in xrows(dst[1], dst[0], r0 + 128 * s, r):
                        P.dma(lambda e, y=y, p0=p0, cnt=cnt, dap=dap: e.dma_start(out=dap, in_=y[p0:p0 + cnt, :]),
                              [y], [(key_dst, ti)], ("xo", y.ri))

        if sched == "ffn":
            W0 = load_ffn(0, 1, 0, 0)
            W1 = load_ffn(1, 1, 0, 1)
            ffn_pass(W0, 0, f1n[0], ("in", None), ("x", XA), "IN", "XA")
            ffn_pass(W1, 1, None, ("x", XA), ("out", None), "XA", "OUT")
        elif sched == "mix0":
            Wm = load_mix0(0)
            p1_pass(Wm, ("in", None), "IN")
            p2_pass(Wm)
            g_pass()
            a_pass()
            o_pass(Wm, ("in", None), ("out", None), "IN", "OUT")
        elif sched == "full":
            Wa = load_ffn(0, 1, 0, 0)
            Wb = load_ffn(1, 1, 0, 1)
            ffn_pass(Wa, 0, f1n[0], ("in", None), ("x", XA), "IN", "XA")
            Wm = load_mix0(0)
            ffn_pass(Wb, 1, None, ("x", XA), ("x", XB), "XA", "XB")
            Wa = load_ffn(1, 2, 0, 0)
            p1_pass(Wm, ("x", XB), "XB")
            p2_pass(Wm)
            g_pass()
            a_pass()
            o_pass(Wm, ("x", XB), ("x", XA), "XB", "XA")
            Wb = load_ffn(0, 2, 0, 1)
            ffn_pass(Wa, 0, f2n[0], ("x", XA), ("x", XB), "XA", "XB")
            Wa = load_ffn(1, 1, 1, 0)
            ffn_pass(Wb, 1, None, ("x", XB), ("x", XA), "XB", "XA")
            Wb = load_ffn(0, 1, 1, 1)
            ffn_pass(Wa, 0, f1n[1], ("x", XA), ("x", XB), "XA", "XB")
            Ws = load_sc(1)
            ffn_pass(Wb, 1, None, ("x", XB), ("x", XA), "XB", "XA")
            Wa = load_ffn(0, 2, 1, 0)
            sconv_pass(Ws, ("x", XA), ("x", XB), "XA", "XB")
            Wb = load_ffn(1, 2, 1, 1)
            ffn_pass(Wa, 0, f2n[1], ("x", XB), ("x", XA), "XB", "XA")
            ffn_pass(Wb, 1, None, ("x", XA), ("out", None), "XA", "OUT")
        elif sched == "p1g":
            Wm = load_mix0(0)
            p1_pass(Wm, ("in", None), "IN")
            g_pass()
        elif sched in ("p12", "p1", "p2"):
            Wm = load_mix0(0)
            if sched != "p2":
                p1_pass(Wm, ("in", None), "IN")
            if sched != "p1":
                p2_pass(Wm)
        elif sched == "sconv":
            W0 = load_sc(0)
            sconv_pass(W0, ("in", None), ("out", None), "IN", "OUT")
        P.finish(st)
        LAST['P'] = P
    return nc


def _consts(NT):
    NF = NT * 512
    NTOK = NF + 80
    c = {}
    c["c_ident"] = np.eye(128, dtype=np.float32)
    j = np.arange(128)
    c["c_tri"] = (j[:, None] <= j[None, :]).astype(np.float32)
    c["c_mgt"] = (j[None, :] > j[:, None]).astype(np.float32)
    c["c_mlt"] = (j[None, :] < j[:, None]).astype(np.float32)
    am = np.zeros((128, 4, 512), np.float32)
    p = np.arange(128)[:, None]; f = np.arange(512)[None, :]
    for d in range(4):
        am[:, d, :] = ((2 * d + p // 64) <= (f // 64)).astype(np.float32)
    c["c_amask"] = am
    b = np.zeros((96, 96), np.float32); b[:64, :64] = 1.0 / 64; b[64:, 64:] = 1.0 / 32
    c["c_b96"] = b
    Pm = np.zeros((96, 96), np.float32)
    for a in range(16):
        Pm[64 + a, 64 + 16 + a] = -1.0
        Pm[64 + 16 + a, 64 + a] = 1.0
    c["c_pT"] = np.ascontiguousarray(Pm.T)
    pos = np.concatenate([16 + np.arange(NF), np.arange(16), 1024 + np.arange(64)]).astype(np.float32)
    inv = (np.float32(10000.0) ** (-np.arange(16, dtype=np.float32) / np.float32(16))).astype(np.float32)
    ang = (pos[:, None] * inv[None, :]).astype(np.float32)
    cs = np.cos(ang.astype(np.float64)).astype(np.float32); sn = np.sin(ang.astype(np.float64)).astype(np.float32)
    c["c_cs"] = np.ascontiguousarray(np.concatenate([cs, sn], 1))
    C96 = np.ones((96, NTOK), np.float32); S96 = np.zeros((96, NTOK), np.float32)
    C96[64:80] = cs.T; C96[80:96] = cs.T; S96[64:80] = sn.T; S96[80:96] = sn.T
    c["c_C96"] = C96; c["c_S96"] = S96
    return c


_CACHE = {}
P2PARTS = 'abqc'
A_LVL = 9
DEBUG = False
LAST = {}
SCHED = 'full'


def kernel(**inp):
    f = lambda a: np.ascontiguousarray(np.asarray(a, dtype=np.float32))
    NT = inp["x_prompt"].shape[1] // 512
    NF = NT * 512
    if NT not in _CACHE:
        _CACHE[NT] = build(NT, SCHED)
    nc = _CACHE[NT]
    cs = _consts(NT)
    shared = dict(cs)
    shared.update(meta=f(inp["meta_tokens"]), f1n=f(inp["ffn1_norm"]), f2n=f(inp["ffn2_norm"]), mixn=f(inp["mix_norm"]),
                  f1g=f(inp["ffn1_w_gate"]), f1u=f(inp["ffn1_w_up"]), f1d=f(inp["ffn1_w_down"]),
                  f2g=f(inp["ffn2_w_gate"]), f2u=f(inp["ffn2_w_up"]), f2d=f(inp["ffn2_w_down"]),
                  w_in=f(inp["ab_w_in"][0]), w_out=f(inp["ab_w_out"][0]), convw=f(inp["gdn_conv_w"][0]),
                  alog=f(inp["gdn_A_log"]), dtb=f(inp["gdn_dt_bias"]), onorm=f(inp["gdn_o_norm"]),
                  qnorm=f(inp["mla_q_norm"]), wuq=f(inp["mla_w_uq"][0]), kvnorm=f(inp["mla_kv_norm"]),
                  wukv=f(inp["mla_w_ukv"][0]), qnn=f(inp["mla_qn_norm"]), qrn=f(inp["mla_qr_norm"]),
                  knn=f(inp["mla_kn_norm"]), krn=f(inp["mla_kr_norm"]),
                  scin=f(inp["sc_w_in"][0]), sccw=f(inp["sc_conv_w"][0]), scout=f(inp["sc_w_out"][0]))
    in_maps = []
    for b in range(8):
        m = dict(shared)
        m.update(xp=f(inp["x_prompt"][b]), xs=f(inp["x_sample"][b]), cckv=f(inp["cache_mla_ckv"][0, b]),
                 ckr=f(inp["cache_mla_krope"][0, b]), gS=f(inp["state_gdn_S"][0, b]),
                 gconv=f(inp["state_gdn_conv"][0, b]), sconv=f(inp["state_sconv"][0, b]))
        in_maps.append(m)
    res = run_bass_kernel_spmd(nc, in_maps, core_ids=list(range(8))).results
    LAST["res"] = res
    g = lambda k: np.stack([np.asarray(res[b][k], dtype=np.float32) for b in range(8)])
    pck = g("o_pckv"); pkr = g("o_pkr")
    pck = np.concatenate([pck[:, NF:], pck[:, :NF]], 1); pkr = np.concatenate([pkr[:, NF:], pkr[:, :NF]], 1)
    return (g("y_p"), g("y_s"), pck[None], pkr[None], g("o_pS")[None], g("o_pconv")[None], g("o_psc")[None],
            g("o_sckv")[None], g("o_skr")[None], g("o_sS")[None], g("o_sconv")[None], g("o_ssc")[None])
```

```python
import numpy as np
from contextlib import ExitStack
import concourse.bass as bass
import concourse.mybir as mybir
from concourse.bass_utils import run_bass_kernel_spmd

F32 = mybir.dt.float32
BF16 = mybir.dt.bfloat16
U8 = mybir.dt.uint8
AF = mybir.ActivationFunctionType
ALU = mybir.AluOpType
EPS = 1e-6
DEBUG = False
DFF = 2816
HFF = 1408
NCH = 11
SAME_ENG_SYNC = True


class Tile:
    def __init__(self, ap, res):
        self.ap = ap
        self.res = tuple(res)

    def __getitem__(self, k):
        return self.ap[k]


def _flat(xs):
    out = []
    for x in xs:
        if isinstance(x, Tile):
            out.extend(x.res)
        elif isinstance(x, (list, tuple)) and x and isinstance(x[0], (Tile, list, tuple)):
            out.extend(_flat(x))
        else:
            out.append(x)
    return out


class Op:
    __slots__ = ("eng", "fn", "R", "W", "dma", "deps", "sig", "val", "sem", "tag")


class Prog:
    ENG = ("pe", "act", "dve", "pool", "sp")

    def __init__(self, nc):
        self.nc = nc
        self.ops = []
        self.dcount = {}
        self._tag = None
        self._g0 = None

    def add(self, eng, fn, R=(), W=(), dma=None):
        o = Op()
        o.eng, o.fn, o.R, o.W, o.dma = eng, fn, _flat(R), _flat(W), dma
        for r in o.R:
            if isinstance(r, tuple) and r and r[0] == "PS" and r not in o.W:
                o.W.append(r)
        o.deps = ()
        o.sig = dma is not None
        o.tag = self._tag
        self.ops.append(o)
        return o

    def begin_group(self):
        self._g0 = len(self.ops)

    def stage(self, item, k):
        self._tag = (item, k)

    def end_group(self):
        seg = self.ops[self._g0:]
        assert all(o.tag is not None for o in seg)
        order = sorted(range(len(seg)), key=lambda i: (seg[i].tag[0] + seg[i].tag[1], -seg[i].tag[1], i))
        self.ops[self._g0:] = [seg[i] for i in order]
        self._tag = None
        self._g0 = None

    def pe(self, fn, R=(), W=()):
        return self.add("pe", fn, R, W)

    def act(self, fn, R=(), W=()):
        return self.add("act", fn, R, W)

    def dve(self, fn, R=(), W=()):
        return self.add("dve", fn, R, W)

    DPOOL = {"sp": 32, "pool": 12}

    def dma(self, fn, R, W, sem, q="sp"):
        c = self.dcount.get(q, 0)
        self.dcount[q] = c + 1
        return self.add(q, fn, R, W, dma=(q, c % self.DPOOL[q]))

    def finish(self, stack):
        nc = self.nc
        ops = self.ops
        lastw = {}
        readers = {}
        lastdma = {}
        for i, o in enumerate(ops):
            d = set()
            if o.dma is not None:
                if o.dma in lastdma:
                    d.add(lastdma[o.dma])
                lastdma[o.dma] = i
            for r in o.R:
                if r in lastw:
                    d.add(lastw[r])
            for r in o.W:
                if r in lastw:
                    d.add(lastw[r])
                rd = readers.get(r)
                if rd:
                    d.update(rd.values())
            d.discard(i)
            o.deps = d
            for r in o.R:
                readers.setdefault(r, {})[o.eng if o.dma is None else ("dma", i)] = i
            for r in o.W:
                lastw[r] = i
                readers[r] = {}
        fin = Op()
        fin.eng, fin.fn, fin.R, fin.W, fin.dma, fin.sig = "sp", None, [], [], None, False
        fin.deps = set(i for i, o in enumerate(ops) if o.dma is not None)
        ops.append(fin)
        for o in ops:
            for j in o.deps:
                y = ops[j]
                if y.dma is None and not (y.eng == o.eng and o.dma is None and (o.eng == "pe" or not SAME_ENG_SYNC)):
                    y.sig = True
        esem = {e: stack.enter_context(nc.semaphore("s_" + e)) for e in self.ENG}
        dsem = {}
        ecnt = {e: 0 for e in self.ENG}
        dcnt = {}
        for o in ops:
            if o.dma is not None:
                if o.dma not in dsem:
                    dsem[o.dma] = stack.enter_context(nc.semaphore("d_%d" % len(dsem)))
                    dcnt[o.dma] = 0
                dcnt[o.dma] += 16
                o.sem, o.val = dsem[o.dma], dcnt[o.dma]
            elif o.sig:
                ecnt[o.eng] += 1
                o.sem, o.val = esem[o.eng], ecnt[o.eng]
        per = {e: [] for e in self.ENG}
        for o in ops:
            per[o.eng].append(o)

        def run(eng_name, e):
            waited = {}
            for o in per[eng_name]:
                need = {}
                for j in o.deps:
                    y = ops[j]
                    if y.dma is None and y.eng == eng_name and o.dma is None and (eng_name == "pe" or not SAME_ENG_SYNC):
                        continue
                    k = id(y.sem)
                    if waited.get(k, 0) >= y.val:
                        continue
                    if k not in need or need[k][1] < y.val:
                        need[k] = (y.sem, y.val)
                for k, (s, v) in need.items():
                    e.wait_ge(s, v)
                    waited[k] = v
                if o.fn is None:
                    continue
                ins = o.fn(e)
                if o.dma is not None:
                    ins.then_inc(o.sem, 16)
                elif o.sig:
                    ins.then_inc(o.sem, 1)

        with nc.Block() as block:
            @block.sync
            def _(e):
                run("sp", e)

            @block.tensor
            def _(e):
                run("pe", e)

            @block.scalar
            def _(e):
                run("act", e)

            @block.vector
            def _(e):
                run("dve", e)

            @block.gpsimd
            def _(e):
                run("pool", e)


class Arena:
    GRAN = 256

    def __init__(self, t):
        self.t = t
        self.off = 0

    def reset(self):
        self.off = 0

    def alloc(self, shape, dtype):
        esz = 4 if dtype == F32 else 2
        n = int(np.prod(shape)) * esz
        off = (self.off + self.GRAN - 1) // self.GRAN * self.GRAN
        self.off = off + n
        assert self.off <= self.t.shape[1], ("arena overflow", self.off)
        ap = self.t[:, off:off + n].bitcast(dtype)
        if len(shape) == 2:
            ap = ap.rearrange("p (a b) -> p a b", b=shape[1])
        elif len(shape) == 3:
            ap = ap.rearrange("p (a b c) -> p a b c", b=shape[1], c=shape[2])
        return Tile(ap, [("AR", g) for g in range(off // self.GRAN, (off + n - 1) // self.GRAN + 1)])


class Ring:
    def __init__(self, tiles):
        self.tiles = tiles
        self.i = 0
        for j, t in enumerate(tiles):
            if isinstance(t, Tile):
                t.ri = j

    def next(self):
        t = self.tiles[self.i % len(self.tiles)]
        self.i += 1
        return t


def build(NT, sched='full'):
    NF = NT * 512
    NTOK = NF + 80
    NK = NTOK + 1024
    nc = bass.Bass("TRN2", target_bir_lowering=False)
    P = Prog(nc)
    D = {}

    def din(name, shape, dt=F32):
        D[name] = nc.dram_tensor(name, list(shape), dt, kind="ExternalInput").ap()
        return D[name]

    def dout(name, shape):
        D[name] = nc.dram_tensor(name, list(shape), F32, kind="ExternalOutput").ap()
        return D[name]

    def dint(name, shape, dt):
        D[name] = nc.dram_tensor(name, list(shape), dt, kind=("ExternalOutput" if DEBUG else "Internal")).ap()
        return D[name]

    xp = din("xp", [NF, 1024]); xs = din("xs", [64, 1024]); meta = din("meta", [16, 1024])
    cckv = din("cckv", [1024, 256]); ckr = din("ckr", [1024, 32]); gS = din("gS", [4, 128, 128])
    gconv = din("gconv", [3, 1536]); sconv = din("sconv", [2, 1024])
    f1n = din("f1n", [2, 1024]); f2n = din("f2n", [2, 1024]); mixn = din("mixn", [2, 1024])
    fw = {}
    for k in ("f1g", "f1u", "f2g", "f2u"):
        fw[k] = din(k, [2, 1024, DFF])
    for k in ("f1d", "f2d"):
        fw[k] = din(k, [2, DFF, 1024])
    w_in = din("w_in", [1024, 2728]); w_out = din("w_out", [1024, 1024]); convw = din("convw", [4, 1536])
    alog = din("alog", [1, 4]); dtb = din("dtb", [1, 4]); onorm = din("onorm", [1, 128])
    qnorm = din("qnorm", [1, 384]); wuq = din("wuq", [384, 768]); kvnorm = din("kvnorm", [1, 256])
    wukv = din("wukv", [256, 1024]); qnn = din("qnn", [1, 64]); qrn = din("qrn", [1, 32])
    knn = din("knn", [1, 64]); krn = din("krn", [1, 32])
    scin = din("scin", [1024, 3072]); sccw = din("sccw", [3, 1024]); scout = din("scout", [1024, 1024])
    c_ident = din("c_ident", [128, 128]); c_tri = din("c_tri", [128, 128]); c_mgt = din("c_mgt", [128, 128])
    c_mlt = din("c_mlt", [128, 128]); c_amask = din("c_amask", [128, 4, 512]); c_b96 = din("c_b96", [96, 96])
    c_pT = din("c_pT", [96, 96]); c_cs = din("c_cs", [NTOK, 32]); c_C96 = din("c_C96", [96, NTOK])
    c_S96 = din("c_S96", [96, NTOK])
    y_p = dout("y_p", [NF, 1024]); y_s = dout("y_s", [64, 1024])
    o_pckv = dout("o_pckv", [NF + 16, 256]); o_pkr = dout("o_pkr", [NF + 16, 32])
    o_pS = dout("o_pS", [4, 128, 128]); o_pconv = dout("o_pconv", [3, 1536]); o_psc = dout("o_psc", [2, 1024])
    o_sckv = dout("o_sckv", [64, 256]); o_skr = dout("o_skr", [64, 32]); o_sS = dout("o_sS", [4, 128, 128])
    o_sconv = dout("o_sconv", [3, 1536]); o_ssc = dout("o_ssc", [2, 1024])
    XA = dint("XA", [NTOK, 1024], F32); XB = dint("XB", [NTOK, 1024], F32)
    HTD = dint("HTD", [8, 128, NTOK], BF16)
    QKVD = dint("QKVD", [12, 128, NTOK], F32); ZTD = dint("ZTD", [4, 128, NTOK], BF16)
    GBD = dint("GBD", [NTOK, 8], F32)
    KTD = dint("KTD", [8, 96, NK], BF16); QTD = dint("QTD", [8, 96, NTOK], BF16)
    VSD = dint("VSD", [NK, 8, 65], BF16); OTD = dint("OTD", [8, 128, NTOK], BF16)

    tiles = [(512 * t, 512) for t in range(NT)] + [(NF, 80)]
    NTL = len(tiles)

    def subs(n):
        return [(s, min(128, n - 128 * s)) for s in range((n + 127) // 128)]

    with ExitStack() as st:
        SLOT_EL = 34368
        slots = [st.enter_context(nc.sbuf_tensor("wslot%d" % i, [128, SLOT_EL], BF16)) for i in range(2)]
        ARB = 69 * 1024
        art = st.enter_context(nc.sbuf_tensor("arena", [128, ARB], U8))
        AR = Arena(art)
        cst = st.enter_context(nc.sbuf_tensor("consts", [128, 6 * 128 + 96 * 2], F32))
        cstb = st.enter_context(nc.sbuf_tensor("constsb", [128, 128 + 96 * 2], BF16))
        psb = [st.enter_context(nc.psum_tensor("psb%d" % i, [128, 512], F32)) for i in range(8)]
        PS = [Tile(psb[i][:], [("PS", i)]) for i in range(8)]
        psr = Ring(PS)
        identf = Tile(cst[:, 0:128], ["c_identf"])
        onesf = Tile(cst[:, 128:256], ["c_onesf"])
        tri = Tile(cst[:, 256:384], ["c_tri"])
        mgt = Tile(cst[:, 384:512], ["c_mgt"])
        mlt = Tile(cst[:, 512:640], ["c_mlt"])
        identb = Tile(cstb[:, 0:128], ["c_identb"])
        b96 = Tile(cstb[:, 128:224], ["c_b96"])
        pT96 = Tile(cstb[:, 224:320], ["c_pT"])
        onesb = Tile(cst[:, 640:768].bitcast(BF16)[:, 0:128], ["c_onesb"])

        P.dma(lambda e: e.dma_start(out=identf[:], in_=c_ident[:, :]), [], [identf], "c0")
        P.dma(lambda e: e.dma_start(out=tri[:], in_=c_tri[:, :]), [], [tri], "c0")
        P.dma(lambda e: e.dma_start(out=mgt[:], in_=c_mgt[:, :]), [], [mgt], "c0")
        P.dma(lambda e: e.dma_start(out=mlt[:], in_=c_mlt[:, :]), [], [mlt], "c0")
        P.dma(lambda e: e.dma_start(out=identb[:], in_=c_ident[:, :]), [], [identb], "c1", q="pool")
        P.dma(lambda e: e.dma_start(out=b96[0:96, :], in_=c_b96[:, :]), [], [b96], "c1", q="pool")
        P.dma(lambda e: e.dma_start(out=pT96[0:96, :], in_=c_pT[:, :]), [], [pT96], "c1", q="pool")
        P.dve(lambda e: e.memset(onesf[:], 1.0), [], [onesf])
        P.dve(lambda e: e.memset(onesb[:], 1.0), [], [onesb])

        def wview(slot, off, kc, ncol):
            return slots[slot][:, off:off + kc * ncol].rearrange("p (k n) -> p k n", n=ncol)

        def wload(slot, part, off, src, kc, ncol):
            dst = wview(slot, off, kc, ncol)
            res = ("WS", slot)
            for k in range(kc):
                P.dma(lambda e, k=k: e.dma_start(out=dst[:, k, :], in_=src[k * 128:(k + 1) * 128, :]),
                      [], [res], ("w", slot, part, k % 3), q="pool")
            return Tile(dst, [res])

        def load_ffn(slot, which, l, half):
            g = fw["f%dg" % which][l]; u = fw["f%du" % which][l]; d = fw["f%dd" % which][l]
            c0 = half * HFF
            wg = wload(slot, 0, 0, g[:, c0:c0 + HFF], 8, HFF)
            wu = wload(slot, 1, 8 * HFF, u[:, c0:c0 + HFF], 8, HFF)
            wd = wload(slot, 2, 16 * HFF, d[c0:c0 + HFF, :], NCH, 1024)
            return wg, wu, wd

        def load_mix0(slot):
            a = wload(slot, 0, 0, w_in, 8, 2728)
            b = wload(slot, 1, 21824, wuq, 3, 768)
            c = wload(slot, 2, 24128, wukv, 2, 1024)
            d = wload(slot, 3, 26176, w_out, 8, 1024)
            return a, b, c, d

        def load_sc(slot):
            a = wload(slot, 0, 0, scin, 8, 3072)
            b = wload(slot, 1, 24576, scout, 8, 1024)
            return a, b

        def xrows(X, kind, r0, r):
            if kind == "in":
                if r0 < NF:
                    return [(0, r, xp[r0:r0 + r, :])]
                return [(0, 16, meta[:, :]), (16, 64, xs[:, :])]
            if kind == "out":
                if r0 < NF:
                    return [(0, r, y_p[r0:r0 + r, :])]
                return [(16, 64, y_s[:, :])]
            return [(0, r, X[r0:r0 + r, :])]

        def norm_A(gt, xt, hn, stat, src, key_src, ti, r0, n):
            hbs = []
            for s, r in subs(n):
                x = xt.next()
                for (p0, cnt, sap) in xrows(src[1], src[0], r0 + 128 * s, r):
                    P.dma(lambda e, x=x, p0=p0, cnt=cnt, sap=sap: e.dma_start(out=x[p0:p0 + cnt, :], in_=sap),
                          [(key_src, ti)], [x], ("xt", x.ri))
                hb = hn.next(); s1 = stat.next(); s2 = stat.next()
                hbs.append(hb)
                P.act(lambda e, x=x, hb=hb, s1=s1, r=r: e.activation(out=hb[0:r, :], in_=x[0:r, :], func=AF.Square, accum_out=s1[0:r, 0:1]), [x], [hb, s1])
                P.act(lambda e, s1=s1, s2=s2, r=r: e.activation(out=s2[0:r, 0:1], in_=s1[0:r, 0:1], func=AF.Ln, bias=EPS, scale=1.0 / 1024), [s1], [s2])
                P.act(lambda e, s2=s2, r=r: e.activation(out=s2[0:r, 1:2], in_=s2[0:r, 0:1], func=AF.Exp, scale=-0.5), [s2], [s2])
                P.dve(lambda e, x=x, hb=hb, s2=s2, r=r: e.scalar_tensor_tensor(out=hb[0:r, :], in0=x[0:r, :], scalar=s2[0:r, 1:2], in1=gt[0:r, :], op0=ALU.mult, op1=ALU.mult), [x, s2, gt], [hb])
            return hbs

        def norm_B(h, hbs, n):
            for (s, r), hb in zip(subs(n), hbs):
                pt = psr.next()
                ptb = pt.ap.bitcast(BF16).rearrange("p (k t) -> p k t", t=128)
                for k in range(8):
                    P.pe(lambda e, k=k, hb=hb, ptb=ptb, r=r: e.transpose(out=ptb[:, k, 0:r], in_=hb[0:r, k * 128:(k + 1) * 128], identity=identb[0:r, 0:r]), [hb, identb], [pt])
                if s % 2 == 0:
                    P.act(lambda e, h=h, ptb=ptb, s=s, r=r: e.copy(out=h[:, :, s * 128:s * 128 + r], in_=ptb[:, :, 0:r]), [pt], [h])
                else:
                    P.dve(lambda e, h=h, ptb=ptb, s=s, r=r: e.tensor_copy(out=h[:, :, s * 128:s * 128 + r], in_=ptb[:, :, 0:r]), [pt], [h])

        def norm_T(h, gt, xt, hn, stat, src, key_src, ti, r0, n):
            for (s, r) in subs(n):
                pass
            hbs = []
            for s, r in subs(n):
                x = xt.next()
                for (p0, cnt, sap) in xrows(src[1], src[0], r0 + 128 * s, r):
                    P.dma(lambda e, x=x, p0=p0, cnt=cnt, sap=sap: e.dma_start(out=x[p0:p0 + cnt, :], in_=sap),
                          [(key_src, ti)], [x], ("xt", x.ri))
                hb = hn.next(); s1 = stat.next(); s2 = stat.next()
                P.act(lambda e, x=x, hb=hb, s1=s1, r=r: e.activation(out=hb[0:r, :], in_=x[0:r, :], func=AF.Square, accum_out=s1[0:r, 0:1]), [x], [hb, s1])
                P.act(lambda e, s1=s1, s2=s2, r=r: e.activation(out=s2[0:r, 0:1], in_=s1[0:r, 0:1], func=AF.Ln, bias=EPS, scale=1.0 / 1024), [s1], [s2])
                P.act(lambda e, s2=s2, r=r: e.activation(out=s2[0:r, 1:2], in_=s2[0:r, 0:1], func=AF.Exp, scale=-0.5), [s2], [s2])
                P.dve(lambda e, x=x, hb=hb, s2=s2, r=r: e.scalar_tensor_tensor(out=hb[0:r, :], in0=x[0:r, :], scalar=s2[0:r, 1:2], in1=gt[0:r, :], op0=ALU.mult, op1=ALU.mult), [x, s2, gt], [hb])
                pt = psr.next()
                ptb = pt.ap.bitcast(BF16).rearrange("p (k t) -> p k t", t=128)
                for k in range(8):
                    P.pe(lambda e, k=k, hb=hb, ptb=ptb, r=r: e.transpose(out=ptb[:, k, 0:r], in_=hb[0:r, k * 128:(k + 1) * 128], identity=identb[0:r, 0:r]), [hb, identb], [pt])
                if s % 2 == 0:
                    P.act(lambda e, h=h, ptb=ptb, s=s, r=r: e.copy(out=h[:, :, s * 128:s * 128 + r], in_=ptb[:, :, 0:r]), [pt], [h])
                else:
                    P.dve(lambda e, h=h, ptb=ptb, s=s, r=r: e.tensor_copy(out=h[:, :, s * 128:s * 128 + r], in_=ptb[:, :, 0:r]), [pt], [h])

        def load_gt(gt, norm_ap):
            P.dma(lambda e: e.dma_start(out=gt[:], in_=norm_ap.partition_broadcast(128)), [], [gt], "gt")

        def ffn_pass(wts, half, norm_ap, src, dst, ti_key_src, ti_key_dst):
            wg, wu, wd = wts
            AR.reset()
            xt = Ring([AR.alloc([1024], F32) for _ in range(4)])
            xn = Ring([AR.alloc([1024], F32) for _ in range(2)])
            hT = Ring([AR.alloc([8, 512], BF16) for _ in range(2)])
            actb = [AR.alloc([512], BF16) for _ in range(NCH)]
            hn = Ring([AR.alloc([1024], BF16) for _ in range(4)])
            sg = Ring([AR.alloc([512], BF16) for _ in range(2)])
            gt = AR.alloc([1024], F32)
            stat = Ring([AR.alloc([4], F32) for _ in range(4)])
            if half == 0:
                load_gt(gt, norm_ap)
            def hload(ti):
                r0, n = tiles[ti]
                h = hT.next()
                P.dma(lambda e, h=h, r0=r0, n=n: e.dma_start(out=h[:, :, 0:n], in_=HTD[:, :, r0:r0 + n].rearrange("k p t -> p k t")),
                      [("HTD", ti)], [h], ("hld", h.ri))
                return h

            hnext = hload(0) if half == 1 else None
            if half == 0:
                hnext = hT.next()
                norm_B(hnext, norm_A(gt, xn, hn, stat, src, ti_key_src, 0, tiles[0][0], tiles[0][1]), tiles[0][1])
            for ti, (r0, n) in enumerate(tiles):
                SB = subs(n)
                h = hnext
                if half == 0:
                    P.dma(lambda e, h=h, r0=r0, n=n: e.dma_start(out=HTD[:, :, r0:r0 + n].rearrange("k p t -> p k t"), in_=h[:, :, 0:n]),
                          [h], [("HTD", ti)], ("hst", h.ri))
                hbs_next = None
                ys = []
                for s, r in SB:
                    y = xt.next()
                    ys.append(y)
                    for (p0, cnt, sap) in xrows(src[1], src[0], r0 + 128 * s, r):
                        P.dma(lambda e, y=y, p0=p0, cnt=cnt, sap=sap: e.dma_start(out=y[p0:p0 + cnt, :], in_=sap),
                              [(ti_key_src, ti)], [y], ("xt", y.ri))
                for c in range(NCH):
                    pg = psr.next(); pu = psr.next()
                    for k in range(8):
                        P.pe(lambda e, c=c, k=k, pg=pg, h=h, n=n: e.matmul(pg[:, 0:n], wg[:, k, c * 128:(c + 1) * 128], h[:, k, 0:n], start=(k == 0), stop=(k == 7)), [wg, h], [pg])
                    for k in range(8):
                        P.pe(lambda e, c=c, k=k, pu=pu, h=h, n=n: e.matmul(pu[:, 0:n], wu[:, k, c * 128:(c + 1) * 128], h[:, k, 0:n], start=(k == 0), stop=(k == 7)), [wu, h], [pu])
                    sgt = sg.next()
                    P.act(lambda e, sgt=sgt, pg=pg, n=n: e.activation(out=sgt[:, 0:n], in_=pg[:, 0:n], func=AF.Silu), [pg], [sgt])
                    P.dve(lambda e, c=c, sgt=sgt, pu=pu, n=n: e.tensor_tensor(out=actb[c][:, 0:n], in0=sgt[:, 0:n], in1=pu[:, 0:n], op=ALU.mult), [sgt, pu], [actb[c]])
                    if half == 0 and c == 1 and ti + 1 < NTL:
                        hbs_next = norm_A(gt, xn, hn, stat, src, ti_key_src, ti + 1, tiles[ti + 1][0], tiles[ti + 1][1])
                if half == 0 and ti + 1 < NTL:
                    hnext = hT.next()
                    norm_B(hnext, hbs_next, tiles[ti + 1][1])
                if half == 1 and ti + 1 < NTL:
                    hnext = hload(ti + 1)
                for s, r in SB:
                    y = ys[s]
                    for dh in range(2):
                        pd = psr.next()
                        for c in range(NCH):
                            P.pe(lambda e, c=c, pd=pd, s=s, r=r, dh=dh: e.matmul(pd[0:r, :], actb[c][:, s * 128:s * 128 + r], wd[:, c, dh * 512:(dh + 1) * 512], start=(c == 0), stop=(c == NCH - 1)), [wd, actb[c]], [pd])
                        P.dve(lambda e, pd=pd, y=y, r=r, dh=dh: e.scalar_tensor_tensor(out=y[0:r, dh * 512:(dh + 1) * 512], in0=pd[0:r, :], scalar=0.5, in1=y[0:r, dh * 512:(dh + 1) * 512], op0=ALU.mult, op1=ALU.add), [pd, y], [y])
                    for (p0, cnt, dap) in xrows(dst[1], dst[0], r0 + 128 * s, r):
                        P.dma(lambda e, y=y, p0=p0, cnt=cnt, dap=dap: e.dma_start(out=dap, in_=y[p0:p0 + cnt, :]),
                              [y], [(ti_key_dst, ti)], ("xo", y.ri))

        def sconv_pass(wts, src, dst, key_src, key_dst):
            wsi, wso = wts
            AR.reset()
            xt = Ring([AR.alloc([1024], F32) for _ in range(3)])
            hT = Ring([AR.alloc([8, 512], BF16) for _ in range(2)])
            hn = Ring([AR.alloc([1024], BF16) for _ in range(2)])
            gt = AR.alloc([1024], F32)
            stat = Ring([AR.alloc([4], F32) for _ in range(4)])
            gT = Ring([[AR.alloc([512], BF16) for _ in range(8)] for _ in range(2)])
            pcr = Ring([AR.alloc([516], F32) for _ in range(2)])
            tmp = Ring([AR.alloc([512], F32) for _ in range(2)])
            yb = Ring([AR.alloc([512], F32) for _ in range(2)])
            hist = {"p": AR.alloc([8, 2], F32), "s": AR.alloc([8, 2], F32)}
            cw = AR.alloc([3, 8], F32)
            load_gt(gt, mixn[1])
            for i in range(3):
                P.dma(lambda e, i=i: e.dma_start(out=cw[:, i, :], in_=sccw[i].rearrange("(c p) -> p c", p=128), allow_slow_non_contiguous=True), [], [cw], "cw")
            for t_ in range(2):
                P.dma(lambda e, t_=t_: e.dma_start(out=hist["s"][:, :, t_], in_=sconv[t_].rearrange("(c p) -> p c", p=128), allow_slow_non_contiguous=True), [], [hist["s"]], "hs")
            P.dve(lambda e: e.memset(hist["p"][:], 0.0), [], [hist["p"]])
            order = [NTL - 1] + list(range(NTL - 1))
            for ti in order:
                r0, n = tiles[ti]
                segs = [(0, 16, "p"), (16, 64, "s")] if ti == NTL - 1 else [(0, n, "p")]
                h = hT.next()
                norm_T(h, gt, xt, hn, stat, src, key_src, ti, r0, n)
                g = gT.next()
                for c in range(8):
                    pcg = psr.next(); pxi = psr.next(); pbg = psr.next()
                    for (pp, base) in ((pcg, 1024), (pxi, 2048), (pbg, 0)):
                        for k in range(8):
                            P.pe(lambda e, pp=pp, base=base, c=c, k=k, h=h, n=n: e.matmul(pp[:, 0:n], wsi[:, k, base + c * 128:base + (c + 1) * 128], h[:, k, 0:n], start=(k == 0), stop=(k == 7)), [wsi, h], [pp])
                    tm = tmp.next()
                    P.act(lambda e, tm=tm, pcg=pcg, n=n: e.copy(out=tm[:, 0:n], in_=pcg[:, 0:n]), [pcg], [tm])
                    for (c0, ns, sq) in segs:
                        pc = pcr.next(); y2 = yb.next(); hs = hist[sq]
                        P.act(lambda e, pc=pc, hs=hs, c=c: e.copy(out=pc[:, 0:2], in_=hs[:, c, :]), [hs], [pc])
                        P.dve(lambda e, pc=pc, tm=tm, pxi=pxi, c0=c0, ns=ns: e.tensor_tensor(out=pc[:, 2:2 + ns], in0=tm[:, c0:c0 + ns], in1=pxi[:, c0:c0 + ns], op=ALU.mult), [tm, pxi], [pc])
                        P.act(lambda e, pc=pc, hs=hs, c=c, ns=ns: e.copy(out=hs[:, c, :], in_=pc[:, ns:ns + 2]), [pc], [hs])
                        P.dve(lambda e, pc=pc, y2=y2, c=c, ns=ns: e.tensor_scalar(out=y2[:, 0:ns], in0=pc[:, 0:ns], scalar1=cw[:, 0, c:c + 1], scalar2=None, op0=ALU.mult), [pc, cw], [y2])
                        for i in (1, 2):
                            P.dve(lambda e, pc=pc, y2=y2, c=c, ns=ns, i=i: e.scalar_tensor_tensor(out=y2[:, 0:ns], in0=pc[:, i:i + ns], scalar=cw[:, i, c:c + 1], in1=y2[:, 0:ns], op0=ALU.mult, op1=ALU.add), [pc, cw, y2], [y2])
                        P.dve(lambda e, g=g, c=c, pbg=pbg, y2=y2, c0=c0, ns=ns: e.tensor_tensor(out=g[c][:, c0:c0 + ns], in0=pbg[:, c0:c0 + ns], in1=y2[:, 0:ns], op=ALU.mult), [pbg, y2], [g[c]])
                for s, r in subs(n):
                    y = xt.next()
                    for (p0, cnt, sap) in xrows(src[1], src[0], r0 + 128 * s, r):
                        P.dma(lambda e, y=y, p0=p0, cnt=cnt, sap=sap: e.dma_start(out=y[p0:p0 + cnt, :], in_=sap),
                              [(key_src, ti)], [y], ("xt", y.ri))
                    for dh in range(2):
                        pd = psr.next()
                        for c in range(8):
                            P.pe(lambda e, c=c, pd=pd, s=s, r=r, dh=dh, g=g: e.matmul(pd[0:r, :], g[c][:, s * 128:s * 128 + r], wso[:, c, dh * 512:(dh + 1) * 512], start=(c == 0), stop=(c == 7)), [wso, g[c]], [pd])
                        P.dve(lambda e, pd=pd, y=y, r=r, dh=dh: e.tensor_tensor(out=y[0:r, dh * 512:(dh + 1) * 512], in0=pd[0:r, :], in1=y[0:r, dh * 512:(dh + 1) * 512], op=ALU.add), [pd, y], [y])
                    for (p0, cnt, dap) in xrows(dst[1], dst[0], r0 + 128 * s, r):
                        P.dma(lambda e, y=y, p0=p0, cnt=cnt, dap=dap: e.dma_start(out=dap, in_=y[p0:p0 + cnt, :]),
                              [y], [(key_dst, ti)], ("xo", y.ri))
            for t_ in range(2):
                P.dma(lambda e, t_=t_: e.dma_start(out=o_psc[t_].rearrange("(c p) -> p c", p=128), in_=hist["p"][:, :, t_], allow_slow_non_contiguous=True), [hist["p"]], ["o_psc"], "ho")
                P.dma(lambda e, t_=t_: e.dma_start(out=o_ssc[t_].rearrange("(c p) -> p c", p=128), in_=hist["s"][:, :, t_], allow_slow_non_contiguous=True), [hist["s"]], ["o_ssc"], "ho")

        def p1_pass(wts, src, key_src):
            w_in_t = wts[0]
            AR.reset()
            xt = Ring([AR.alloc([1024], F32) for _ in range(2)])
            hT = Ring([AR.alloc([8, 512], BF16) for _ in range(2)])
            hn = Ring([AR.alloc([1024], BF16) for _ in range(2)])
            gt = AR.alloc([1024], F32)
            stat = Ring([AR.alloc([4], F32) for _ in range(4)])
            f5 = Ring([AR.alloc([512], F32) for _ in range(6)])
            b5 = Ring([AR.alloc([512], BF16) for _ in range(4)])
            xcr = Ring([AR.alloc([516], BF16) for _ in range(3)])
            dgr = Ring([[AR.alloc([128], BF16) for _ in range(4)] for _ in range(2)])
            histb = {"p": AR.alloc([12, 3], BF16), "s": AR.alloc([12, 3], BF16)}
            hc32 = {"p": AR.alloc([12, 3], F32), "s": AR.alloc([12, 3], F32)}
            cwg = AR.alloc([4, 12], F32)
            dtb_t = AR.alloc([4], F32); negA = AR.alloc([4], F32)
            sm = Ring([AR.alloc([8], F32) for _ in range(6)])
            load_gt(gt, mixn[0])
            for i in range(4):
                P.dma(lambda e, i=i: e.dma_start(out=cwg[:, i, :], in_=convw[i].rearrange("(c p) -> p c", p=128), allow_slow_non_contiguous=True), [], [cwg], "cw")
            for t_ in range(3):
                P.dma(lambda e, t_=t_: e.dma_start(out=hc32["s"][:, :, t_], in_=gconv[t_].rearrange("(c p) -> p c", p=128), allow_slow_non_contiguous=True), [], [hc32["s"]], "hs")
            P.act(lambda e: e.copy(out=histb["s"][:], in_=hc32["s"][:]), [hc32["s"]], [histb["s"]])
            P.dve(lambda e: e.memset(histb["p"][:], 0.0), [], [histb["p"]])
            P.dma(lambda e: e.dma_start(out=dtb_t[:], in_=dtb[0].partition_broadcast(128)), [], [dtb_t], "pv")
            P.dma(lambda e: e.dma_start(out=negA[:], in_=alog[0].partition_broadcast(128)), [], [negA], "pv2")
            P.act(lambda e: e.activation(out=negA[:], in_=negA[:], func=AF.Exp), [negA], [negA])
            P.dve(lambda e: e.tensor_scalar(out=negA[:], in0=negA[:], scalar1=-1.0, scalar2=None, op0=ALU.mult), [negA], [negA])
            pp_r = Ring(PS[0:3]); pc_r = Ring(PS[3:6]); pm_r = Ring(PS[6:8])
            order = [NTL - 1] + list(range(NTL - 1))
            for ti in order:
                r0, n = tiles[ti]
                segs = [(0, 16, "p"), (16, 64, "s")] if ti == NTL - 1 else [(0, n, "p")]
                h = hT.next()
                norm_T(h, gt, xt, hn, stat, src, key_src, ti, r0, n)
                P.dma(lambda e, h=h, r0=r0, n=n: e.dma_start(out=HTD[:, :, r0:r0 + n].rearrange("k p t -> p k t"), in_=h[:, :, 0:n]),
                      [h], [("HTD", ti)], ("hst", h.ri))
                SK = (n == 512)
                if SK:
                    P.begin_group()
                for c in range(12):
                    if SK:
                        P.stage(c, 0)
                    pp = pp_r.next()
                    for k in range(8):
                        P.pe(lambda e, pp=pp, c=c, k=k, h=h, n=n: e.matmul(pp[:, 0:n], w_in_t[:, k, c * 128:(c + 1) * 128], h[:, k, 0:n], start=(k == 0), stop=(k == 7)), [w_in_t, h], [pp])
                    if SK:
                        P.stage(c, 1)
                    dg = dgr.next()
                    for i in range(4):
                        P.dve(lambda e, dg=dg, i=i, c=c: e.tensor_scalar(out=dg[i][:], in0=identb[:], scalar1=cwg[:, i, c:c + 1], scalar2=None, op0=ALU.mult), [identb, cwg], [dg[i]])
                    for (c0, ns, sq_) in segs:
                        xc = xcr.next(); hb = histb[sq_]
                        P.act(lambda e, xc=xc, hb=hb, c=c: e.copy(out=xc[:, 0:3], in_=hb[:, c, :]), [hb], [xc])
                        P.act(lambda e, xc=xc, pp=pp, c0=c0, ns=ns: e.copy(out=xc[:, 3:3 + ns], in_=pp[:, c0:c0 + ns]), [pp], [xc])
                        P.dve(lambda e, xc=xc, hb=hb, c=c, ns=ns: e.tensor_copy(out=hb[:, c, :], in_=xc[:, ns:ns + 3]), [xc], [hb])
                        if sq_ == "s" or ti == NT - 1:
                            P.dve(lambda e, pp=pp, c=c, c0=c0, ns=ns, sq_=sq_: e.tensor_copy(out=hc32[sq_][:, c, :], in_=pp[:, c0 + ns - 3:c0 + ns]), [pp], [hc32[sq_]])
                        if SK:
                            P.stage(c, 2)
                        pc2 = pc_r.next()
                        for i in range(4):
                            P.pe(lambda e, pc2=pc2, dg=dg, i=i, xc=xc, ns=ns: e.matmul(pc2[:, 0:ns], dg[i][:], xc[:, i:i + ns], start=(i == 0), stop=(i == 3)), [dg[i], xc], [pc2])
                        if SK:
                            P.stage(c, 3)
                        so = f5.next()
                        P.act(lambda e, so=so, pc2=pc2, ns=ns: e.activation(out=so[:, 0:ns], in_=pc2[:, 0:ns], func=AF.Exp, scale=-1.0), [pc2], [so])
                        P.act(lambda e, so=so, ns=ns: e.activation(out=so[:, 0:ns], in_=so[:, 0:ns], func=AF.Ln, bias=1.0), [so], [so])
                        P.act(lambda e, so=so, ns=ns: e.activation(out=so[:, 0:ns], in_=so[:, 0:ns], func=AF.Exp, scale=-1.0), [so], [so])
                        P.dve(lambda e, so=so, pc2=pc2, ns=ns: e.tensor_tensor(out=so[:, 0:ns], in0=so[:, 0:ns], in1=pc2[:, 0:ns], op=ALU.mult), [so, pc2], [so])
                        if c < 8:
                            sq = b5.next()
                            P.act(lambda e, sq=sq, so=so, ns=ns: e.activation(out=sq[:, 0:ns], in_=so[:, 0:ns], func=AF.Square), [so], [sq])
                            if SK:
                                P.stage(c, 4)
                            pm = pm_r.next()
                            P.pe(lambda e, pm=pm, sq=sq, ns=ns: e.matmul(pm[:, 0:ns], onesb[:], sq[:, 0:ns], start=True, stop=True), [onesb, sq], [pm])
                            if SK:
                                P.stage(c, 5)
                            t1 = f5.next()
                            P.act(lambda e, t1=t1, pm=pm, ns=ns: e.activation(out=t1[:, 0:ns], in_=pm[:, 0:ns], func=AF.Ln, bias=EPS), [pm], [t1])
                            P.act(lambda e, t1=t1, ns=ns: e.activation(out=t1[:, 0:ns], in_=t1[:, 0:ns], func=AF.Exp, scale=-0.5), [t1], [t1])
                            oq = f5.next()
                            P.dve(lambda e, oq=oq, so=so, t1=t1, ns=ns, c=c: e.scalar_tensor_tensor(out=oq[:, 0:ns], in0=so[:, 0:ns], scalar=(128.0 ** -0.5 if c < 4 else 1.0), in1=t1[:, 0:ns], op0=ALU.mult, op1=ALU.mult), [so, t1], [oq])
                        else:
                            oq = so
                        if SK:
                            P.stage(c, 5)
                        P.dma(lambda e, oq=oq, c=c, r0=r0, c0=c0, ns=ns: e.dma_start(out=QKVD[c, :, r0 + c0:r0 + c0 + ns], in_=oq[:, 0:ns]), [oq], [("QKVD", ti)], ("f5", oq.ri))
                for c in range(4):
                    if SK:
                        P.stage(12 + c, 0)
                    pp = pp_r.next()
                    for k in range(8):
                        P.pe(lambda e, pp=pp, c=c, k=k, h=h, n=n: e.matmul(pp[:, 0:n], w_in_t[:, k, 1536 + c * 128:1536 + (c + 1) * 128], h[:, k, 0:n], start=(k == 0), stop=(k == 7)), [w_in_t, h], [pp])
                    if SK:
                        P.stage(12 + c, 1)
                    zb = b5.next(); zt = f5.next()
                    P.act(lambda e, zt=zt, pp=pp, n=n: e.activation(out=zt[:, 0:n], in_=pp[:, 0:n], func=AF.Exp, scale=-1.0), [pp], [zt])
                    P.act(lambda e, zt=zt, n=n: e.activation(out=zt[:, 0:n], in_=zt[:, 0:n], func=AF.Ln, bias=1.0), [zt], [zt])
                    P.act(lambda e, zt=zt, n=n: e.activation(out=zt[:, 0:n], in_=zt[:, 0:n], func=AF.Exp, scale=-1.0), [zt], [zt])
                    P.dve(lambda e, zb=zb, zt=zt, pp=pp, n=n: e.tensor_tensor(out=zb[:, 0:n], in0=zt[:, 0:n], in1=pp[:, 0:n], op=ALU.mult), [zt, pp], [zb])
                    P.dma(lambda e, zb=zb, c=c, r0=r0, n=n: e.dma_start(out=ZTD[c, :, r0:r0 + n], in_=zb[:, 0:n]), [zb], [("ZTD", ti)], ("b5", zb.ri))
                for s, r in subs(n):
                    if SK:
                        P.stage(16 + s, 0)
                    pp = pp_r.next()
                    for k in range(8):
                        P.pe(lambda e, pp=pp, k=k, h=h, s=s, r=r: e.matmul(pp[0:r, 0:8], h[:, k, s * 128:s * 128 + r], w_in_t[:, k, 2048:2056], start=(k == 0), stop=(k == 7)), [w_in_t, h], [pp])
                    if SK:
                        P.stage(16 + s, 1)
                    ta = sm.next(); tb = sm.next(); gb = sm.next()
                    P.dve(lambda e, ta=ta, pp=pp, r=r: e.tensor_tensor(out=ta[0:r, 0:4], in0=pp[0:r, 0:4], in1=dtb_t[0:r, :], op=ALU.add), [pp, dtb_t], [ta])
                    P.dve(lambda e, ta=ta, tb=tb, r=r: e.tensor_scalar(out=tb[0:r, 0:4], in0=ta[0:r, 0:4], scalar1=-1.0, scalar2=None, op0=ALU.mult), [ta], [tb])
                    P.dve(lambda e, ta=ta, tb=tb, r=r: e.tensor_tensor(out=tb[0:r, 0:4], in0=tb[0:r, 0:4], in1=ta[0:r, 0:4], op=ALU.min), [ta, tb], [tb])
                    P.act(lambda e, tb=tb, r=r: e.activation(out=tb[0:r, 0:4], in_=tb[0:r, 0:4], func=AF.Exp), [tb], [tb])
                    P.act(lambda e, tb=tb, r=r: e.activation(out=tb[0:r, 0:4], in_=tb[0:r, 0:4], func=AF.Ln, bias=1.0), [tb], [tb])
                    P.dve(lambda e, ta=ta, tb=tb, r=r: e.scalar_tensor_tensor(out=ta[0:r, 0:4], in0=ta[0:r, 0:4], scalar=0.0, in1=tb[0:r, 0:4], op0=ALU.max, op1=ALU.add), [ta, tb], [ta])
                    P.dve(lambda e, ta=ta, gb=gb, r=r: e.tensor_tensor(out=gb[0:r, 0:4], in0=ta[0:r, 0:4], in1=negA[0:r, :], op=ALU.mult), [ta, negA], [gb])
                    P.act(lambda e, gb=gb, pp=pp, r=r: e.activation(out=gb[0:r, 4:8], in_=pp[0:r, 4:8], func=AF.Exp, scale=-1.0), [pp], [gb])
                    P.act(lambda e, gb=gb, r=r: e.activation(out=gb[0:r, 4:8], in_=gb[0:r, 4:8], func=AF.Ln, bias=1.0), [gb], [gb])
                    P.act(lambda e, gb=gb, r=r: e.activation(out=gb[0:r, 4:8], in_=gb[0:r, 4:8], func=AF.Exp, scale=-1.0), [gb], [gb])
                    P.dma(lambda e, gb=gb, r0=r0, s=s, r=r: e.dma_start(out=GBD[r0 + s * 128:r0 + s * 128 + r, :], in_=gb[0:r, :]), [gb], [("GBD", ti)], ("sm", gb.ri))
                if SK:
                    P.end_group()
            for t_ in range(3):
                P.dma(lambda e, t_=t_: e.dma_start(out=o_pconv[t_].rearrange("(c p) -> p c", p=128), in_=hc32["p"][:, :, t_], allow_slow_non_contiguous=True), [hc32["p"]], ["o_pconv"], "ho")
                P.dma(lambda e, t_=t_: e.dma_start(out=o_sconv[t_].rearrange("(c p) -> p c", p=128), in_=hc32["s"][:, :, t_], allow_slow_non_contiguous=True), [hc32["s"]], ["o_sconv"], "ho")

        def p2_pass(wts):
            w_in_t, wuq_t, wukv_t = wts[0], wts[1], wts[2]
            wukv_v = wukv_t.ap.rearrange("p k (h t d) -> p k h t d", h=8, t=2)
            AR.reset()
            hT = Ring([AR.alloc([8, 512], BF16) for _ in range(1)])
            f5 = Ring([AR.alloc([512], F32) for _ in range(5)])
            sqr = Ring([AR.alloc([512], BF16) for _ in range(3)]); qnr = Ring([AR.alloc([512], BF16) for _ in range(4)]); o2r = Ring([AR.alloc([512], BF16) for _ in range(2)])
            pA = Ring(PS[0:4]); pB = Ring(PS[4:6]); pC = Ring(PS[6:8])
            c96 = Ring([AR.alloc([512], F32) for _ in range(1)]); s96 = Ring([AR.alloc([512], F32) for _ in range(1)])
            cst_r = Ring([AR.alloc([4, 32], F32) for _ in range(2)])
            cqs = [AR.alloc([512], F32) for _ in range(3)]
            cqn = [AR.alloc([512], BF16) for _ in range(3)]
            sqc = [AR.alloc([512], BF16) for _ in range(3)]
            ckvnT = Ring([[AR.alloc([512], BF16) for _ in range(2)] for _ in range(2)])
            krT = Ring([AR.alloc([512], BF16) for _ in range(2)])
            kvn = Ring([AR.alloc([288], F32) for _ in range(3)])
            gain_t = AR.alloc([288], F32)
            vt = Ring([AR.alloc([8, 65], BF16) for _ in range(2)])
            st = Ring([AR.alloc([4], F32) for _ in range(3)])
            r16 = Ring([AR.alloc([16], F32) for _ in range(4)])
            inv_t = AR.alloc([2], F32); gain96 = AR.alloc([1], F32); knn_t = AR.alloc([1], F32); qnorm_t = AR.alloc([3], F32)
            xck = Ring([AR.alloc([288], F32) for _ in range(4)])
            P.dma(lambda e: e.dma_start(out=gain_t[:, 0:256], in_=kvnorm[0].partition_broadcast(128)), [], [gain_t], "pv")
            P.dma(lambda e: e.dma_start(out=gain_t[:, 256:288], in_=krn[0].partition_broadcast(128)), [], [gain_t], "pv2")
            P.dma(lambda e: e.dma_start(out=gain96[0:64, :], in_=qnn[0].rearrange("(p o) -> p o", o=1)), [], [gain96], "pv3")
            P.dma(lambda e: e.dma_start(out=gain96[64:96, :], in_=qrn[0].rearrange("(p o) -> p o", o=1)), [], [gain96], "pv4")
            P.dma(lambda e: e.dma_start(out=knn_t[0:64, :], in_=knn[0].rearrange("(p o) -> p o", o=1)), [], [knn_t], "pv5")
            P.dma(lambda e: e.dma_start(out=qnorm_t[:], in_=qnorm[0].rearrange("(c p) -> p c", p=128), allow_slow_non_contiguous=True), [], [qnorm_t], "pv6")
            P.dve(lambda e: e.memset(inv_t[:, 0:1], 1.0 / 256), [], [inv_t])
            P.dve(lambda e: e.memset(inv_t[:, 1:2], 1.0 / 32), [], [inv_t])
            wv = AR.alloc([2, 512], BF16)
            for k in range(2):
                P.dve(lambda e, k=k: e.tensor_copy(out=wv[:, k, :].rearrange("p (h d) -> p h d", d=64), in_=wukv_v[:, k, :, 1, :]), [wukv_t], [wv])
            kvb = Ring([AR.alloc([384], BF16) for _ in range(2)])
            for v_ in vt.tiles:
                P.dve(lambda e, v_=v_: e.memset(v_[:], 1.0), [], [v_])
            for kv_ in kvb.tiles:
                P.dve(lambda e, kv_=kv_: e.memset(kv_[:], 0.0), [], [kv_])
            for kv_ in kvn.tiles:
                P.dve(lambda e, kv_=kv_: e.memset(kv_[:], 0.0), [], [kv_])

            def kv_expand(ck, kr_, n, kcol, vrow, key):
                P.begin_group()
                for hh in range(8):
                    P.stage(hh, 0)
                    pk = pA.next()
                    for k in range(2):
                        P.pe(lambda e, pk=pk, k=k, hh=hh, ck=ck, n=n: e.matmul(pk[0:64, 0:n], wukv_t[:, k, hh * 128:hh * 128 + 64], ck[k][:, 0:n], start=(k == 0), stop=(k == 1)), [wukv_t, ck[k]], [pk])
                    P.stage(hh, 1)
                    sq = sqr.next()
                    P.act(lambda e, sq=sq, pk=pk, n=n: e.activation(out=sq[0:64, 0:n], in_=pk[0:64, 0:n], func=AF.Square), [pk], [sq])
                    P.stage(hh, 2)
                    pm = pB.next()
                    P.pe(lambda e, pm=pm, sq=sq, n=n: e.matmul(pm[0:64, 0:n], b96[0:64, 0:64], sq[0:64, 0:n], start=True, stop=True), [b96, sq], [pm])
                    P.stage(hh, 3)
                    t1 = f5.next()
                    P.act(lambda e, t1=t1, pm=pm, n=n: e.activation(out=t1[0:64, 0:n], in_=pm[0:64, 0:n], func=AF.Ln, bias=EPS), [pm], [t1])
                    P.act(lambda e, t1=t1, n=n: e.activation(out=t1[0:64, 0:n], in_=t1[0:64, 0:n], func=AF.Exp, scale=-0.5), [t1], [t1])
                    kn = o2r.next()
                    P.dve(lambda e, kn=kn, pk=pk, t1=t1, n=n: e.scalar_tensor_tensor(out=kn[0:64, 0:n], in0=pk[0:64, 0:n], scalar=knn_t[0:64, 0:1], in1=t1[0:64, 0:n], op0=ALU.mult, op1=ALU.mult), [pk, knn_t, t1], [kn])
                    P.dma(lambda e, kn=kn, hh=hh, n=n: e.dma_start(out=KTD[hh, 0:64, kcol:kcol + n], in_=kn[0:64, 0:n]), [kn], [key], ("b5", kn.ri))
                    P.dma(lambda e, kr_=kr_, hh=hh, n=n: e.dma_start(out=KTD[hh, 64:96, kcol:kcol + n], in_=kr_[0:32, 0:n]), [kr_], [key], ("krd", hh % 4))
                for s, r in subs(n):
                    P.stage(8 + s, 0)
                    pv = pC.next()
                    pvv = pv.ap.rearrange("p (h d) -> p h d", d=64)
                    for k in range(2):
                        P.pe(lambda e, pv=pv, k=k, s=s, r=r, ck=ck: e.matmul(pv[0:r, :], ck[k][:, s * 128:s * 128 + r], wv[:, k, :], start=(k == 0), stop=(k == 1)), [wv, ck[k]], [pv])
                    P.stage(8 + s, 1)
                    v_ = vt.next()
                    P.act(lambda e, v_=v_, pvv=pvv, r=r: e.copy(out=v_[0:r, :, 0:64], in_=pvv[0:r, :, :]), [pv], [v_])
                    P.dma(lambda e, v_=v_, s=s, r=r: e.dma_start(out=VSD[vrow + s * 128:vrow + s * 128 + r, :, :], in_=v_[0:r, :, :]), [v_], [key], ("vt", v_.ri))
                P.end_group()

            for ti, (r0, n) in enumerate(tiles):
                h = hT.next()
                P.dma(lambda e, h=h, r0=r0, n=n: e.dma_start(out=h[:, :, 0:n], in_=HTD[:, :, r0:r0 + n].rearrange("k p t -> p k t")),
                      [("HTD", ti)], [h], ("hld", h.ri))
                cs_t = cst_r.next(); Ct = c96.next(); St = s96.next()
                if n == 512:
                    P.dma(lambda e, cs_t=cs_t, r0=r0: e.dma_start(out=cs_t[:, :, :], in_=c_cs[r0:r0 + 512, :].rearrange("(s p) c -> p s c", p=128)), [], [cs_t], ("cst", cs_t.ri))
                else:
                    P.dma(lambda e, cs_t=cs_t, r0=r0, n=n: e.dma_start(out=cs_t[0:n, 0, :], in_=c_cs[r0:r0 + n, :]), [], [cs_t], ("cst", cs_t.ri))
                P.dma(lambda e, Ct=Ct, r0=r0, n=n: e.dma_start(out=Ct[0:96, 0:n], in_=c_C96[:, r0:r0 + n]), [], [Ct], ("c96", Ct.ri))
                P.dma(lambda e, St=St, r0=r0, n=n: e.dma_start(out=St[0:96, 0:n], in_=c_S96[:, r0:r0 + n]), [], [St], ("s96", St.ri))
                ck = ckvnT.next(); kr_ = krT.next()
                for s, r in (subs(n) if 'a' in P2PARTS else []):
                    pp = psr.next()
                    for k in range(8):
                        P.pe(lambda e, pp=pp, k=k, h=h, s=s, r=r: e.matmul(pp[0:r, 0:288], h[:, k, s * 128:s * 128 + r], w_in_t[:, k, 2440:2728], start=(k == 0), stop=(k == 7)), [w_in_t, h], [pp])
                    s1 = st.next(); s1b = st.next(); s2 = st.next(); jk = xck.next(); kv = kvn.next(); kraw = xck.next()
                    P.act(lambda e, kraw=kraw, pp=pp, r=r: e.copy(out=kraw[0:r, 0:288], in_=pp[0:r, 0:288]), [pp], [kraw])
                    P.act(lambda e, jk=jk, kraw=kraw, s1=s1, r=r: e.activation(out=jk[0:r, 0:256], in_=kraw[0:r, 0:256], func=AF.Square, accum_out=s1[0:r, 0:1]), [kraw], [jk, s1])
                    P.act(lambda e, jk=jk, kraw=kraw, s1b=s1b, r=r: e.activation(out=jk[0:r, 256:288], in_=kraw[0:r, 256:288], func=AF.Square, accum_out=s1b[0:r, 0:1]), [kraw], [jk, s1b])
                    P.act(lambda e, s1=s1, s2=s2, r=r: e.activation(out=s2[0:r, 0:1], in_=s1[0:r, 0:1], func=AF.Ln, bias=EPS, scale=1.0 / 256), [s1], [s2])
                    P.act(lambda e, s1b=s1b, s2=s2, r=r: e.activation(out=s2[0:r, 1:2], in_=s1b[0:r, 0:1], func=AF.Ln, bias=EPS, scale=1.0 / 32), [s1b], [s2])
                    P.act(lambda e, s2=s2, r=r: e.activation(out=s2[0:r, 0:2], in_=s2[0:r, 0:2], func=AF.Exp, scale=-0.5), [s2], [s2])
                    P.dve(lambda e, kv=kv, kraw=kraw, s2=s2, r=r: e.scalar_tensor_tensor(out=kv[0:r, 0:256], in0=kraw[0:r, 0:256], scalar=s2[0:r, 0:1], in1=gain_t[0:r, 0:256], op0=ALU.mult, op1=ALU.mult), [kraw, s2, gain_t], [kv])
                    P.dve(lambda e, jk=jk, kraw=kraw, s2=s2, r=r: e.scalar_tensor_tensor(out=jk[0:r, 256:288], in0=kraw[0:r, 256:288], scalar=s2[0:r, 1:2], in1=gain_t[0:r, 256:288], op0=ALU.mult, op1=ALU.mult), [kraw, s2, gain_t], [jk])
                    if A_LVL < 2:
                        continue
                    cosv = cs_t[0:r, s, 0:16]; sinv = cs_t[0:r, s, 16:32]
                    a1 = r16.next(); a2 = r16.next(); a3 = r16.next(); a4 = r16.next()
                    P.dve(lambda e, a1=a1, jk=jk, cosv=cosv, r=r: e.tensor_tensor(out=a1[0:r, :], in0=jk[0:r, 256:272], in1=cosv, op=ALU.mult), [jk, cs_t], [a1])
                    P.dve(lambda e, a2=a2, jk=jk, sinv=sinv, r=r: e.tensor_tensor(out=a2[0:r, :], in0=jk[0:r, 272:288], in1=sinv, op=ALU.mult), [jk, cs_t], [a2])
                    P.dve(lambda e, a3=a3, jk=jk, cosv=cosv, r=r: e.tensor_tensor(out=a3[0:r, :], in0=jk[0:r, 272:288], in1=cosv, op=ALU.mult), [jk, cs_t], [a3])
                    P.dve(lambda e, a4=a4, jk=jk, sinv=sinv, r=r: e.tensor_tensor(out=a4[0:r, :], in0=jk[0:r, 256:272], in1=sinv, op=ALU.mult), [jk, cs_t], [a4])
                    P.dve(lambda e, kv=kv, a1=a1, a2=a2, r=r: e.tensor_tensor(out=kv[0:r, 256:272], in0=a1[0:r, :], in1=a2[0:r, :], op=ALU.subtract), [a1, a2], [kv])
                    P.dve(lambda e, kv=kv, a3=a3, a4=a4, r=r: e.tensor_tensor(out=kv[0:r, 272:288], in0=a3[0:r, :], in1=a4[0:r, :], op=ALU.add), [a3, a4], [kv])
                    if A_LVL < 3:
                        continue
                    if r0 < NF:
                        rr = r0 + s * 128
                        P.dma(lambda e, kv=kv, rr=rr, r=r: e.dma_start(out=o_pckv[rr:rr + r, :], in_=kv[0:r, 0:256]), [kv], ["o_pckv"], ("kvo", kv.ri))
                        P.dma(lambda e, kv=kv, rr=rr, r=r: e.dma_start(out=o_pkr[rr:rr + r, :], in_=kv[0:r, 256:288]), [kv], ["o_pkr"], ("kvo2", kv.ri))
                    else:
                        P.dma(lambda e, kv=kv: e.dma_start(out=o_pckv[NF:NF + 16, :], in_=kv[0:16, 0:256]), [kv], ["o_pckv"], ("kvo", kv.ri))
                        P.dma(lambda e, kv=kv: e.dma_start(out=o_pkr[NF:NF + 16, :], in_=kv[0:16, 256:288]), [kv], ["o_pkr"], ("kvo2", kv.ri))
                        P.dma(lambda e, kv=kv: e.dma_start(out=o_sckv[:, :], in_=kv[16:80, 0:256]), [kv], ["o_sckv"], ("kvo", kv.ri))
                        P.dma(lambda e, kv=kv: e.dma_start(out=o_skr[:, :], in_=kv[16:80, 256:288]), [kv], ["o_skr"], ("kvo2", kv.ri))
                    if A_LVL < 4:
                        continue
                    pt = psr.next()
                    ptv = pt.ap.bitcast(BF16).rearrange("p (j t) -> p j t", t=128)
                    kb = kvb.next()
                    P.act(lambda e, kb=kb, kv=kv, r=r: e.copy(out=kb[0:r, 0:288], in_=kv[0:r, 0:288]), [kv], [kb])
                    for j in range(3):
                        P.pe(lambda e, ptv=ptv, j=j, kb=kb, r=r: e.transpose(out=ptv[:, j, 0:r], in_=kb[0:r, j * 128:(j + 1) * 128], identity=identb[0:r, 0:r]), [kb, identb], [pt])
                    if A_LVL < 5:
                        continue
                    for j in range(2):
                        P.act(lambda e, ck=ck, ptv=ptv, j=j, s=s, r=r: e.copy(out=ck[j][:, s * 128:s * 128 + r], in_=ptv[:, j, 0:r]), [pt], [ck[j]])
                    if A_LVL < 6:
                        continue
                    P.act(lambda e, kr_=kr_, ptv=ptv, s=s, r=r: e.copy(out=kr_[0:32, s * 128:s * 128 + r], in_=ptv[0:32, 2, 0:r]), [pt], [kr_])
                if 'b' in P2PARTS:
                    kv_expand(ck, kr_, n, r0, r0, ("KV", ti))
                if 'q' not in P2PARTS:
                    continue
                pm = psr.next()
                for c in range(3):
                    pc_ = psr.next()
                    for k in range(8):
                        P.pe(lambda e, pc_=pc_, c=c, k=k, h=h, n=n: e.matmul(pc_[:, 0:n], w_in_t[:, k, 2056 + c * 128:2056 + (c + 1) * 128], h[:, k, 0:n], start=(k == 0), stop=(k == 7)), [w_in_t, h], [pc_])
                    P.act(lambda e, c=c, pc_=pc_, n=n: e.copy(out=cqs[c][:, 0:n], in_=pc_[:, 0:n]), [pc_], [cqs[c]])
                    P.act(lambda e, c=c, pc_=pc_, n=n: e.activation(out=sqc[c][:, 0:n], in_=pc_[:, 0:n], func=AF.Square), [pc_], [sqc[c]])
                for c in range(3):
                    P.pe(lambda e, pm=pm, c=c, n=n: e.matmul(pm[:, 0:n], onesb[:], sqc[c][:, 0:n], start=(c == 0), stop=(c == 2)), [onesb, sqc[c]], [pm])
                t0_ = f5.next()
                P.act(lambda e, t0_=t0_, pm=pm, n=n: e.activation(out=t0_[:, 0:n], in_=pm[:, 0:n], func=AF.Ln, bias=EPS, scale=1.0 / 384), [pm], [t0_])
                P.act(lambda e, t0_=t0_, n=n: e.activation(out=t0_[:, 0:n], in_=t0_[:, 0:n], func=AF.Exp, scale=-0.5), [t0_], [t0_])
                for c in range(3):
                    P.dve(lambda e, c=c, t0_=t0_, n=n: e.scalar_tensor_tensor(out=cqn[c][:, 0:n], in0=cqs[c][:, 0:n], scalar=qnorm_t[:, c:c + 1], in1=t0_[:, 0:n], op0=ALU.mult, op1=ALU.mult), [cqs[c], qnorm_t, t0_], [cqn[c]])
                P.begin_group()
                for hh in range(8):
                    P.stage(hh, 0)
                    pq = pA.next()
                    for c in range(3):
                        P.pe(lambda e, pq=pq, c=c, hh=hh, n=n: e.matmul(pq[0:96, 0:n], wuq_t[:, c, hh * 96:(hh + 1) * 96], cqn[c][:, 0:n], start=(c == 0), stop=(c == 2)), [wuq_t, cqn[c]], [pq])
                    P.stage(hh, 1)
                    sq = sqr.next()
                    P.act(lambda e, sq=sq, pq=pq, n=n: e.activation(out=sq[0:96, 0:n], in_=pq[0:96, 0:n], func=AF.Square), [pq], [sq])
                    P.stage(hh, 2)
                    pm2 = pB.next()
                    P.pe(lambda e, pm2=pm2, sq=sq, n=n: e.matmul(pm2[0:96, 0:n], b96[0:96, 0:96], sq[0:96, 0:n], start=True, stop=True), [b96, sq], [pm2])
                    P.stage(hh, 3)
                    t1 = f5.next()
                    P.act(lambda e, t1=t1, pm2=pm2, n=n: e.activation(out=t1[0:96, 0:n], in_=pm2[0:96, 0:n], func=AF.Ln, bias=EPS), [pm2], [t1])
                    P.act(lambda e, t1=t1, n=n: e.activation(out=t1[0:96, 0:n], in_=t1[0:96, 0:n], func=AF.Exp, scale=-0.5), [t1], [t1])
                    qn = qnr.next()
                    P.dve(lambda e, qn=qn, pq=pq, t1=t1, n=n: e.scalar_tensor_tensor(out=qn[0:96, 0:n], in0=pq[0:96, 0:n], scalar=gain96[0:96, 0:1], in1=t1[0:96, 0:n], op0=ALU.mult, op1=ALU.mult), [pq, gain96, t1], [qn])
                    P.stage(hh, 4)
                    pr = pC.next()
                    P.pe(lambda e, pr=pr, qn=qn, n=n: e.matmul(pr[0:96, 0:n], pT96[0:96, 0:96], qn[0:96, 0:n], start=True, stop=True), [pT96, qn], [pr])
                    P.stage(hh, 5)
                    u1 = f5.next(); u2 = f5.next(); qf = o2r.next()
                    P.dve(lambda e, u1=u1, qn=qn, Ct=Ct, n=n: e.tensor_tensor(out=u1[0:96, 0:n], in0=qn[0:96, 0:n], in1=Ct[0:96, 0:n], op=ALU.mult), [qn, Ct], [u1])
                    P.dve(lambda e, u2=u2, pr=pr, St=St, n=n: e.tensor_tensor(out=u2[0:96, 0:n], in0=pr[0:96, 0:n], in1=St[0:96, 0:n], op=ALU.mult), [pr, St], [u2])
                    P.dve(lambda e, qf=qf, u1=u1, u2=u2, n=n: e.tensor_tensor(out=qf[0:96, 0:n], in0=u1[0:96, 0:n], in1=u2[0:96, 0:n], op=ALU.add), [u1, u2], [qf])
                    P.dma(lambda e, qf=qf, hh=hh, r0=r0, n=n: e.dma_start(out=QTD[hh, :, r0:r0 + n], in_=qf[0:96, 0:n]), [qf], [("QTD", ti)], ("b5", qf.ri))
                P.end_group()
            for ct in (range(2) if 'c' in P2PARTS else []):
                ck = ckvnT.next(); kr_ = krT.next()
                for s in range(4):
                    rr = ct * 512 + s * 128
                    kv = kvn.next()
                    P.dma(lambda e, kv=kv, rr=rr: e.dma_start(out=kv[:, 0:256], in_=cckv[rr:rr + 128, :]), [], [kv], ("kvo", kv.ri))
                    P.dma(lambda e, kv=kv, rr=rr: e.dma_start(out=kv[:, 256:288], in_=ckr[rr:rr + 128, :]), [], [kv], ("kvo2", kv.ri))
                    pt = psr.next()
                    ptv = pt.ap.bitcast(BF16).rearrange("p (j t) -> p j t", t=128)
                    kb = kvb.next()
                    P.act(lambda e, kb=kb, kv=kv: e.copy(out=kb[:, 0:288], in_=kv[:, 0:288]), [kv], [kb])
                    for j in range(3):
                        P.pe(lambda e, ptv=ptv, j=j, kb=kb: e.transpose(out=ptv[:, j, :], in_=kb[:, j * 128:(j + 1) * 128], identity=identb[:]), [kb, identb], [pt])
                    for j in range(2):
                        P.act(lambda e, ck=ck, ptv=ptv, j=j, s=s: e.copy(out=ck[j][:, s * 128:(s + 1) * 128], in_=ptv[:, j, :]), [pt], [ck[j]])
                    P.act(lambda e, kr_=kr_, ptv=ptv, s=s: e.copy(out=kr_[0:32, s * 128:(s + 1) * 128], in_=ptv[0:32, 2, :]), [pt], [kr_])
                kv_expand(ck, kr_, 512, NTOK + ct * 512, NTOK + ct * 512, ("KV", "c%d" % ct))

        def g_pass():
            AR.reset()
            def T4():
                return AR.alloc([4, 128], F32)
            ld = [dict(q=T4(), k=T4(), v=T4(), z=AR.alloc([4, 128], BF16), gb=AR.alloc([8], F32)) for _ in range(2)]
            Rt = T4(); D1 = T4(); D2 = T4(); Ege = T4(); Egt = T4(); Elt = T4(); QKT = T4(); tmpM = T4()
            def T2():
                return AR.alloc([2, 128], F32)
            Pk = [[T2(), T2()], [T2(), T2()]]; Qk = [[T2(), T2()], [T2(), T2()]]; Rk = [[T2(), T2()], [T2(), T2()]]
            Vb = T4(); Kb = T4(); kt = T4(); nWkT = T4(); Wsb = T4(); qdT = T4(); og = T4()
            sqo = AR.alloc([4, 128], BF16); ogz = Ring([AR.alloc([4, 128], BF16) for _ in range(2)])
            Ss = {"p": T4(), "s": T4()}
            smr = Ring([AR.alloc([4], F32) for _ in range(12)])
            onorm_t = AR.alloc([1], F32)
            P.dma(lambda e: e.dma_start(out=onorm_t[:, :], in_=onorm[0].rearrange("(p o) -> p o", o=1)), [], [onorm_t], "pv")
            for d in ld:
                for nm in ("q", "k", "v", "gb", "z"):
                    P.dve(lambda e, t=d[nm]: e.memset(t[:], 0.0), [], [d[nm]])
            P.dve(lambda e: e.memset(Ss["p"][:], 0.0), [], [Ss["p"]])
            P.dma(lambda e: e.dma_start(out=Ss["s"][:, :, :], in_=gS.rearrange("h k v -> k h v")), [], [Ss["s"]], "pv2")
            chunks = [(NF + 16, 64, "s", NTL - 1), (NF, 16, "p", NTL - 1)] + [(128 * ci, 128, "p", ci // 4) for ci in range(4 * NT)]
            tri_b = tri.ap.unsqueeze(1).broadcast_to([128, 4, 128])
            mgt_b = mgt.ap.unsqueeze(1).broadcast_to([128, 4, 128])
            mlt_b = mlt.ap.unsqueeze(1).broadcast_to([128, 4, 128])
            id_b = identf.ap.unsqueeze(1).broadcast_to([128, 4, 128])

            def loads(i):
                row0, C, sq_, ti = chunks[i]
                d = ld[i % 2]
                for a, nm in enumerate(("q", "k", "v")):
                    P.dma(lambda e, a=a, t=d[nm], row0=row0, C=C: e.dma_start(out=t[:, :, 0:C], in_=QKVD[4 * a:4 * a + 4, :, row0:row0 + C].rearrange("h p t -> p h t")),
                          [("QKVD", ti)], [d[nm]], ("gl", nm, i % 2))
                P.dma(lambda e, t=d["z"], row0=row0, C=C: e.dma_start(out=t[:, :, 0:C], in_=ZTD[:, :, row0:row0 + C].rearrange("h p t -> p h t")),
                      [("ZTD", ti)], [d["z"]], ("gl", "z", i % 2))
                P.dma(lambda e, t=d["gb"], row0=row0, C=C: e.dma_start(out=t[0:C, :], in_=GBD[row0:row0 + C, :]),
                      [("GBD", ti)], [d["gb"]], ("gl", "gb", i % 2))

            def mm4(ps, lh, rh, start=True, stop=True, R=(), lh2=None):
                for hh in range(4):
                    P.pe(lambda e, hh=hh: e.matmul(ps[:, hh * 128:(hh + 1) * 128], lh[:, hh, :], rh[:, hh, :], start=start, stop=stop), list(R), [ps])

            def compute(i):
                row0, C, sq_, ti = chunks[i]
                d = ld[i % 2]
                qT, kT, vT, zT, gb = d["q"], d["k"], d["v"], d["z"], d["gb"]
                S = Ss[sq_]
                pG = psr.next(); pGL = psr.next()
                P.pe(lambda e: e.matmul(pG[:, 0:4], tri[:], gb[:, 0:4], start=True, stop=True), [tri, gb], [pG])
                P.pe(lambda e: e.matmul(pGL[:, 0:4], onesf[:], gb[:, 0:4], start=True, stop=True), [onesf, gb], [pGL])
                G = smr.next(); eGL = smr.next(); eG = smr.next(); bEG = smr.next(); eGm = smr.next()
                P.act(lambda e: e.copy(out=G[:, :], in_=pG[:, 0:4]), [pG], [G])
                P.act(lambda e: e.activation(out=eGL[:, :], in_=pGL[:, 0:4], func=AF.Exp), [pGL], [eGL])
                P.act(lambda e: e.activation(out=eG[:, :], in_=pG[:, 0:4], func=AF.Exp), [pG], [eG])
                P.dve(lambda e: e.tensor_tensor(out=bEG[:, :], in0=eG[:, :], in1=gb[:, 4:8], op=ALU.mult), [eG, gb], [bEG])
                P.dve(lambda e: e.tensor_tensor(out=eGm[:, :], in0=pGL[:, 0:4], in1=G[:, :], op=ALU.subtract), [pGL, G], [eGm])
                P.act(lambda e: e.activation(out=eGm[:, :], in_=eGm[:, :], func=AF.Exp), [eGm], [eGm])
                for hh in range(4):
                    P.dve(lambda e, hh=hh: e.tensor_scalar(out=Rt[:, hh, :], in0=tri[:], scalar1=gb[:, hh:hh + 1], scalar2=None, op0=ALU.mult), [tri, gb], [Rt])
                pGrow = psr.next()
                P.pe(lambda e: e.matmul(pGrow[:, :], onesf[:], Rt[:, :, :], start=True, stop=True), [onesf, Rt], [pGrow])
                pGr = pGrow.ap.rearrange("p (h t) -> p h t", t=128)
                for hh in range(4):
                    P.dve(lambda e, hh=hh: e.tensor_scalar(out=D1[:, hh, :], in0=pGr[:, hh, :], scalar1=G[:, hh:hh + 1], scalar2=0.0, op0=ALU.subtract, op1=ALU.min), [pGrow, G], [D1])
                    P.dve(lambda e, hh=hh: e.tensor_scalar(out=D2[:, hh, :], in0=pGr[:, hh, :], scalar1=G[:, hh:hh + 1], scalar2=0.0, op0=ALU.subtract, op1=ALU.max), [pGrow, G], [D2])
                P.act(lambda e: e.activation(out=D1[:, :, :], in_=D1[:, :, :], func=AF.Exp), [D1], [D1])
                P.act(lambda e: e.activation(out=D2[:, :, :], in_=D2[:, :, :], func=AF.Exp, scale=-1.0), [D2], [D2])
                P.act(lambda e: e.activation(out=qdT[:, :, :], in_=pGr[:, :, :], func=AF.Exp), [pGrow], [qdT])
                P.dve(lambda e: e.tensor_tensor(out=qdT[:, :, :], in0=qdT[:, :, :], in1=qT[:, :, :], op=ALU.mult), [qdT, qT], [qdT])
                P.dve(lambda e: e.tensor_tensor(out=Ege[:, :, :], in0=D1[:, :, :], in1=tri_b, op=ALU.mult), [D1, tri], [Ege])
                P.dve(lambda e: e.tensor_tensor(out=Egt[:, :, :], in0=D1[:, :, :], in1=mgt_b, op=ALU.mult), [D1, mgt], [Egt])
                P.dve(lambda e: e.tensor_tensor(out=Elt[:, :, :], in0=D2[:, :, :], in1=mlt_b, op=ALU.mult), [D2, mlt], [Elt])
                for hh in range(4):
                    P.dve(lambda e, hh=hh: e.tensor_scalar(out=Rt[:, hh, :], in0=identf[:], scalar1=gb[:, 4 + hh:5 + hh], scalar2=None, op0=ALU.mult), [identf, gb], [Rt])
                pBrow = psr.next()
                P.pe(lambda e: e.matmul(pBrow[:, :], onesf[:], Rt[:, :, :], start=True, stop=True), [onesf, Rt], [pBrow])
                pBr = pBrow.ap.rearrange("p (h t) -> p h t", t=128)
                pKK = psr.next(); pKQ = psr.next()
                mm4(pKK, kT, kT, R=[kT]); mm4(pKQ, kT, qT, R=[kT, qT])
                pKKv = pKK.ap.rearrange("p (h t) -> p h t", t=128); pKQv = pKQ.ap.rearrange("p (h t) -> p h t", t=128)
                P.dve(lambda e: e.tensor_tensor(out=QKT[:, :, :], in0=pKQv, in1=Ege[:, :, :], op=ALU.mult), [pKQ, Ege], [QKT])
                P.dve(lambda e: e.scalar_tensor_tensor(out=tmpM[:, :, :], in0=pKKv, scalar=-1.0, in1=Egt[:, :, :], op0=ALU.mult, op1=ALU.mult), [pKK, Egt], [tmpM])
                P0, Q0, R0 = Pk[0], Qk[0], Rk[0]
                id_b2 = identf.ap.unsqueeze(1).broadcast_to([128, 2, 128])
                for g in range(2):
                    P.dve(lambda e, g=g: e.tensor_tensor(out=P0[g][:, :, :], in0=tmpM[:, 2 * g:2 * g + 2, :], in1=pBr[:, 2 * g:2 * g + 2, :], op=ALU.mult), [tmpM, pBrow], [P0[g]])
                tmpQ = D1
                P.dve(lambda e: e.tensor_tensor(out=tmpQ[:, :, :], in0=pKKv, in1=Elt[:, :, :], op=ALU.mult), [pKK, Elt], [tmpQ])
                for hh in range(4):
                    P.dve(lambda e, hh=hh: e.tensor_scalar(out=Q0[hh // 2][:, hh % 2, :], in0=tmpQ[:, hh, :], scalar1=gb[:, 4 + hh:5 + hh], scalar2=-1.0, op0=ALU.mult, op1=ALU.mult), [tmpQ, gb], [Q0[hh // 2]])
                for g in range(2):
                    P.dve(lambda e, g=g: e.tensor_tensor(out=R0[g][:, :, :], in0=P0[g][:, :, :], in1=id_b2, op=ALU.add), [P0[g], identf], [R0[g]])
                cur = 0

                def mmg(ps, lh, rh, R=()):
                    for j in range(2):
                        P.pe(lambda e, j=j: e.matmul(ps[:, j * 128:(j + 1) * 128], lh[:, j, :], rh[:, j, :], start=True, stop=True), list(R), [ps])

                def pv2(ps):
                    return ps.ap[:, 0:256].rearrange("p (h t) -> p h t", t=128)

                for lvl in range(1, 7):
                    Pc, Qc, Rc = Pk[cur], Qk[cur], Rk[cur]
                    Pn, Qn, Rn = Pk[1 - cur], Qk[1 - cur], Rk[1 - cur]
                    pQ = [psr.next(), psr.next()]
                    pP = [psr.next(), psr.next()] if lvl < 6 else None
                    for g in range(2):
                        mmg(pQ[g], Pc[g], Qc[g], R=[Pc[g], Qc[g]])
                        if lvl < 6:
                            mmg(pP[g], Qc[g], Pc[g], R=[Pc[g], Qc[g]])
                    for g in range(2):
                        P.act(lambda e, g=g, Qn=Qn, pq=pQ[g]: e.copy(out=Qn[g][:, :, :], in_=pv2(pq)), [pQ[g]], [Qn[g]])
                        if lvl < 6:
                            P.dve(lambda e, g=g, Pn=Pn, pp_=pP[g]: e.tensor_copy(out=Pn[g][:, :, :], in_=pv2(pp_)), [pP[g]], [Pn[g]])
                    pR = [psr.next(), psr.next()]
                    for g in range(2):
                        mmg(pR[g], Qn[g], Rc[g], R=[Qn[g], Rc[g]])
                    for g in range(2):
                        P.dve(lambda e, g=g, Rn=Rn, Rc=Rc, pr=pR[g]: e.tensor_tensor(out=Rn[g][:, :, :], in0=Rc[g][:, :, :], in1=pv2(pr), op=ALU.add), [Rc[g], pR[g]], [Rn[g]])
                    cur = 1 - cur
                TT = Rk[cur]
                pK = psr.next(); pV = psr.next()
                for hh in range(4):
                    P.pe(lambda e, hh=hh: e.matmul(pK[:, hh * 128:(hh + 1) * 128], kT[:, hh, :], identf[:], start=True, stop=True), [kT, identf], [pK])
                for hh in range(4):
                    P.pe(lambda e, hh=hh: e.matmul(pV[:, hh * 128:(hh + 1) * 128], vT[:, hh, :], identf[:], start=True, stop=True), [vT, identf], [pV])
                for hh in range(4):
                    P.act(lambda e, hh=hh: e.activation(out=Vb[:, hh, :], in_=pV[:, hh * 128:(hh + 1) * 128], func=AF.Copy, scale=gb[:, 4 + hh:5 + hh]), [pV, gb], [Vb])
                    P.dve(lambda e, hh=hh: e.tensor_scalar(out=Kb[:, hh, :], in0=pK[:, hh * 128:(hh + 1) * 128], scalar1=bEG[:, hh:hh + 1], scalar2=None, op0=ALU.mult), [pK, bEG], [Kb])
                    P.act(lambda e, hh=hh: e.activation(out=kt[:, hh, :], in_=pK[:, hh * 128:(hh + 1) * 128], func=AF.Copy, scale=eGm[:, hh:hh + 1]), [pK, eGm], [kt])
                pWk = psr.next()
                for hh in range(4):
                    P.pe(lambda e, hh=hh: e.matmul(pWk[:, hh * 128:(hh + 1) * 128], Kb[:, hh, :], TT[hh // 2][:, hh % 2, :], start=True, stop=True), [Kb, TT[hh // 2]], [pWk])
                P.act(lambda e: e.mul(out=nWkT[:, :, :], in_=pWk.ap.rearrange("p (h t) -> p h t", t=128), mul=-1.0), [pWk], [nWkT])
                pW = psr.next()
                for hh in range(4):
                    P.pe(lambda e, hh=hh: e.matmul(pW[:, hh * 128:(hh + 1) * 128], TT[hh // 2][:, hh % 2, :], Vb[:, hh, :], start=True, stop=False), [TT[hh // 2], Vb], [pW])
                    P.pe(lambda e, hh=hh: e.matmul(pW[:, hh * 128:(hh + 1) * 128], nWkT[:, hh, :], S[:, hh, :], start=False, stop=True), [nWkT, S], [pW])
                P.act(lambda e: e.copy(out=Wsb[:, :, :], in_=pW.ap.rearrange("p (h t) -> p h t", t=128)), [pW], [Wsb])
                pO = psr.next()
                for hh in range(4):
                    P.pe(lambda e, hh=hh: e.matmul(pO[:, hh * 128:(hh + 1) * 128], S[:, hh, :], qdT[:, hh, :], start=True, stop=False), [S, qdT], [pO])
                    P.pe(lambda e, hh=hh: e.matmul(pO[:, hh * 128:(hh + 1) * 128], Wsb[:, hh, :], QKT[:, hh, :], start=False, stop=True), [Wsb, QKT], [pO])
                pS_ = psr.next()
                mm4(pS_, kt, Wsb, R=[kt, Wsb])
                for hh in range(4):
                    P.dve(lambda e, hh=hh: e.scalar_tensor_tensor(out=S[:, hh, :], in0=S[:, hh, :], scalar=eGL[:, hh:hh + 1], in1=pS_[:, hh * 128:(hh + 1) * 128], op0=ALU.mult, op1=ALU.add), [S, eGL, pS_], [S])
                P.act(lambda e: e.activation(out=sqo[:, :, :], in_=pO.ap.rearrange("p (h t) -> p h t", t=128), func=AF.Square), [pO], [sqo])
                pMS = psr.next()
                P.pe(lambda e: e.matmul(pMS[:, :], onesb[:], sqo[:, :, :], start=True, stop=True), [onesb, sqo], [pMS])
                P.act(lambda e: e.activation(out=og[:, :, :], in_=pMS.ap.rearrange("p (h t) -> p h t", t=128), func=AF.Ln, bias=EPS, scale=1.0 / 128), [pMS], [og])
                P.act(lambda e: e.activation(out=og[:, :, :], in_=og[:, :, :], func=AF.Exp, scale=-0.5), [og], [og])
                P.dve(lambda e: e.scalar_tensor_tensor(out=og[:, :, :], in0=pO.ap.rearrange("p (h t) -> p h t", t=128), scalar=onorm_t[:, 0:1], in1=og[:, :, :], op0=ALU.mult, op1=ALU.mult), [pO, onorm_t, og], [og])
                oz = ogz.next()
                P.dve(lambda e: e.tensor_tensor(out=oz[:, :, :], in0=og[:, :, :], in1=zT[:, :, :], op=ALU.mult), [og, zT], [oz])
                P.dma(lambda e: e.dma_start(out=OTD[0:4, :, row0:row0 + C].rearrange("h p t -> p h t"), in_=oz[:, :, 0:C]), [oz], [("OTD", ti)], ("ogz", oz.ri))

            loads(0)
            for i in range(len(chunks)):
                if i + 1 < len(chunks):
                    loads(i + 1)
                compute(i)
                if i == 0:
                    P.dma(lambda e: e.dma_start(out=o_sS.rearrange("h k v -> k h v"), in_=Ss["s"][:, :, :]), [Ss["s"]], ["o_sS"], "so")
            P.dma(lambda e: e.dma_start(out=o_pS.rearrange("h k v -> k h v"), in_=Ss["p"][:, :, :]), [Ss["p"]], ["o_pS"], "so")

        def a_pass():
            AR.reset()
            NKT = 4 * NT + 2 + 8
            hd = [dict(K=AR.alloc([NK], BF16), Q=AR.alloc([NTOK], BF16), V=AR.alloc([NKT, 65], BF16)) for _ in range(2)]
            ptr = Ring([AR.alloc([512], BF16) for _ in range(4)])
            amask = AR.alloc([4, 512], BF16)
            osb = Ring([AR.alloc([512], F32) for _ in range(2)])
            rrow = Ring([AR.alloc([512], F32) for _ in range(2)])
            onr = Ring([AR.alloc([512], BF16) for _ in range(2)])
            P.dma(lambda e: e.dma_start(out=amask[:, :, :], in_=c_amask[:, :, :]), [], [amask], "c1", q="pool")
            psS = Ring(PS[0:5]); psO = Ring(PS[5:7]); psB = PS[7]
            allkv = [("KV", ti) for ti in range(NTL)] + [("KV", "c0"), ("KV", "c1")]
            allq = [("QTD", ti) for ti in range(NTL)]
            SC = 96.0 ** -0.5

            def hloads(hh):
                d = hd[hh % 2]
                P.dma(lambda e: e.dma_start(out=d["K"][0:96, :], in_=KTD[hh, :, :]), allkv, [d["K"]], ("aK", hh % 2))
                P.dma(lambda e: e.dma_start(out=d["Q"][0:96, :], in_=QTD[hh, :, :]), allq, [d["Q"]], ("aQ", hh % 2))
                for t in range(NT):
                    P.dma(lambda e, t=t: e.dma_start(out=d["V"][:, 4 * t:4 * t + 4, :], in_=VSD[512 * t:512 * t + 512, hh, :].rearrange("(k p) d -> p k d", p=128)), allkv, [d["V"]], ("aV", hh % 2, t % 2))
                P.dma(lambda e: e.dma_start(out=d["V"][0:16, 4 * NT, :], in_=VSD[NF:NF + 16, hh, :]), allkv, [d["V"]], ("aV", hh % 2, 0))
                P.dma(lambda e: e.dma_start(out=d["V"][0:64, 4 * NT + 1, :], in_=VSD[NF + 16:NF + 80, hh, :]), allkv, [d["V"]], ("aV", hh % 2, 1))
                for t in range(2):
                    P.dma(lambda e, t=t: e.dma_start(out=d["V"][:, 4 * NT + 2 + 4 * t:4 * NT + 6 + 4 * t, :], in_=VSD[NTOK + 512 * t:NTOK + 512 * t + 512, hh, :].rearrange("(k p) d -> p k d", p=128)), allkv, [d["V"]], ("aV", hh % 2, t % 2))

            pend = []

            def attend(hh, d, q0, nq, keys, ti):
                pO = psO.next()
                pSs = {}

                def emitS(ki):
                    kc, nk, vti, mk = keys[ki]
                    pS = psS.next()
                    pSs[ki] = pS
                    P.pe(lambda e: e.matmul(pS[0:nk, 0:nq], d["K"][0:96, kc:kc + nk], d["Q"][0:96, q0:q0 + nq], start=True, stop=True), [d["K"], d["Q"]], [pS])

                LA = 4
                for ki in range(min(LA, len(keys))):
                    emitS(ki)
                for ki, (kc, nk, vti, mk) in enumerate(keys):
                    pS = pSs.pop(ki); pt = ptr.next()
                    P.act(lambda e, pS=pS, pt=pt, nk=nk: e.activation(out=pt[0:nk, 0:nq], in_=pS[0:nk, 0:nq], func=AF.Exp, scale=SC), [pS], [pt])
                    if mk is not None:
                        P.dve(lambda e, pt=pt, mk=mk, nk=nk: e.tensor_tensor(out=pt[0:nk, 0:nq], in0=pt[0:nk, 0:nq], in1=amask[0:nk, mk, 0:nq], op=ALU.mult), [pt, amask], [pt])
                    if ki + LA < len(keys):
                        emitS(ki + LA)
                    P.pe(lambda e, pt=pt, nk=nk, vti=vti, ki=ki: e.matmul(pO[0:65, 0:nq], d["V"][0:nk, vti, :], pt[0:nk, 0:nq], start=(ki == 0), stop=(ki == len(keys) - 1)), [d["V"], pt], [pO])
                    if ki == 2 and pend:
                        pend.pop()()
                if pend:
                    pend.pop()()
                rr = rrow.next(); ob = osb.next(); on = onr.next()
                P.dve(lambda e: e.reciprocal(out=rr[64:65, 0:nq], in_=pO[64:65, 0:nq]), [pO], [rr])
                P.act(lambda e: e.copy(out=ob[0:64, 0:nq], in_=pO[0:64, 0:nq]), [pO], [ob])

                def fin_b():
                    P.pe(lambda e: e.matmul(psB[0:64, 0:nq], onesf[64:65, 0:64], rr[64:65, 0:nq], start=True, stop=True), [onesf, rr], [psB])
                    P.dve(lambda e: e.tensor_tensor(out=on[0:64, 0:nq], in0=ob[0:64, 0:nq], in1=psB[0:64, 0:nq], op=ALU.mult), [ob, psB], [on])
                    P.dma(lambda e: e.dma_start(out=OTD[4 + hh // 2, (hh % 2) * 64:(hh % 2) * 64 + 64, q0:q0 + nq], in_=on[0:64, 0:nq]), [on], [("OTD", ti)], ("aO", on.ri))
                pend.append(fin_b)

            hloads(0)
            for hh in range(8):
                if hh + 1 < 8:
                    hloads(hh + 1)
                d = hd[hh % 2]
                meta_k = (NF, 16, 4 * NT, None)
                attend(hh, d, NF, 16, [meta_k], NTL - 1)
                attend(hh, d, NF + 16, 64, [(NTOK + 128 * j, 128, 4 * NT + 2 + j, None) for j in range(8)] + [(NF + 16, 64, 4 * NT + 1, None)], NTL - 1)
                for t in range(NT):
                    keys = [meta_k] + [(128 * k, 128, k, (k - 4 * t) if k >= 4 * t else None) for k in range(4 * t + 4)]
                    attend(hh, d, 512 * t, 512, keys, t)
            while pend:
                pend.pop()()

        def o_pass(wts, src, dst, key_src, key_dst):
            w_out_t = wts[3]
            AR.reset()
            xt = Ring([AR.alloc([1024], F32) for _ in range(4)])
            ot = Ring([[AR.alloc([512], BF16) for _ in range(8)] for _ in range(2)])
            oi = 0
            for ti, (r0, n) in enumerate(tiles):
                o = ot.next(); oi += 1
                for c in range(8):
                    P.dma(lambda e, c=c, o=o, r0=r0, n=n: e.dma_start(out=o[c][:, 0:n], in_=OTD[c, :, r0:r0 + n]), [("OTD", ti)], [o[c]], ("oO", oi % 2, c % 4))
                for s, r in subs(n):
                    y = xt.next()
                    for (p0, cnt, sap) in xrows(src[1], src[0], r0 + 128 * s, r):
                        P.dma(lambda e, y=y, p0=p0, cnt=cnt, sap=sap: e.dma_start(out=y[p0:p0 + cnt, :], in_=sap),
                              [(key_src, ti)], [y], ("xt", y.ri))
                    for dh in range(2):
                        pd = psr.next()
                        for c in range(8):
                            P.pe(lambda e, c=c, pd=pd, s=s, r=r, dh=dh, o=o: e.matmul(pd[0:r, :], o[c][:, s * 128:s * 128 + r], w_out_t[:, c, dh * 512:(dh + 1) * 512], start=(c == 0), stop=(c == 7)), [w_out_t, o[c]], [pd])
                        P.dve(lambda e, pd=pd, y=y, r=r, dh=dh: e.tensor_tensor(out=y[0:r, dh * 512:(dh + 1) * 512], in0=pd[0:r, :], in1=y[0:r, dh * 512:(dh + 1) * 512], op=ALU.add), [pd, y], [y])
                    for (p0, cnt, dap) in xrows(dst[1], dst[0], r0 + 128 * s, r):
                        P.dma(lambda e, y=y, p0=p0, cnt=cnt, dap=dap: e.dma_start(out=dap, in_=y[p0:p0 + cnt, :]),
                              [y], [(key_dst, ti)], ("xo", y.ri))

        if sched == "ffn":
            W0 = load_ffn(0, 1, 0, 0)
            W1 = load_ffn(1, 1, 0, 1)
            ffn_pass(W0, 0, f1n[0], ("in", None), ("x", XA), "IN", "XA")
            ffn_pass(W1, 1, None, ("x", XA), ("out", None), "XA", "OUT")
        elif sched == "mix0":
            Wm = load_mix0(0)
            p1_pass(Wm, ("in", None), "IN")
            p2_pass(Wm)
            g_pass()
            a_pass()
            o_pass(Wm, ("in", None), ("out", None), "IN", "OUT")
        elif sched == "full":
            Wa = load_ffn(0, 1, 0, 0)
            Wb = load_ffn(1, 1, 0, 1)
            ffn_pass(Wa, 0, f1n[0], ("in", None), ("x", XA), "IN", "XA")
            Wm = load_mix0(0)
            ffn_pass(Wb, 1, None, ("x", XA), ("x", XB), "XA", "XB")
            Wa = load_ffn(1, 2, 0, 0)
            p1_pass(Wm, ("x", XB), "XB")
            p2_pass(Wm)
            g_pass()
            a_pass()
            o_pass(Wm, ("x", XB), ("x", XA), "XB", "XA")
            Wb = load_ffn(0, 2, 0, 1)
            ffn_pass(Wa, 0, f2n[0], ("x", XA), ("x", XB), "XA", "XB")
            Wa = load_ffn(1, 1, 1, 0)
            ffn_pass(Wb, 1, None, ("x", XB), ("x", XA), "XB", "XA")
            Wb = load_ffn(0, 1, 1, 1)
            ffn_pass(Wa, 0, f1n[1], ("x", XA), ("x", XB), "XA", "XB")
            Ws = load_sc(1)
            ffn_pass(Wb, 1, None, ("x", XB), ("x", XA), "XB", "XA")
            Wa = load_ffn(0, 2, 1, 0)
            sconv_pass(Ws, ("x", XA), ("x", XB), "XA", "XB")
            Wb = load_ffn(1, 2, 1, 1)
            ffn_pass(Wa, 0, f2n[1], ("x", XB), ("x", XA), "XB", "XA")
            ffn_pass(Wb, 1, None, ("x", XA), ("out", None), "XA", "OUT")
        elif sched == "p1g":
            Wm = load_mix0(0)
            p1_pass(Wm, ("in", None), "IN")
            g_pass()
        elif sched in ("p12", "p1", "p2"):
            Wm = load_mix0(0)
            if sched != "p2":
                p1_pass(Wm, ("in", None), "IN")
            if sched != "p1":
                p2_pass(Wm)
        elif sched == "sconv":
            W0 = load_sc(0)
            sconv_pass(W0, ("in", None), ("out", None), "IN", "OUT")
        P.finish(st)
        LAST['P'] = P
    return nc


def _consts(NT):
    NF = NT * 512
    NTOK = NF + 80
    c = {}
    c["c_ident"] = np.eye(128, dtype=np.float32)
    j = np.arange(128)
    c["c_tri"] = (j[:, None] <= j[None, :]).astype(np.float32)
    c["c_mgt"] = (j[None, :] > j[:, None]).astype(np.float32)
    c["c_mlt"] = (j[None, :] < j[:, None]).astype(np.float32)
    am = np.zeros((128, 4, 512), np.float32)
    p = np.arange(128)[:, None]; f = np.arange(512)[None, :]
    for d in range(4):
        am[:, d, :] = ((2 * d + p // 64) <= (f // 64)).astype(np.float32)
    c["c_amask"] = am
    b = np.zeros((96, 96), np.float32); b[:64, :64] = 1.0 / 64; b[64:, 64:] = 1.0 / 32
    c["c_b96"] = b
    Pm = np.zeros((96, 96), np.float32)
    for a in range(16):
        Pm[64 + a, 64 + 16 + a] = -1.0
        Pm[64 + 16 + a, 64 + a] = 1.0
    c["c_pT"] = np.ascontiguousarray(Pm.T)
    pos = np.concatenate([16 + np.arange(NF), np.arange(16), 1024 + np.arange(64)]).astype(np.float32)
    inv = (np.float32(10000.0) ** (-np.arange(16, dtype=np.float32) / np.float32(16))).astype(np.float32)
    ang = (pos[:, None] * inv[None, :]).astype(np.float32)
    cs = np.cos(ang.astype(np.float64)).astype(np.float32); sn = np.sin(ang.astype(np.float64)).astype(np.float32)
    c["c_cs"] = np.ascontiguousarray(np.concatenate([cs, sn], 1))
    C96 = np.ones((96, NTOK), np.float32); S96 = np.zeros((96, NTOK), np.float32)
    C96[64:80] = cs.T; C96[80:96] = cs.T; S96[64:80] = sn.T; S96[80:96] = sn.T
    c["c_C96"] = C96; c["c_S96"] = S96
    return c


_CACHE = {}
P2PARTS = 'abqc'
A_LVL = 9
DEBUG = False
LAST = {}
SCHED = 'full'


def kernel(**inp):
    f = lambda a: np.ascontiguousarray(np.asarray(a, dtype=np.float32))
    NT = inp["x_prompt"].shape[1] // 512
    NF = NT * 512
    if NT not in _CACHE:
        _CACHE[NT] = build(NT, SCHED)
    nc = _CACHE[NT]
    cs = _consts(NT)
    shared = dict(cs)
    shared.update(meta=f(inp["meta_tokens"]), f1n=f(inp["ffn1_norm"]), f2n=f(inp["ffn2_norm"]), mixn=f(inp["mix_norm"]),
                  f1g=f(inp["ffn1_w_gate"]), f1u=f(inp["ffn1_w_up"]), f1d=f(inp["ffn1_w_down"]),
                  f2g=f(inp["ffn2_w_gate"]), f2u=f(inp["ffn2_w_up"]), f2d=f(inp["ffn2_w_down"]),
                  w_in=f(inp["ab_w_in"][0]), w_out=f(inp["ab_w_out"][0]), convw=f(inp["gdn_conv_w"][0]),
                  alog=f(inp["gdn_A_log"]), dtb=f(inp["gdn_dt_bias"]), onorm=f(inp["gdn_o_norm"]),
                  qnorm=f(inp["mla_q_norm"]), wuq=f(inp["mla_w_uq"][0]), kvnorm=f(inp["mla_kv_norm"]),
                  wukv=f(inp["mla_w_ukv"][0]), qnn=f(inp["mla_qn_norm"]), qrn=f(inp["mla_qr_norm"]),
                  knn=f(inp["mla_kn_norm"]), krn=f(inp["mla_kr_norm"]),
                  scin=f(inp["sc_w_in"][0]), sccw=f(inp["sc_conv_w"][0]), scout=f(inp["sc_w_out"][0]))
    in_maps = []
    for b in range(8):
        m = dict(shared)
        m.update(xp=f(inp["x_prompt"][b]), xs=f(inp["x_sample"][b]), cckv=f(inp["cache_mla_ckv"][0, b]),
                 ckr=f(inp["cache_mla_krope"][0, b]), gS=f(inp["state_gdn_S"][0, b]),
                 gconv=f(inp["state_gdn_conv"][0, b]), sconv=f(inp["state_sconv"][0, b]))
        in_maps.append(m)
    res = run_bass_kernel_spmd(nc, in_maps, core_ids=list(range(8))).results
    LAST["res"] = res
    g = lambda k: np.stack([np.asarray(res[b][k], dtype=np.float32) for b in range(8)])
    pck = g("o_pckv"); pkr = g("o_pkr")
    pck = np.concatenate([pck[:, NF:], pck[:, :NF]], 1); pkr = np.concatenate([pkr[:, NF:], pkr[:, :NF]], 1)
    return (g("y_p"), g("y_s"), pck[None], pkr[None], g("o_pS")[None], g("o_pconv")[None], g("o_psc")[None],
            g("o_sckv")[None], g("o_skr")[None], g("o_sS")[None], g("o_sconv")[None], g("o_ssc")[None])
```
